# Optimizing a Trainium2 kernel written in Bass

```python
import jax
import jax.numpy as jnp
from jax import lax
import numpy as np

D_MODEL = 2048
BATCH = 4
SEQ = 4096
DEPTH = 4

GRID_W = 64
CTX_LEN = 256
N_MIXERS = 4
FFN_HIDDEN = ((8 * D_MODEL + 3 * 256 - 1) // (3 * 256)) * 256
RET_DK = 256
RET_DV = 512
RET_HEADS = D_MODEL // RET_DK
RET_CHUNK = 128
GQA_HEAD_DIM = 128
GQA_HEADS = D_MODEL // GQA_HEAD_DIM
GQA_KV_HEADS = GQA_HEADS // 4
MLA_HEADS = D_MODEL // 128
MLA_Q_RANK = 512
MLA_KV_RANK = 512
MLA_NOPE = 128
MLA_ROPE = 64
MLA_V = 128
HGRN_DIM = 128
HGRN_HEADS = D_MODEL // HGRN_DIM
HGRN_CHUNK = 64

ATTN_BLOCK = 128
ROPE_THETA = 10000.0
EPS = 1e-6
DEEPNORM_ALPHA = (2 * DEPTH) ** 0.25
DEEPNORM_BETA = (8 * DEPTH) ** -0.25
F32 = jnp.float32

kernel_name = "hybrid_retention_gqa_mla_hgrn2_diffusion_trunk"


def layer_norm(x, g, b):
    xf = x.astype(F32)
    xc = xf - jnp.mean(xf, -1, keepdims=True)
    var = jnp.mean(xc * xc, -1, keepdims=True)
    return (xc * lax.rsqrt(var + EPS) * g.astype(F32) + b.astype(F32)).astype(x.dtype)


def rms_norm(x, g=None):
    xf = x.astype(F32)
    y = xf * lax.rsqrt(jnp.mean(xf * xf, -1, keepdims=True) + EPS)
    if g is not None:
        y = y * g.astype(F32)
    return y.astype(x.dtype)


def rope_1d(x, ang):
    cos = jnp.cos(ang)[:, None, :]
    sin = jnp.sin(ang)[:, None, :]
    x1, x2 = jnp.split(x.astype(F32), 2, axis=-1)
    return jnp.concatenate([x1 * cos - x2 * sin, x1 * sin + x2 * cos], -1).astype(x.dtype)


def rope_2d(x, row, col):
    half = x.shape[-1] // 2
    freqs = ROPE_THETA ** (-jnp.arange(0, half, 2, dtype=F32) / half)
    a_row = row.astype(F32)[:, None] * freqs
    a_col = col.astype(F32)[:, None] * freqs
    return jnp.concatenate([rope_1d(x[..., :half], a_row), rope_1d(x[..., half:], a_col)], -1)


def modulate(z, shift, scale):
    return z * (1.0 + scale) + shift


def swiglu(z, w_in, w_out):
    a, b = jnp.split(z @ w_in, 2, axis=-1)
    return (jax.nn.silu(a) * b) @ w_out


def attention(q, k, v, scale):
    s = jnp.einsum('bqhgd,bkhd->bhgqk', q, k).astype(F32) * scale
    p = jax.nn.softmax(s, axis=-1).astype(v.dtype)
    return jnp.einsum('bhgqk,bkhv->bqhgv', p, v)


def blocked_attention(q, k, v, scale):
    B, L = q.shape[:2]
    qb = jnp.moveaxis(q.reshape((B, L // ATTN_BLOCK, ATTN_BLOCK) + q.shape[2:]), 1, 0)
    out = lax.map(lambda blk: attention(blk, k, v, scale), qb)
    out = jnp.moveaxis(out, 0, 1)
    return out.reshape((B, L) + out.shape[3:])


def flip_t(t):
    return jnp.flip(t, axis=2)


def retention_scan(q, k, v, log_gamma, state):
    B, H, L, _ = q.shape
    dv = v.shape[-1]
    C = RET_CHUNK
    lg = log_gamma.astype(F32)
    pos = jnp.arange(C, dtype=F32)
    diff = pos[:, None] - pos[None, :]
    decay_mask = jnp.where(diff >= 0, jnp.exp(jnp.maximum(diff, 0.0) * lg[:, None, None]), 0.0)
    q_decay = jnp.exp((pos + 1.0) * lg[:, None])[..., None]
    k_decay = jnp.exp((C - 1.0 - pos) * lg[:, None])[..., None]
    chunk_decay = jnp.exp(C * lg)[:, None, None]

    def to_chunks(t):
        return jnp.moveaxis(t.astype(F32).reshape(B, H, L // C, C, t.shape[-1]), 2, 0)

    def step(S, xs):
        qc, kc, vc = xs
        scores = jnp.einsum('bhcd,bhsd->bhcs', qc, kc) * decay_mask
        o = jnp.einsum('bhcs,bhsv->bhcv', scores, vc) + jnp.einsum('bhcd,bhdv->bhcv', qc * q_decay, S)
        S = S * chunk_decay + jnp.einsum('bhsd,bhsv->bhdv', kc * k_decay, vc)
        return S, o

    S, o = lax.scan(step, state, (to_chunks(q), to_chunks(k), to_chunks(v)))
    return jnp.moveaxis(o, 0, 2).reshape(B, H, L, dv), S


def gla_scan(q, k, v, log_f, state):
    B, H, L, _ = q.shape
    dv = v.shape[-1]
    C = HGRN_CHUNK
    tri = jnp.arange(C)[:, None] >= jnp.arange(C)[None, :]

    def to_chunks(t):
        return jnp.moveaxis(t.reshape(B, H, L // C, C, t.shape[-1]), 2, 0)

    def step(S, xs):
        qc, kc, vc, gc = xs
        b = jnp.cumsum(gc, axis=-2)
        b_last = b[..., -1:, :]
        q_in = qc * jnp.exp(b)
        scores = jnp.where(tri, jnp.einsum('bhcd,bhsd->bhcs', q_in, kc * jnp.exp(-b)), 0.0)
        o = jnp.einsum('bhcd,bhdv->bhcv', q_in, S) + jnp.einsum('bhcs,bhsv->bhcv', scores, vc)
        S = S * jnp.exp(b_last)[..., 0, :, None] + jnp.einsum('bhsd,bhsv->bhdv', kc * jnp.exp(b_last - b), vc)
        return S, o

    S, o = lax.scan(step, state, (to_chunks(q), to_chunks(k), to_chunks(v), to_chunks(log_f)))
    return jnp.moveaxis(o, 0, 2).reshape(B, H, L, dv), S


def retention_mixer(h, hc, w_in, lg_fwd, lg_bwd, w_out, row, col, need_ctx):
    B = h.shape[0]
    hk, hv = RET_HEADS * RET_DK, RET_HEADS * RET_DV

    def project(z, rotate):
        L = z.shape[1]
        q, k, v, g = jnp.split(z @ w_in, [hk, 2 * hk, 2 * hk + hv], axis=-1)
        q = q.reshape(B, L, RET_HEADS, RET_DK)
        k = k.reshape(B, L, RET_HEADS, RET_DK)
        v = v.reshape(B, L, RET_HEADS, RET_DV)
        if rotate:
            q = rope_2d(q, row, col)
            k = rope_2d(k, row, col)
        heads = lambda t: jnp.swapaxes(t, 1, 2).astype(F32)
        return heads(q), heads(k) * RET_DK ** -0.5, heads(v), g

    def bidir(q, k, v, s_f, s_b):
        o_f, s_f = retention_scan(q, k, v, lg_fwd, s_f)
        o_b, s_b = retention_scan(flip_t(q), flip_t(k), flip_t(v), lg_bwd, s_b)
        return o_f + flip_t(o_b), s_f, s_b

    def readout(o, g):
        o = rms_norm(jnp.swapaxes(o, 1, 2))
        o = o.reshape(B, o.shape[1], hv).astype(g.dtype) * jax.nn.silu(g)
        return o @ w_out

    qc, kc, vc, gc = project(hc, False)
    q, k, v, g = project(h, True)
    zero = jnp.zeros((B, RET_HEADS, RET_DK, RET_DV), F32)
    oc, s_f, s_b = bidir(qc, kc, vc, zero, zero)
    o, _, _ = bidir(q, k, v, s_f, s_b)
    y = readout(o, g)
    yc = readout(oc, gc) if need_ctx else None
    return y, yc


def gqa_mixer(h, hc, w_in, q_gain, k_gain, w_out, row, col, need_ctx):
    B = h.shape[0]
    d = GQA_HEAD_DIM
    G = GQA_HEADS // GQA_KV_HEADS
    scale = d ** -0.5

    def project(z, rotate):
        L = z.shape[1]
        q, k, v = jnp.split(z @ w_in, [GQA_HEADS * d, (GQA_HEADS + GQA_KV_HEADS) * d], axis=-1)
        q = rms_norm(q.reshape(B, L, GQA_HEADS, d), q_gain)
        k = rms_norm(k.reshape(B, L, GQA_KV_HEADS, d), k_gain)
        v = v.reshape(B, L, GQA_KV_HEADS, d)
        if rotate:
            q = rope_2d(q, row, col)
            k = rope_2d(k, row, col)
        return q.reshape(B, L, GQA_KV_HEADS, G, d), k, v

    qc, kc, vc = project(hc, False)
    q, k, v = project(h, True)
    k_all = jnp.concatenate([kc, k], axis=1)
    v_all = jnp.concatenate([vc, v], axis=1)
    o = blocked_attention(q, k_all, v_all, scale)
    y = o.reshape(B, o.shape[1], GQA_HEADS * d) @ w_out
    yc = None
    if need_ctx:
        oc = attention(qc, kc, vc, scale)
        yc = oc.reshape(B, oc.shape[1], GQA_HEADS * d) @ w_out
    return y, yc


def mla_mixer(h, hc, w_in, q_norm, w_q_up, kv_norm, w_kv_up, w_out, row, col, need_ctx):
    B = h.shape[0]
    H = MLA_HEADS
    scale = (MLA_NOPE + MLA_ROPE) ** -0.5

    def project(z, rotate):
        L = z.shape[1]
        cq, ckv, k_rope = jnp.split(z @ w_in, [MLA_Q_RANK, MLA_Q_RANK + MLA_KV_RANK], axis=-1)
        q = (rms_norm(cq, q_norm) @ w_q_up).reshape(B, L, H, MLA_NOPE + MLA_ROPE)
        kv = (rms_norm(ckv, kv_norm) @ w_kv_up).reshape(B, L, H, MLA_NOPE + MLA_V)
        q_nope, q_rope = jnp.split(q, [MLA_NOPE], axis=-1)
        k_nope, v = jnp.split(kv, [MLA_NOPE], axis=-1)
        k_rope = k_rope[:, :, None, :]
        if rotate:
            q_rope = rope_2d(q_rope, row, col)
            k_rope = rope_2d(k_rope, row, col)
        q = jnp.concatenate([q_nope, q_rope], -1)[:, :, :, None, :]
        k = jnp.concatenate([k_nope, jnp.broadcast_to(k_rope, (B, L, H, MLA_ROPE))], -1)
        return q, k, v

    qc, kc, vc = project(hc, False)
    q, k, v = project(h, True)
    k_all = jnp.concatenate([kc, k], axis=1)
    v_all = jnp.concatenate([vc, v], axis=1)
    o = blocked_attention(q, k_all, v_all, scale)
    y = o.reshape(B, o.shape[1], H * MLA_V) @ w_out
    yc = None
    if need_ctx:
        oc = attention(qc, kc, vc, scale)
        yc = oc.reshape(B, oc.shape[1], H * MLA_V) @ w_out
    return y, yc


def hgrn2_mixer(h, hc, w_in, lower_bound, out_gain, w_out, need_ctx):
    B = h.shape[0]
    H, d = HGRN_HEADS, HGRN_DIM
    width = H * d
    lb = lower_bound.astype(F32).reshape(H, 1, d)

    def heads(t):
        return jnp.swapaxes(t.reshape(B, t.shape[1], H, d), 1, 2).astype(F32)

    def forget(fp):
        f = lb + (1.0 - lb) * jax.nn.sigmoid(heads(fp))
        return 1.0 - f, jnp.log(f)

    def project(z):
        q, f_f, f_b, i, g = jnp.split(z @ w_in, 5, axis=-1)
        k_f, lf_f = forget(f_f)
        k_b, lf_b = forget(f_b)
        return jax.nn.silu(heads(q)), heads(i), k_f, lf_f, k_b, lf_b, g

    def bidir(q, i, k_f, lf_f, k_b, lf_b, s_f, s_b):
        o_f, s_f = gla_scan(q, k_f, i, lf_f, s_f)
        o_b, s_b = gla_scan(flip_t(q), flip_t(k_b), flip_t(i), flip_t(lf_b), s_b)
        return o_f + flip_t(o_b), s_f, s_b

    def readout(o, g):
        o = rms_norm(jnp.swapaxes(o, 1, 2), out_gain).reshape(B, o.shape[2], width)
        return (o.astype(g.dtype) * jax.nn.silu(g)) @ w_out

    qc, ic, kfc, lfc, kbc, lbc, gc = project(hc)
    q, i, k_f, lf_f, k_b, lf_b, g = project(h)
    zero = jnp.zeros((B, H, d, d), F32)
    oc, s_f, s_b = bidir(qc, ic, kfc, lfc, kbc, lbc, zero, zero)
    o, _, _ = bidir(q, i, k_f, lf_f, k_b, lf_b, s_f, s_b)
    y = readout(o, g)
    yc = readout(oc, gc) if need_ctx else None
    return y, yc


def setup_inputs(seed: int = 0) -> dict:
    key = jax.random.key(seed)
    ks = jax.random.split(key, 32)
    counter = iter(range(32))
    D = D_MODEL

    def normal(shape, scale):
        return jax.random.normal(ks[next(counter)], shape, F32) * scale

    def n_of(m):
        return len(range(m, DEPTH, N_MIXERS))

    nR, nG, nM, nH = n_of(0), n_of(1), n_of(2), n_of(3)
    ret_base = jnp.log(1.0 - 2.0 ** (-5.0 - jnp.arange(RET_HEADS, dtype=F32)))
    ret_in = 2 * RET_HEADS * RET_DK + 2 * RET_HEADS * RET_DV
    gqa_in = (GQA_HEADS + 2 * GQA_KV_HEADS) * GQA_HEAD_DIM
    mla_in = MLA_Q_RANK + MLA_KV_RANK + MLA_ROPE
    hgrn_w = HGRN_HEADS * HGRN_DIM
    return {
        "x": normal((BATCH, SEQ, D), 1.0),
        "c": normal((BATCH, D), 1.0),
        "ctx": normal((BATCH, CTX_LEN, D), 1.0),
        "c_ctx": normal((D,), 1.0),
        "ada_w": normal((DEPTH, D, 6 * D), 0.5 * D ** -0.5),
        "ada_b": normal((DEPTH, 6 * D), 0.01),
        "ln_g": 1.0 + normal((DEPTH, 2, D), 0.02),
        "ln_b": normal((DEPTH, 2, D), 0.01),
        "ffn_w_in": normal((DEPTH, D, 2 * FFN_HIDDEN), D ** -0.5),
        "ffn_w_out": normal((DEPTH, FFN_HIDDEN, D), DEEPNORM_BETA * FFN_HIDDEN ** -0.5),
        "ret_w_in": normal((nR, D, ret_in), D ** -0.5),
        "ret_decay_fwd": ret_base[None] * (1.0 + normal((nR, RET_HEADS), 0.05)),
        "ret_decay_bwd": ret_base[None] * (1.0 + normal((nR, RET_HEADS), 0.05)),
        "ret_w_out": normal((nR, RET_HEADS * RET_DV, D), DEEPNORM_BETA * (RET_HEADS * RET_DV) ** -0.5),
        "gqa_w_in": normal((nG, D, gqa_in), D ** -0.5),
        "gqa_q_norm": 1.0 + normal((nG, GQA_HEAD_DIM), 0.02),
        "gqa_k_norm": 1.0 + normal((nG, GQA_HEAD_DIM), 0.02),
        "gqa_w_out": normal((nG, GQA_HEADS * GQA_HEAD_DIM, D), DEEPNORM_BETA * (GQA_HEADS * GQA_HEAD_DIM) ** -0.5),
        "mla_w_in": normal((nM, D, mla_in), D ** -0.5),
        "mla_q_norm": 1.0 + normal((nM, MLA_Q_RANK), 0.02),
        "mla_w_q_up": normal((nM, MLA_Q_RANK, MLA_HEADS * (MLA_NOPE + MLA_ROPE)), MLA_Q_RANK ** -0.5),
        "mla_kv_norm": 1.0 + normal((nM, MLA_KV_RANK), 0.02),
        "mla_w_kv_up": normal((nM, MLA_KV_RANK, MLA_HEADS * (MLA_NOPE + MLA_V)), MLA_KV_RANK ** -0.5),
        "mla_w_out": normal((nM, MLA_HEADS * MLA_V, D), DEEPNORM_BETA * (MLA_HEADS * MLA_V) ** -0.5),
        "hgrn_w_in": normal((nH, D, 5 * hgrn_w), D ** -0.5),
        "hgrn_lb_raw": normal((DEPTH, hgrn_w), 0.1),
        "hgrn_out_norm": 1.0 + normal((nH, HGRN_DIM), 0.02),
        "hgrn_w_out": normal((nH, hgrn_w, D), DEEPNORM_BETA * hgrn_w ** -0.5),
    }


def reference(x, c, ctx, c_ctx, ada_w, ada_b, ln_g, ln_b, ffn_w_in, ffn_w_out,
              ret_w_in, ret_decay_fwd, ret_decay_bwd, ret_w_out,
              gqa_w_in, gqa_q_norm, gqa_k_norm, gqa_w_out,
              mla_w_in, mla_q_norm, mla_w_q_up, mla_kv_norm, mla_w_kv_up, mla_w_out,
              hgrn_w_in, hgrn_lb_raw, hgrn_out_norm, hgrn_w_out):
    rows = x.shape[1] // GRID_W
    row = jnp.repeat(jnp.arange(rows), GRID_W)
    col = jnp.tile(jnp.arange(GRID_W), rows)
    p = jax.nn.softmax(hgrn_lb_raw.astype(F32), axis=0)
    lower_bounds = jnp.cumsum(p, axis=0) - p[0]

    for i in range(DEPTH):
        m, j = i % N_MIXERS, i // N_MIXERS
        need_ctx = i < DEPTH - 1
        sh1, sc1, g1, sh2, sc2, g2 = jnp.split(jax.nn.silu(c) @ ada_w[i] + ada_b[i], 6, axis=-1)
        csh1, csc1, cg1, csh2, csc2, cg2 = jnp.split(jax.nn.silu(c_ctx) @ ada_w[i] + ada_b[i], 6, axis=-1)
        h = modulate(x, sh1[:, None], sc1[:, None])
        hc = modulate(ctx, csh1, csc1)
        if m == 0:
            y, yc = retention_mixer(h, hc, ret_w_in[j], ret_decay_fwd[j], ret_decay_bwd[j], ret_w_out[j],
                                    row, col, need_ctx)
        elif m == 1:
            y, yc = gqa_mixer(h, hc, gqa_w_in[j], gqa_q_norm[j], gqa_k_norm[j], gqa_w_out[j],
                              row, col, need_ctx)
        elif m == 2:
            y, yc = mla_mixer(h, hc, mla_w_in[j], mla_q_norm[j], mla_w_q_up[j], mla_kv_norm[j],
                              mla_w_kv_up[j], mla_w_out[j], row, col, need_ctx)
        else:
            y, yc = hgrn2_mixer(h, hc, hgrn_w_in[j], lower_bounds[i], hgrn_out_norm[j], hgrn_w_out[j], need_ctx)
        x = layer_norm(DEEPNORM_ALPHA * x + g1[:, None] * y, ln_g[i, 0], ln_b[i, 0])
        h = modulate(x, sh2[:, None], sc2[:, None])
        x = layer_norm(DEEPNORM_ALPHA * x + g2[:, None] * swiglu(h, ffn_w_in[i], ffn_w_out[i]), ln_g[i, 1], ln_b[i, 1])
        if need_ctx:
            ctx = layer_norm(DEEPNORM_ALPHA * ctx + cg1 * yc, ln_g[i, 0], ln_b[i, 0])
            hc = modulate(ctx, csh2, csc2)
            ctx = layer_norm(DEEPNORM_ALPHA * ctx + cg2 * swiglu(hc, ffn_w_in[i], ffn_w_out[i]), ln_g[i, 1], ln_b[i, 1])
    return x
```

```python
import numpy as np
from contextlib import ExitStack
import concourse.bass as bass
import concourse.mybir as mybir
from concourse.alu_op_type import AluOpType as ALU
from concourse.bass_utils import run_bass_kernel_spmd

AF = mybir.ActivationFunctionType
F32 = mybir.dt.float32
BF16 = mybir.dt.bfloat16
AX = mybir.AxisListType

EPOCH = 32000
DMA_RING = 8
ARENA_BYTES = 207 * 1024
D = 2048
KC = 16
FFN_H = 5632
CTX_T = 2
GRID_W = 64
EPS = 1e-6
DEPTH = 4
ALPHA = (2 * DEPTH) ** 0.25
THETA = 10000.0


class Buf:
    __slots__ = ("key", "last_w", "readers")

    def __init__(self, key):
        self.key = key
        self.last_w = None
        self.readers = {}


class Op:
    __slots__ = ("idx", "eng", "fn", "deps", "stream", "tick", "signal", "isdma")


class Prog:
    ENG = ("pe", "act", "dve", "pool", "sp")

    def __init__(self, nc):
        self.nc = nc
        self.ops = []
        self.eng_ops = {e: [] for e in self.ENG}
        self.bufs = {}
        self.dma_cnt = {}
        self.last_on = {}
        self.last_all = {}

    def buf(self, key):
        b = self.bufs.get(key)
        if b is None:
            b = Buf(key)
            self.bufs[key] = b
        return b

    def op(self, eng, fn, reads=(), writes=(), dma=None):
        o = Op()
        o.idx = len(self.ops)
        o.eng = eng
        o.fn = fn
        o.isdma = dma is not None
        if dma is not None:
            n = self.dma_cnt.get(dma, 0)
            self.dma_cnt[dma] = n + 1
            o.stream = "dma_%s_%d" % (dma, n % DMA_RING)
        else:
            o.stream = eng
        o.signal = o.isdma
        o.tick = 0
        deps = {}
        ops = self.ops
        if o.isdma:
            prev = self.last_on.get(o.stream)
            if prev is not None:
                deps[o.stream] = prev
            self.last_on[o.stream] = o.idx

        def add(d):
            if d is None:
                return
            ps = ops[d].stream
            if ps == "pe" and o.stream == "pe":
                return
            if deps.get(ps, -1) < d:
                deps[ps] = d

        for b in reads:
            add(b.last_w)
        for b in writes:
            add(b.last_w)
            for d in b.readers.values():
                add(d)
        for b in writes:
            b.last_w = o.idx
            b.readers = {}
        wset = set(id(b) for b in writes)
        for b in reads:
            if id(b) not in wset:
                b.readers[o.stream] = o.idx
        o.deps = deps
        ops.append(o)
        self.eng_ops[eng].append(o)
        self.last_all[o.stream] = o.idx
        return o

    def barrier(self):
        snap = dict(self.last_all)
        for eng in self.ENG:
            o = Op()
            o.idx = len(self.ops)
            o.eng = eng
            o.fn = None
            o.isdma = False
            o.stream = eng
            o.signal = False
            o.tick = 0
            o.deps = {s_: d for s_, d in snap.items() if not (s_ == "pe" and eng == "pe")}
            self.ops.append(o)
            self.eng_ops[eng].append(o)

    def emit(self):
        nc = self.nc
        ops = self.ops
        for o in ops:
            for d in o.deps.values():
                ops[d].signal = True
        counters = {}
        for o in ops:
            if o.signal:
                counters[o.stream] = counters.get(o.stream, 0) + (16 if o.isdma else 1)
                o.tick = counters[o.stream]
        es = ExitStack()
        sems = {}
        for s, total in counters.items():
            n_ep = (total + EPOCH - 1) // EPOCH
            sems[s] = [es.enter_context(nc.semaphore("s_%s_%d" % (s, e))) for e in range(n_ep)]
        self.n_sems = sum(len(v) for v in sems.values())

        def sem_of(stream, tick):
            e = (tick - 1) // EPOCH
            return sems[stream][e], tick - e * EPOCH, e

        block = es.enter_context(nc.Block())

        def make_section(eng):
            my_ops = self.eng_ops[eng]

            def section(h):
                waited = {}
                ep_done = {}
                for o in my_ops:
                    for s, d in o.deps.items():
                        t = ops[d].tick
                        if waited.get(s, 0) >= t:
                            continue
                        sem, val, e = sem_of(s, t)
                        if s.startswith("dma_") and e > 0:
                            for pe_ in range(ep_done.get(s, 0), e):
                                h.wait_ge(sems[s][pe_], EPOCH)
                            ep_done[s] = max(ep_done.get(s, 0), e)
                        h.wait_ge(sem, val)
                        waited[s] = t
                    if o.fn is None:
                        continue
                    ins = o.fn(h)
                    if o.signal:
                        sem, val, e = sem_of(o.stream, o.tick)
                        ins.then_inc(sem, 16 if o.isdma else 1)
                mine = []
                for o in my_ops:
                    if o.isdma and o.stream not in mine:
                        mine.append(o.stream)
                for s in mine:
                    tot = counters[s]
                    for e in range(len(sems[s])):
                        h.wait_ge(sems[s][e], min(EPOCH, tot - e * EPOCH))
            return section

        for eng, reg in (("sp", block.sync), ("pe", block.tensor), ("act", block.scalar),
                         ("dve", block.vector), ("pool", block.gpsimd)):
            if self.eng_ops[eng]:
                reg(make_section(eng))
        es.close()


class Ctx:
    def __init__(self, nc, n_lat_tiles):
        self.nc = nc
        self.P = Prog(nc)
        self.es = ExitStack()
        self.NL = n_lat_tiles
        self.NT = CTX_T + n_lat_tiles
        self.cnt = 0
        self.rr = {}
        self.cap = ARENA_BYTES // 2
        self.arena = self.es.enter_context(nc.sbuf_tensor("arena", [128, self.cap], BF16))
        self.top = 0
        self.peak = 0

    def sb(self, name, shape, dt):
        n = 1
        for d_ in shape[1:]:
            n *= d_
        nb16 = n * (2 if dt == F32 else 1)
        off = self.top
        self.top += (nb16 + 15) // 16 * 16
        self.peak = max(self.peak, self.top)
        assert self.top <= self.cap, "SBUF arena overflow at %s: %d > %d" % (name, self.top * 2, self.cap * 2)
        v = self.arena[0:shape[0], off:off + nb16]
        if dt == F32:
            v = v.bitcast(F32)
        if len(shape) == 3:
            v = v.rearrange("p (a b) -> p a b", b=shape[2])
        return v

    def scope(self):
        C = self

        class _S:
            def __enter__(s_):
                s_.mark = C.top
                s_.rr = dict(C.rr)

            def __exit__(s_, *a):
                C.P.barrier()
                C.top = s_.mark
        return _S()

    def ps(self, name, shape, dt):
        return self.es.enter_context(self.nc.psum_tensor(name, shape, dt))

    def dram(self, name, shape, dt):
        return self.nc.dram_tensor(name, shape, dt, kind="Internal").ap()

    def b(self, key):
        return self.P.buf(key)

    def next(self, key, n):
        v = self.rr.get(key, 0)
        self.rr[key] = v + 1
        return v % n

    def blocks(self, tb=4):
        out = [[0, 1]]
        t = CTX_T
        while t < self.NT:
            out.append(list(range(t, min(t + tb, self.NT))))
            t += tb
        return out

    def hTv(self, kcn, ntok):
        return self.hT[:, 0:kcn * ntok].rearrange("p (k t) -> p k t", t=ntok)


def dbufs(C, name, t, c0, c1, gw=512):
    return [C.b((name, t, g)) for g in range(c0 // gw, (c1 - 1) // gw + 1)]


def setup_common(C):
    nc, P = C.nc, C.P
    C.identf = C.sb("identf", [128, 128], F32)
    C.identb = C.sb("identb", [128, 128], BF16)
    C.onesb = C.sb("onesb", [128, 128], BF16)
    P.op("sp", lambda h: h.dma_start(out=C.identf[:], in_=C.cst[:, 0:128]), writes=[C.b("identf")], dma="ld")
    P.op("dve", lambda h: h.tensor_copy(out=C.identb[:], in_=C.identf[:]), reads=[C.b("identf")], writes=[C.b("identb")])
    C.onesf = C.sb("onesf", [128, 128], F32)
    P.op("sp", lambda h: h.dma_start(out=C.onesf[:], in_=C.cst[:, 128:256]), writes=[C.b("onesf")], dma="ld")
    P.op("dve", lambda h: h.tensor_copy(out=C.onesb[:], in_=C.onesf[:]), reads=[C.b("onesf")], writes=[C.b("onesb")])
    C.epsc = C.sb("epsc", [128, 1], F32)
    P.op("sp", lambda h: h.dma_start(out=C.epsc[:], in_=C.cst[:, 256:257], allow_slow_non_contiguous=True), writes=[C.b("epsc")], dma="ld")
    C.gb = [C.ps("gb%d" % i, [128, 512], F32) for i in range(4)]
    C.ob = C.ps("ob", [128, 512], F32)
    C.db = C.ps("db", [128, 512], F32)
    C.tpf = C.ps("tpf", [128, 512], F32)
    C.tpbf = C.ps("tpb", [128, 512], F32)
    C.tpb = C.tpbf.bitcast(BF16)
    C.stg_f = [C.sb("stgf%d" % i, [128, 512], F32) for i in range(4)]
    C.stg_b = [C.sb("stgb%d" % i, [128, 512], BF16) for i in range(4)]
    C.hT = C.sb("hT", [128, 44 * 256], BF16)
    C.xin = [C.sb("xin%d" % i, [128, 2048], F32) for i in range(2)]
    C.xinb = [C.sb("xinb%d" % i, [128, 5632], BF16) for i in range(1)]
    C.wp = [C.sb("wp%d" % i, [128, 16, 512], BF16) for i in range(2)]
    C.sa = C.sb("sa", [128, 4, 512], F32)
    C.vecT = C.sb("vecT", [128, 8, KC], F32)


def cast_weight(C, name, w_ap, K, N, PW=512):
    kc = K // 128
    npan = N // PW
    wb = C.dram(name + "_bf", [npan, 128, kc, PW], BF16)
    for j in range(npan):
        src = w_ap[:, j * PW:(j + 1) * PW].rearrange("(k p) c -> p k c", p=128)
        C.P.op("pool", lambda h, j=j, src=src: h.dma_start(out=wb[j], in_=src),
               writes=[C.b((name, j))], dma="cast")
    return dict(ap=wb, name=name, kc=kc, npan=npan, pw=PW)


def evac(C, out_ap, in_ap, reads, writes, func=None, scale=1.0):
    if func is not None:
        C.P.op("act", lambda h: h.activation(out=out_ap, in_=in_ap, func=func, scale=scale), reads=reads, writes=writes)
        return
    if C.next("evac", 2) == 0:
        C.P.op("act", lambda h: h.activation(out=out_ap, in_=in_ap, func=AF.Copy), reads=reads, writes=writes)
    else:
        C.P.op("dve", lambda h: h.tensor_copy(out=out_ap, in_=in_ap), reads=reads, writes=writes)


def load_hT_mod(C, src, sname, tiles, scT, shT, vname):
    P = C.P
    for tl, t in enumerate(tiles):
        xi = C.next("xin", 2)
        xt = C.xin[xi]
        P.op("sp", lambda h, xt=xt, t=t: h.dma_start(out=xt[:], in_=src[t * 128:(t + 1) * 128, :]),
             reads=dbufs(C, sname, t, 0, D), writes=[C.b(("xin", xi))], dma="ld")
        for g in range(4):
            for q in range(4):
                kc = g * 4 + q
                P.op("pe", lambda h, xt=xt, kc=kc, q=q: h.transpose(C.tpf[:, q * 128:(q + 1) * 128],
                                                                   xt[:, kc * 128:(kc + 1) * 128], C.identf[:]),
                     reads=[C.b(("xin", xi)), C.b("identf")], writes=[C.b("tpf")])
            for q in range(4):
                kc = g * 4 + q
                P.op("act", lambda h, kc=kc, q=q, tl=tl: h.activation(
                    out=C.hTv(KC, 512)[:, kc, tl * 128:(tl + 1) * 128], in_=C.tpf[:, q * 128:(q + 1) * 128],
                    func=AF.Identity, scale=scT[:, kc:kc + 1], bias=shT[:, kc:kc + 1]),
                    reads=[C.b("tpf"), C.b(vname)], writes=[C.b(("hT", tl))])


def ntok_for(K):
    return 512 if K <= 2816 else 256


def load_hT_bf(C, src, sname, tiles, K):
    P = C.P
    kcn = K // 128
    hv = C.hTv(kcn, ntok_for(K))
    for tl, t in enumerate(tiles):
        xi = C.next("xinb", 1)
        xt = C.xinb[xi]
        P.op("sp", lambda h, xt=xt, t=t: h.dma_start(out=xt[:, 0:K], in_=src[t * 128:(t + 1) * 128, :]),
             reads=dbufs(C, sname, t, 0, K), writes=[C.b(("xinb", xi))], dma="ld")
        for g0 in range(0, kcn, 8):
            n = min(8, kcn - g0)
            for q in range(n):
                kc = g0 + q
                P.op("pe", lambda h, xt=xt, kc=kc, q=q: h.transpose(C.tpb[:, q * 128:(q + 1) * 128],
                                                                   xt[:, kc * 128:(kc + 1) * 128], C.identb[:]),
                     reads=[C.b(("xinb", xi)), C.b("identb")], writes=[C.b("tpb")])
            src_ap = C.tpb[:, 0:n * 128].rearrange("p (k c) -> p k c", c=128)
            dst_ap = hv[:, g0:g0 + n, tl * 128:(tl + 1) * 128]
            evac(C, dst_ap, src_ap, [C.b("tpb")], [C.b(("hT", tl))])


def load_hT_fm(C, srcT, sname, tiles, K):
    kcn = K // 128
    t0 = tiles[0]
    n = len(tiles) * 128
    hv = C.hTv(kcn, ntok_for(K))
    C.P.op("sp", lambda h: h.dma_start(out=hv[:, 0:kcn, 0:n],
                                       in_=srcT[:, t0 * 128:t0 * 128 + n].rearrange("(k p) t -> p k t", p=128)),
           reads=[C.b((sname, t)) for t in tiles], writes=[C.b(("hT", tl)) for tl in range(len(tiles))], dma="ld")


def linear(C, W, tiles, epi, panels=None, hv=None, hbufs=None):
    P = C.P
    kc_tot, pw = W["kc"], W["pw"]
    panels = list(range(W["npan"])) if panels is None else panels
    kgroups = [(k0, min(16, kc_tot - k0)) for k0 in range(0, kc_tot, 16)]
    if hv is None:
        hv = C.hTv(kc_tot, ntok_for(kc_tot * 128))
        hbufs = [C.b(("hT", tl)) for tl in range(len(tiles))]
    for j in panels:
        banks = []
        for tl, t in enumerate(tiles):
            banks.append(C.next("gb", 4))
        for gi, (k0, kn) in enumerate(kgroups):
            wi = C.next("wp", 2)
            wt = C.wp[wi]
            P.op("sp", lambda h, wt=wt, j=j, k0=k0, kn=kn: h.dma_start(out=wt[:, 0:kn, 0:pw], in_=W["ap"][j, :, k0:k0 + kn, :]),
                 reads=[C.b((W["name"], j))], writes=[C.b(("wp", wi))], dma="ld")
            for tl, t in enumerate(tiles):
                bi = banks[tl]
                for k in range(kn):
                    kk = k0 + k
                    P.op("pe", lambda h, bi=bi, tl=tl, kk=kk, k=k, wt=wt: h.matmul(
                        C.gb[bi][:, 0:pw], lhsT=hv[:, kk, tl * 128:(tl + 1) * 128], rhs=wt[:, k, 0:pw],
                        start=(kk == 0), stop=(kk == kc_tot - 1)),
                        reads=[hbufs[tl], C.b(("wp", wi))], writes=[C.b(("gb", bi))])
        for tl, t in enumerate(tiles):
            bi = banks[tl]
            epi(t, tl, j, C.gb[bi], C.b(("gb", bi)))


def store(C, dst, dname, t, c0, c1, src_ap, src_buf, gw=512):
    C.P.op("sp", lambda h: h.dma_start(out=dst[t * 128:(t + 1) * 128, c0:c1], in_=src_ap),
           reads=[src_buf], writes=dbufs(C, dname, t, c0, c1, gw), dma="st")


def epi_copy(C, dst, dname, dt, func=None):
    ring = C.stg_f if dt == F32 else C.stg_b
    rname = "stgf" if dt == F32 else "stgb"

    def epi(t, tl, j, bank, bbuf, pw=512):
        si = C.next(rname, 4)
        sbuf = C.b((rname, si))
        evac(C, ring[si][:, 0:pw], bank[:, 0:pw], [bbuf], [sbuf], func=func)
        store(C, dst, dname, t, j * pw, (j + 1) * pw, ring[si][:, 0:pw], sbuf)
    return epi


def adaln_all(C, cvec, ada_w, ada_b, layers):
    P = C.P
    cs = C.sb("cs", [2, D], F32)
    csT = C.sb("csT", [128, KC, 2], BF16)
    C.csT = csT
    P.op("sp", lambda h: h.dma_start(out=cs[:], in_=cvec), writes=[C.b("cs")], dma="ld")
    P.op("act", lambda h: h.activation(out=cs[:], in_=cs[:], func=AF.Silu), reads=[C.b("cs")], writes=[C.b("cs")])
    for kc in range(KC):
        P.op("pe", lambda h, kc=kc: h.transpose(C.tpf[:, kc * 2:kc * 2 + 2], cs[:, kc * 128:(kc + 1) * 128], C.identf[0:2, 0:2]),
             reads=[C.b("cs"), C.b("identf")], writes=[C.b("tpf")])
    P.op("dve", lambda h: h.tensor_copy(out=csT[:], in_=C.tpf[:, 0:KC * 2].rearrange("p (k c) -> p k c", c=2)),
         reads=[C.b("tpf")], writes=[C.b("csT")])
    MOD = C.dram("MOD", [DEPTH, 2, 6 * D], F32)
    C.MOD = MOD
    mrow = [C.sb("mrow%d" % i, [2, 512], F32) for i in range(2)]
    brow = [C.sb("brow%d" % i, [2, 512], F32) for i in range(2)]
    for l in layers:
        for j in range(24):
            ri = C.next("mrow", 2)
            for r in range(2):
                P.op("sp", lambda h, l=l, r=r, j=j, ri=ri: h.dma_start(out=brow[ri][r:r + 1, :], in_=ada_b[l:l + 1, j * 512:(j + 1) * 512]),
                     writes=[C.b(("brow", ri))], dma="ld")
            wi = C.next("wp", 2)
            wt = C.wp[wi]
            src = ada_w[l][:, j * 512:(j + 1) * 512].rearrange("(k p) c -> p k c", p=128)
            P.op("pool", lambda h, wt=wt, src=src: h.dma_start(out=wt[:], in_=src), writes=[C.b(("wp", wi))], dma="cast")
            bi = C.next("gb", 4)
            for kc in range(KC):
                P.op("pe", lambda h, bi=bi, kc=kc, wt=wt: h.matmul(C.gb[bi][0:2, :], lhsT=csT[:, kc, :], rhs=wt[:, kc, :],
                                                                  start=(kc == 0), stop=(kc == KC - 1)),
                     reads=[C.b("csT"), C.b(("wp", wi))], writes=[C.b(("gb", bi))])
            P.op("dve", lambda h, bi=bi, ri=ri: h.tensor_tensor(out=mrow[ri][:], in0=C.gb[bi][0:2, :], in1=brow[ri][:], op=ALU.add),
                 reads=[C.b(("gb", bi)), C.b(("brow", ri))], writes=[C.b(("mrow", ri))])
            if 4 <= j < 8 or 16 <= j < 20:
                P.op("dve", lambda h, ri=ri: h.tensor_scalar_add(out=mrow[ri][:], in0=mrow[ri][:], scalar1=1.0),
                     reads=[C.b(("mrow", ri))], writes=[C.b(("mrow", ri))])
            P.op("sp", lambda h, l=l, j=j, ri=ri: h.dma_start(out=MOD[l, :, j * 512:(j + 1) * 512], in_=mrow[ri][:]),
                 reads=[C.b(("mrow", ri))], writes=[C.b(("MOD", l))], dma="st")


def load_layer_vectors(C, l, ln_g, ln_b):
    P = C.P
    for wi, off in enumerate((0, D, 3 * D, 4 * D)):
        for r in range(2):
            src = C.MOD[l, r, off:off + D].rearrange("(k p) -> p k", p=128)
            P.op("sp", lambda h, wi=wi, r=r, src=src: h.dma_start(out=C.vecT[:, wi * 2 + r, :], in_=src, allow_slow_non_contiguous=True),
                 reads=[C.b(("MOD", l))], writes=[C.b("vecT")], dma="ld")


def load_bcast(C, l, sub, ln_g, ln_b):
    P = C.P
    C.bc = [C.sb("bc%d" % i, [128, D], F32) for i in range(4)]
    goff = 2 * D if sub == 0 else 5 * D
    srcs = [C.MOD[l, 0:1, goff:goff + D], C.MOD[l, 1:2, goff:goff + D], ln_g[l, sub:sub + 1, :], ln_b[l, sub:sub + 1, :]]
    for i, s in enumerate(srcs):
        P.op("sp", lambda h, i=i, s=s: h.dma_start(out=C.bc[i][:], in_=s.partition_broadcast(128)),
             reads=[C.b(("MOD", l))], writes=[C.b(("bc", i))], dma="ld")


def resid_ln(C, X, xname, Y, yname, OUT, oname, out_row0=None):
    P = C.P
    C.rl_x = C.xin
    C.rl_y = [C.sb("rly%d" % i, [128, D], F32) for i in range(2)]
    C.rl_st = C.sb("rlst", [128, 4, 6], F32)
    C.rl_mv = C.sb("rlmv", [128, 4], F32)
    for t in range(C.NT):
        if out_row0 is not None and t < CTX_T:
            continue
        i = C.next("xin", 2)
        xt, yt = C.rl_x[i], C.rl_y[i]
        bx, by = C.b(("xin", i)), C.b(("rly", i))
        P.op("sp", lambda h, xt=xt, t=t: h.dma_start(out=xt[:], in_=X[t * 128:(t + 1) * 128, :]),
             reads=dbufs(C, xname, t, 0, D), writes=[bx], dma="ld")
        P.op("sp", lambda h, yt=yt, t=t: h.dma_start(out=yt[:], in_=Y[t * 128:(t + 1) * 128, :]),
             reads=dbufs(C, yname, t, 0, D), writes=[by], dma="ld")
        gi = 1 if t < CTX_T else 0
        P.op("dve", lambda h, yt=yt, gi=gi: h.tensor_tensor(out=yt[:], in0=yt[:], in1=C.bc[gi][:], op=ALU.mult),
             reads=[by, C.b(("bc", gi))], writes=[by])
        P.op("dve", lambda h, xt=xt, yt=yt: h.scalar_tensor_tensor(out=yt[:], in0=xt[:], scalar=ALPHA, in1=yt[:],
                                                                  op0=ALU.mult, op1=ALU.add),
             reads=[bx, by], writes=[by])
        for q in range(4):
            P.op("dve", lambda h, yt=yt, q=q: h.bn_stats(out=C.rl_st[:, q, :], in_=yt[:, q * 512:(q + 1) * 512]),
                 reads=[by], writes=[C.b("rlst")])
        P.op("dve", lambda h: h.bn_aggr(out=C.rl_mv[:, 0:2], in_=C.rl_st[:].rearrange("p a b -> p (a b)")),
             reads=[C.b("rlst")], writes=[C.b("rlmv")])
        P.op("act", lambda h: h.activation(out=C.rl_mv[:, 2:3], in_=C.rl_mv[:, 1:2], func=AF.Sqrt, bias=C.epsc[:, 0:1], scale=1.0),
             reads=[C.b("rlmv"), C.b("epsc")], writes=[C.b("rlmv")])
        P.op("dve", lambda h: h.reciprocal(out=C.rl_mv[:, 2:3], in_=C.rl_mv[:, 2:3]), reads=[C.b("rlmv")], writes=[C.b("rlmv")])
        P.op("dve", lambda h: h.scalar_tensor_tensor(out=C.rl_mv[:, 3:4], in0=C.rl_mv[:, 0:1], scalar=-1.0, in1=C.rl_mv[:, 2:3],
                                                     op0=ALU.mult, op1=ALU.mult),
             reads=[C.b("rlmv")], writes=[C.b("rlmv")])
        P.op("act", lambda h, xt=xt, yt=yt: h.activation(out=xt[:], in_=yt[:], func=AF.Identity, scale=C.rl_mv[:, 2:3], bias=C.rl_mv[:, 3:4]),
             reads=[by, C.b("rlmv")], writes=[bx])
        P.op("dve", lambda h, xt=xt: h.tensor_tensor(out=xt[:], in0=xt[:], in1=C.bc[2][:], op=ALU.mult),
             reads=[bx, C.b(("bc", 2))], writes=[bx])
        P.op("dve", lambda h, xt=xt: h.tensor_tensor(out=xt[:], in0=xt[:], in1=C.bc[3][:], op=ALU.add),
             reads=[bx, C.b(("bc", 3))], writes=[bx])
        if out_row0 is None:
            P.op("sp", lambda h, xt=xt, t=t: h.dma_start(out=OUT[t * 128:(t + 1) * 128, :], in_=xt[:]),
                 reads=[bx], writes=dbufs(C, oname, t, 0, D), dma="st")
        else:
            r0 = (t - CTX_T) * 128
            P.op("sp", lambda h, xt=xt, r0=r0: h.dma_start(out=OUT[r0:r0 + 128, :], in_=xt[:]),
                 reads=[bx], writes=[C.b((oname, t))], dma="st")


def ffn(C, l, X, xname, Wi, Wo, U, Y2):
    P = C.P

    for tiles in C.blocks():
        r = 1 if tiles[0] < CTX_T else 0
        load_hT_mod(C, X, xname, tiles, C.vecT[:, 3 * 2 + r, :], C.vecT[:, 2 * 2 + r, :], "vecT")

        def epi(t, tl, j, bank, bbuf):
            if j < 11:
                P.op("act", lambda h: h.activation(out=C.sa[:, tl, :], in_=bank[:], func=AF.Silu),
                     reads=[bbuf], writes=[C.b(("sa", tl))])
            else:
                si = C.next("stgb", 4)
                sbuf = C.b(("stgb", si))
                P.op("dve", lambda h: h.tensor_tensor(out=C.stg_b[si][:], in0=bank[:], in1=C.sa[:, tl, :], op=ALU.mult),
                     reads=[bbuf, C.b(("sa", tl))], writes=[sbuf])
                store(C, U, "U", t, (j - 11) * 512, (j - 10) * 512, C.stg_b[si][:], sbuf)
        order = []
        for j in range(11):
            order += [j, 11 + j]
        linear(C, Wi, tiles, epi, panels=order)
    for tiles in C.blocks(2):
        load_hT_bf(C, U, "U", tiles, FFN_H)
        linear(C, Wo, tiles, epi_copy(C, Y2, "Y2", F32))


def out_proj_tok(C, SRC, sname, K, Wo, Y):
    for tiles in C.blocks(4 if K <= 2816 else 2):
        load_hT_bf(C, SRC, sname, tiles, K)
        linear(C, Wo, tiles, epi_copy(C, Y, "Y", F32))


def linear_fm(C, W, ntok, epi, mchunks=None, hv=None, hbufs=None):
    P = C.P
    kc_tot, pw = W["kc"], W["pw"]
    cw = min(128, pw)
    cpp = pw // cw
    if hv is None:
        hv = C.hTv(kc_tot, ntok_for(kc_tot * 128))
        hbufs = [C.b(("hT", tl)) for tl in range((ntok + 127) // 128)]
    for j in range(W["npan"]):
        ms = [m for m in range(j * cpp, j * cpp + cpp) if mchunks is None or m in mchunks]
        if not ms:
            continue
        wi = C.next("wp", 2)
        wt = C.wp[wi]
        P.op("sp", lambda h, wt=wt, j=j: h.dma_start(out=wt[:, 0:kc_tot, 0:pw], in_=W["ap"][j, :, :, :]),
             reads=[C.b((W["name"], j))], writes=[C.b(("wp", wi))], dma="ld")
        for m in ms:
            sub = m % cpp
            bi = C.next("gb", 4)
            for k in range(kc_tot):
                P.op("pe", lambda h, bi=bi, k=k, wt=wt, sub=sub: h.matmul(
                    C.gb[bi][0:cw, 0:ntok], lhsT=wt[:, k, sub * cw:(sub + 1) * cw], rhs=hv[:, k, 0:ntok],
                    start=(k == 0), stop=(k == kc_tot - 1)),
                    reads=list(hbufs) + [C.b(("wp", wi))], writes=[C.b(("gb", bi))])
            epi(m, C.gb[bi], C.b(("gb", bi)))


def store_rows(C, dst, dname, key, r0, nrows, t0, ntok, src_ap, src_buf):
    tl = list(range(t0, t0 + (ntok + 127) // 128))
    C.P.op("sp", lambda h: h.dma_start(out=dst[r0:r0 + nrows, t0 * 128:t0 * 128 + ntok], in_=src_ap),
           reads=[src_buf], writes=[C.b((dname, key, t)) for t in tl], dma="st")


def store_fm(C, dst, dname, m, t0, ntok, src_ap, src_buf):
    tl = list(range(t0, t0 + (ntok + 127) // 128))
    C.P.op("sp", lambda h: h.dma_start(out=dst[m * 128:(m + 1) * 128, t0 * 128:t0 * 128 + ntok], in_=src_ap),
           reads=[src_buf], writes=[C.b((dname, m, t)) for t in tl], dma="st")


def attn_core(C, groups, scale, OT, oname):
    P = C.P
    T = C.NT * 128
    allt = list(range(C.NT))
    if True:
        C.att_kT = C.sb("att_kT", [128, 2, T], BF16)
        C.att_v = C.sb("att_v", [128, C.NT, 128], BF16)
        C.att_qT = [C.sb("att_qT%d" % i, [128, 2, 512], BF16) for i in range(2)]
        C.att_pT = [C.sb("att_pT%d" % i, [128, 512], BF16) for i in range(3)]
        C.att_rd = C.sb("att_rd", [128, 512], F32)
        C.att_oo = [C.sb("att_oo%d" % i, [128, 512], BF16) for i in range(2)]
    kT, vv, qT, pT, rd, oo = C.att_kT, C.att_v, C.att_qT, C.att_pT, C.att_rd, C.att_oo
    for grp in groups:
        nch = len(grp["k_chunks"])
        for ci, (ap, r0, nr, kp) in enumerate(grp["k_chunks"]):
            P.op("sp", lambda h, ci=ci, ap=ap, r0=r0, nr=nr: h.dma_start(out=kT[0:nr, ci, :], in_=ap[r0:r0 + nr, :]),
                 reads=[C.b(kp + (t,)) for t in allt], writes=[C.b("att_kT")], dma="ld")
        vap, vname, vc0, vg = grp["v"]
        P.op("sp", lambda h, vap=vap, vc0=vc0: h.dma_start(out=vv[:], in_=vap[:, vc0:vc0 + 128].rearrange("(t p) c -> p t c", p=128)),
             reads=[C.b((vname, t, (vc0 // 512))) for t in allt], writes=[C.b("att_v")], dma="ld")
        for hd in grp["heads"]:
            for tiles in C.blocks():
                nq = len(tiles) * 128
                t0 = tiles[0]
                keys = [0, 1] if t0 < CTX_T else allt
                qi = C.next("att_qT", 2)
                for ci, (ap, r0, nr, kp) in enumerate(hd["q_chunks"]):
                    P.op("sp", lambda h, qi=qi, ci=ci, ap=ap, r0=r0, nr=nr, t0=t0, nq=nq: h.dma_start(
                        out=qT[qi][0:nr, ci, 0:nq], in_=ap[r0:r0 + nr, t0 * 128:t0 * 128 + nq]),
                        reads=[C.b(kp + (t,)) for t in tiles], writes=[C.b(("att_qT", qi))], dma="ld")
                for ki, kt in enumerate(keys):
                    bi = C.next("gb", 4)
                    for ci, (ap, r0, nr, kp) in enumerate(grp["k_chunks"]):
                        P.op("pe", lambda h, bi=bi, kt=kt, qi=qi, nq=nq, ci=ci, nr=nr: h.matmul(
                            C.gb[bi][:, 0:nq], lhsT=kT[0:nr, ci, kt * 128:(kt + 1) * 128], rhs=qT[qi][0:nr, ci, 0:nq],
                            start=(ci == 0), stop=(ci == nch - 1)),
                            reads=[C.b("att_kT"), C.b(("att_qT", qi))], writes=[C.b(("gb", bi))])
                    pi = C.next("att_pT", 3)
                    P.op("act", lambda h, bi=bi, pi=pi, nq=nq: h.activation(out=pT[pi][:, 0:nq], in_=C.gb[bi][:, 0:nq], func=AF.Exp, scale=scale),
                         reads=[C.b(("gb", bi))], writes=[C.b(("att_pT", pi))])
                    first, lastk = (ki == 0), (ki == len(keys) - 1)
                    P.op("pe", lambda h, kt=kt, pi=pi, nq=nq, first=first, lastk=lastk: h.matmul(C.ob[:, 0:nq], lhsT=vv[:, kt, :], rhs=pT[pi][:, 0:nq], start=first, stop=lastk),
                         reads=[C.b("att_v"), C.b(("att_pT", pi))], writes=[C.b("ob")])
                    P.op("pe", lambda h, pi=pi, nq=nq, first=first, lastk=lastk: h.matmul(C.db[:, 0:nq], lhsT=C.onesb[:], rhs=pT[pi][:, 0:nq], start=first, stop=lastk),
                         reads=[C.b("onesb"), C.b(("att_pT", pi))], writes=[C.b("db")])
                P.op("dve", lambda h, nq=nq: h.reciprocal(out=rd[:, 0:nq], in_=C.db[:, 0:nq]), reads=[C.b("db")], writes=[C.b("att_rd")])
                oi = C.next("att_oo", 2)
                P.op("dve", lambda h, nq=nq, oi=oi: h.tensor_tensor(out=oo[oi][:, 0:nq], in0=C.ob[:, 0:nq], in1=rd[:, 0:nq], op=ALU.mult),
                     reads=[C.b("ob"), C.b("att_rd")], writes=[C.b(("att_oo", oi))])
                store_rows(C, OT, oname, hd["out"], hd["out"] * 128, 128, t0, nq, oo[oi][:, 0:nq], C.b(("att_oo", oi)))


def out_proj_fm(C, OT, oname, nchunks, Wo, Y):
    P = C.P
    K_ = nchunks * 128
    nt = ntok_for(K_)
    for tiles in C.blocks(nt // 128):
        hv = C.hTv(nchunks, nt)
        n = len(tiles) * 128
        t0 = tiles[0]
        P.op("sp", lambda h, hv=hv, n=n, t0=t0: h.dma_start(out=hv[:, 0:nchunks, 0:n], in_=OT[:, t0 * 128:t0 * 128 + n].rearrange("(k p) t -> p k t", p=128)),
             reads=[C.b((oname, m, t)) for m in range(nchunks) for t in tiles], writes=[C.b(("hT", tl)) for tl in range(len(tiles))], dma="ld")
        linear(C, Wo, tiles, epi_copy(C, Y, "Y", F32))


def gqa_mixer(C, l, X, Y, w_in, w_in_sw, qk_gain, rope_tab, w_out):
    P = C.P
    T = C.NT * 128
    Wq = cast_weight(C, "gqa_in%d" % l, w_in, D, 3072)
    Ws = cast_weight(C, "gqa_sw%d" % l, w_in_sw, D, 2560)
    Wo = cast_weight(C, "gqa_out%d" % l, w_out, D, D)
    QT = C.dram("gqa_QT", [2560, T], BF16)
    V = C.dram("gqa_V", [T, 512], BF16)
    OT = C.dram("gqa_OT", [D, T], BF16)
    tab = C.sb("ropetab", [128, 2, 512], F32)
    gq = C.sb("gqag", [128, 4], F32)
    P.op("sp", lambda h: h.dma_start(out=gq[:], in_=qk_gain.rearrange("a p -> p a"), allow_slow_non_contiguous=True),
         writes=[C.b("gqag")], dma="ld")
    sq = C.sb("gqa_sq", [128, 512], BF16)
    rs = C.sb("gqa_rs", [128, 512], F32)
    t1 = C.sb("gqa_t1", [128, 512], F32)
    t2 = C.sb("gqa_t2", [128, 512], F32)
    qo = [C.sb("gqa_qo%d" % i, [128, 512], BF16) for i in range(2)]

    def mk_epis(m, w, ntok, t0):
        def epi_a(mm, bank, bbuf):
            P.op("act", lambda h: h.activation(out=sq[:, 0:ntok], in_=bank[:, 0:ntok], func=AF.Square), reads=[bbuf], writes=[C.b("gqa_sq")])
            P.op("pe", lambda h: h.matmul(C.db[:, 0:ntok], lhsT=C.onesb[:], rhs=sq[:, 0:ntok], start=True, stop=True),
                 reads=[C.b("gqa_sq"), C.b("onesb")], writes=[C.b("db")])
            P.op("act", lambda h: h.activation(out=rs[:, 0:ntok], in_=C.db[:, 0:ntok], func=AF.Sqrt, scale=1.0 / 128, bias=C.epsc[:, 0:1]),
                 reads=[C.b("db"), C.b("epsc")], writes=[C.b("gqa_rs")])
            P.op("dve", lambda h: h.reciprocal(out=rs[:, 0:ntok], in_=rs[:, 0:ntok]), reads=[C.b("gqa_rs")], writes=[C.b("gqa_rs")])
            P.op("dve", lambda h: h.scalar_tensor_tensor(out=t1[:, 0:ntok], in0=bank[:, 0:ntok], scalar=gq[:, 2 * w:2 * w + 1], in1=tab[:, 0, 0:ntok],
                                                         op0=ALU.mult, op1=ALU.mult),
                 reads=[bbuf, C.b("ropetab"), C.b("gqag")], writes=[C.b("gqa_t1")])

        def epi_b(mm, bank, bbuf):
            P.op("dve", lambda h: h.scalar_tensor_tensor(out=t2[:, 0:ntok], in0=bank[:, 0:ntok], scalar=gq[:, 2 * w + 1:2 * w + 2], in1=tab[:, 1, 0:ntok],
                                                         op0=ALU.mult, op1=ALU.mult),
                 reads=[bbuf, C.b("ropetab"), C.b("gqag")], writes=[C.b("gqa_t2")])
            P.op("dve", lambda h: h.tensor_tensor(out=t1[:, 0:ntok], in0=t1[:, 0:ntok], in1=t2[:, 0:ntok], op=ALU.add),
                 reads=[C.b("gqa_t1"), C.b("gqa_t2")], writes=[C.b("gqa_t1")])
            qi = C.next("gqa_qo", 2)
            P.op("dve", lambda h: h.tensor_tensor(out=qo[qi][:, 0:ntok], in0=t1[:, 0:ntok], in1=rs[:, 0:ntok], op=ALU.mult),
                 reads=[C.b("gqa_t1"), C.b("gqa_rs")], writes=[C.b(("gqa_qo", qi))])
            store_rows(C, QT, "gqa_QT", m, m * 128, 128, t0, ntok, qo[qi][:, 0:ntok], C.b(("gqa_qo", qi)))
        return epi_a, epi_b

    for tiles in C.blocks():
        r = 1 if tiles[0] < CTX_T else 0
        ntok = len(tiles) * 128
        t0 = tiles[0]
        load_hT_mod(C, X, "X", tiles, C.vecT[:, 1 * 2 + r, :], C.vecT[:, 0 * 2 + r, :], "vecT")
        for i in range(2):
            P.op("sp", lambda h, i=i, t0=t0, ntok=ntok: h.dma_start(out=tab[:, i, 0:ntok], in_=rope_tab[i, :, t0 * 128:t0 * 128 + ntok]),
                 writes=[C.b("ropetab")], dma="ld")
        for m in range(20):
            ea, eb = mk_epis(m, 0 if m < 16 else 1, ntok, t0)
            linear_fm(C, Wq, ntok, ea, mchunks=[m])
            linear_fm(C, Ws, ntok, eb, mchunks=[m])
        ecv = epi_copy(C, V, "gqa_V", BF16)
        linear(C, Wq, tiles, lambda t, tl, j, bank, bbuf, ecv=ecv: ecv(t, tl, j - 5, bank, bbuf), panels=[5])

    groups = []
    for g in range(4):
        groups.append(dict(k_chunks=[(QT, (16 + g) * 128, 128, ("gqa_QT", 16 + g))], v=(V, "gqa_V", g * 128, 0),
                           heads=[dict(q_chunks=[(QT, (g * 4 + hh) * 128, 128, ("gqa_QT", g * 4 + hh))], out=g * 4 + hh) for hh in range(4)]))
    attn_core(C, groups, 128 ** -0.5, OT, "gqa_OT")
    out_proj_fm(C, OT, "gqa_OT", 16, Wo, Y)


def mla_mixer(C, l, X, Y, w_a, w_kr, w_kr_sw, w_qn, w_qr, w_qr_sw, w_kn, w_v, norms, rope_tab, w_out):
    P = C.P
    T = C.NT * 128
    Wa = cast_weight(C, "mla_a%d" % l, w_a, D, 1024)
    import os
    NC_ = int(os.environ.get("MLA_NCAST", "99"))
    specs = [("mla_kr", w_kr, D, 64, 64), ("mla_krs", w_kr_sw, D, 64, 64), ("mla_qn", w_qn, 512, 2048, 512), ("mla_qr", w_qr, 512, 1024, 64),
             ("mla_qrs", w_qr_sw, 512, 1024, 64), ("mla_kn", w_kn, 512, 2048, 512), ("mla_v", w_v, 512, 2048, 512), ("mla_out", w_out, D, D, 512)]
    Ws = []
    for i, (nm, w_, k_, n_, pw_) in enumerate(specs):
        if i >= NC_:
            return
        Ws.append(cast_weight(C, nm + "%d" % l, w_, k_, n_, PW=pw_))
    Wkr, Wkrs, Wqn, Wqr, Wqrs, Wkn, Wv, Wo = Ws
    KN = C.dram("mla_KN", [2048, T], BF16)
    KR = C.dram("mla_KR", [64, T], BF16)
    QN = C.dram("mla_QN", [2048, T], BF16)
    QR = C.dram("mla_QR", [1024, T], BF16)
    V = C.dram("mla_V", [T, 2048], BF16)
    OT = C.dram("mla_OT", [D, T], BF16)
    tab = C.sb("mla_tab", [64, 2, 512], F32)
    gn = C.sb("mla_gn", [128, 2, 4], F32)
    for i in range(2):
        P.op("sp", lambda h, i=i: h.dma_start(out=gn[:, i, :], in_=norms[i].rearrange("(k p) -> p k", p=128), allow_slow_non_contiguous=True),
             writes=[C.b("mla_gn")], dma="ld")
    sq = C.sb("mla_sq", [128, 512], BF16)
    rs = C.sb("mla_rs", [128, 512], F32)
    ssa = C.sb("mla_ssa", [128, 512], F32)
    raw = C.sb("mla_raw", [128, 4, 512], F32)
    cn = [C.sb("mla_cn%d" % i, [128, 4, 512], BF16) for i in range(2)]
    t1 = C.sb("mla_t1", [64, 512], F32)
    t2 = C.sb("mla_t2", [64, 512], F32)
    ro = [C.sb("mla_ro%d" % i, [64, 512], BF16) for i in range(2)]

    def mk_epi_a(ntok):
        def epi_a(m, bank, bbuf):
            grp, mm = m // 4, m % 4
            P.op("act", lambda h: h.activation(out=sq[:, 0:ntok], in_=bank[:, 0:ntok], func=AF.Square), reads=[bbuf], writes=[C.b("mla_sq")])
            P.op("pe", lambda h: h.matmul(C.db[:, 0:ntok], lhsT=C.onesb[:], rhs=sq[:, 0:ntok], start=True, stop=True),
                 reads=[C.b("mla_sq"), C.b("onesb")], writes=[C.b("db")])
            if mm == 0:
                P.op("dve", lambda h: h.tensor_copy(out=ssa[:, 0:ntok], in_=C.db[:, 0:ntok]), reads=[C.b("db")], writes=[C.b("mla_ssa")])
            else:
                P.op("dve", lambda h: h.tensor_tensor(out=ssa[:, 0:ntok], in0=C.db[:, 0:ntok], in1=ssa[:, 0:ntok], op=ALU.add),
                     reads=[C.b("db"), C.b("mla_ssa")], writes=[C.b("mla_ssa")])
            P.op("dve", lambda h: h.tensor_copy(out=raw[:, mm, 0:ntok], in_=bank[:, 0:ntok]), reads=[bbuf], writes=[C.b(("mla_raw", mm))])
            if mm == 3:
                P.op("act", lambda h: h.activation(out=rs[:, 0:ntok], in_=ssa[:, 0:ntok], func=AF.Sqrt, scale=1.0 / 512, bias=C.epsc[:, 0:1]),
                     reads=[C.b("mla_ssa"), C.b("epsc")], writes=[C.b("mla_rs")])
                P.op("dve", lambda h: h.reciprocal(out=rs[:, 0:ntok], in_=rs[:, 0:ntok]), reads=[C.b("mla_rs")], writes=[C.b("mla_rs")])
                for q in range(4):
                    P.op("dve", lambda h, q=q: h.scalar_tensor_tensor(out=cn[grp][:, q, 0:ntok], in0=raw[:, q, 0:ntok], scalar=gn[:, grp, q:q + 1],
                                                                     in1=rs[:, 0:ntok], op0=ALU.mult, op1=ALU.mult),
                         reads=[C.b(("mla_raw", q)), C.b("mla_rs"), C.b("mla_gn")], writes=[C.b(("mla_cn", grp))])
        return epi_a

    def mk_rope_epis(dst, dname, key, r0, ntok, t0):
        def ea(m, bank, bbuf):
            P.op("dve", lambda h: h.tensor_tensor(out=t1[:, 0:ntok], in0=bank[0:64, 0:ntok], in1=tab[:, 0, 0:ntok], op=ALU.mult),
                 reads=[bbuf, C.b("mla_tab")], writes=[C.b("mla_t1")])

        def eb(m, bank, bbuf):
            P.op("dve", lambda h: h.tensor_tensor(out=t2[:, 0:ntok], in0=bank[0:64, 0:ntok], in1=tab[:, 1, 0:ntok], op=ALU.mult),
                 reads=[bbuf, C.b("mla_tab")], writes=[C.b("mla_t2")])
            ri = C.next("mla_ro", 2)
            P.op("dve", lambda h: h.tensor_tensor(out=ro[ri][:, 0:ntok], in0=t1[:, 0:ntok], in1=t2[:, 0:ntok], op=ALU.add),
                 reads=[C.b("mla_t1"), C.b("mla_t2")], writes=[C.b(("mla_ro", ri))])
            store_rows(C, dst, dname, key, r0, 64, t0, ntok, ro[ri][:, 0:ntok], C.b(("mla_ro", ri)))
        return ea, eb

    def mk_copy_fm(dst, dname, ntok, t0):
        def e(m, bank, bbuf):
            si = C.next("stgb", 4)
            sbuf = C.b(("stgb", si))
            evac(C, C.stg_b[si][:, 0:ntok], bank[:, 0:ntok], [bbuf], [sbuf])
            store_rows(C, dst, dname, m, m * 128, 128, t0, ntok, C.stg_b[si][:, 0:ntok], sbuf)
        return e

    for tiles in C.blocks():
        r = 1 if tiles[0] < CTX_T else 0
        ntok = len(tiles) * 128
        t0 = tiles[0]
        load_hT_mod(C, X, "X", tiles, C.vecT[:, 1 * 2 + r, :], C.vecT[:, 0 * 2 + r, :], "vecT")
        for i in range(2):
            P.op("sp", lambda h, i=i, t0=t0, ntok=ntok: h.dma_start(out=tab[:, i, 0:ntok], in_=rope_tab[i, :, t0 * 128:t0 * 128 + ntok]),
                 writes=[C.b("mla_tab")], dma="ld")
        import os
        STOP = int(os.environ.get("MLA_STOP", "99"))
        if STOP < 1:
            continue
        linear_fm(C, Wa, ntok, mk_epi_a(ntok))
        if STOP < 2:
            continue
        ea, eb = mk_rope_epis(KR, "mla_KR", 0, 0, ntok, t0)
        linear_fm(C, Wkr, ntok, ea)
        linear_fm(C, Wkrs, ntok, eb)
        cqv, ckv = cn[0], cn[1]
        cqb, ckb = [C.b(("mla_cn", 0))] * 4, [C.b(("mla_cn", 1))] * 4
        if STOP < 3:
            continue
        linear_fm(C, Wqn, ntok, mk_copy_fm(QN, "mla_QN", ntok, t0), hv=cqv, hbufs=cqb[:1])
        if STOP < 4:
            continue
        for m in range(16):
            ea, eb = mk_rope_epis(QR, "mla_QR", m, m * 64, ntok, t0)
            linear_fm(C, Wqr, ntok, ea, mchunks=[m], hv=cqv, hbufs=cqb[:1])
            linear_fm(C, Wqrs, ntok, eb, mchunks=[m], hv=cqv, hbufs=cqb[:1])
        if STOP < 5:
            continue
        linear_fm(C, Wkn, ntok, mk_copy_fm(KN, "mla_KN", ntok, t0), hv=ckv, hbufs=ckb[:1])
        linear(C, Wv, tiles, epi_copy(C, V, "mla_V", BF16), hv=ckv, hbufs=ckb)
    if STOP < 6:
        return
    groups = []
    for hd in range(16):
        groups.append(dict(k_chunks=[(KN, hd * 128, 128, ("mla_KN", hd)), (KR, 0, 64, ("mla_KR", 0))], v=(V, "mla_V", hd * 128, 0),
                           heads=[dict(q_chunks=[(QN, hd * 128, 128, ("mla_QN", hd)), (QR, hd * 64, 64, ("mla_QR", hd))], out=hd)]))
    attn_core(C, groups, 192 ** -0.5, OT, "mla_OT")
    out_proj_fm(C, OT, "mla_OT", 16, Wo, Y)


def ret_mixer(C, l, X, Y, w_in, w_qk_sw, decay, rope_tab, w_out):
    P = C.P
    T = C.NT * 128
    NT, NL = C.NT, C.NL
    allt = list(range(NT))
    Win = cast_weight(C, "ret_in%d" % l, w_in, D, 12288)
    Wsw = cast_weight(C, "ret_sw%d" % l, w_qk_sw, D, 4096)
    Wo = cast_weight(C, "ret_out%d" % l, w_out, 4096, D)
    QK = C.dram("ret_QK", [4096, T], BF16)
    V = C.dram("ret_V", [T, 4096], BF16)
    G = C.dram("ret_G", [4096, T], F32)
    OT = C.dram("ret_OT", [4096, T], BF16)
    mark_proj = C.top
    tab = C.sb("ret_tab", [128, 4, 512], F32)
    t1 = C.sb("ret_t1", [128, 512], F32)
    t2 = C.sb("ret_t2", [128, 512], F32)
    qo = [C.sb("ret_qo%d" % i, [128, 512], BF16) for i in range(2)]

    def mk_epis(m, ntok, t0):
        part = m % 2

        def ea(mm, bank, bbuf):
            P.op("dve", lambda h: h.tensor_tensor(out=t1[:, 0:ntok], in0=bank[:, 0:ntok], in1=tab[:, part, 0:ntok], op=ALU.mult),
                 reads=[bbuf, C.b("ret_tab")], writes=[C.b("ret_t1")])

        def eb(mm, bank, bbuf):
            P.op("dve", lambda h: h.tensor_tensor(out=t2[:, 0:ntok], in0=bank[:, 0:ntok], in1=tab[:, 2 + part, 0:ntok], op=ALU.mult),
                 reads=[bbuf, C.b("ret_tab")], writes=[C.b("ret_t2")])
            qi = C.next("ret_qo", 2)
            P.op("dve", lambda h: h.tensor_tensor(out=qo[qi][:, 0:ntok], in0=t1[:, 0:ntok], in1=t2[:, 0:ntok], op=ALU.add),
                 reads=[C.b("ret_t1"), C.b("ret_t2")], writes=[C.b(("ret_qo", qi))])
            store_rows(C, QK, "ret_QK", m, m * 128, 128, t0, ntok, qo[qi][:, 0:ntok], C.b(("ret_qo", qi)))
        return ea, eb

    def mk_g(ntok, t0):
        def e(m, bank, bbuf):
            si = C.next("stgf", 4)
            sbuf = C.b(("stgf", si))
            P.op("act", lambda h: h.activation(out=C.stg_f[si][:, 0:ntok], in_=bank[:, 0:ntok], func=AF.Silu), reads=[bbuf], writes=[sbuf])
            store_rows(C, G, "ret_G", m - 64, (m - 64) * 128, 128, t0, ntok, C.stg_f[si][:, 0:ntok], sbuf)
        return e

    for tiles in C.blocks():
        r = 1 if tiles[0] < CTX_T else 0
        ntok = len(tiles) * 128
        t0 = tiles[0]
        load_hT_mod(C, X, "X", tiles, C.vecT[:, 1 * 2 + r, :], C.vecT[:, 0 * 2 + r, :], "vecT")
        for i in range(2):
            for part in range(2):
                P.op("sp", lambda h, i=i, part=part, t0=t0, ntok=ntok: h.dma_start(
                    out=tab[:, i * 2 + part, 0:ntok], in_=rope_tab[i, part * 128:(part + 1) * 128, t0 * 128:t0 * 128 + ntok]),
                    writes=[C.b("ret_tab")], dma="ld")
        for m in range(32):
            ea, eb = mk_epis(m, ntok, t0)
            linear_fm(C, Win, ntok, ea, mchunks=[m])
            linear_fm(C, Wsw, ntok, eb, mchunks=[m])
        ecv = epi_copy(C, V, "ret_V", BF16)
        linear(C, Win, tiles, lambda t, tl, j, bank, bbuf, ecv=ecv: ecv(t, tl, j - 8, bank, bbuf), panels=list(range(8, 16)))
        linear_fm(C, Win, ntok, mk_g(ntok, t0), mchunks=list(range(64, 96)))

    P.barrier()
    C.top = mark_proj
    lg = C.sb("ret_lg", [128, 16], F32)
    P.op("sp", lambda h: h.dma_start(out=lg[:], in_=decay.partition_broadcast(128)), writes=[C.b("ret_lg")], dma="ld")
    NSC = NT + 3
    iota = C.sb("ret_iota", [128, 64], F32)
    cm = C.sb("ret_cm", [128, 4, 128], F32)
    P.op("sp", lambda h: h.dma_start(out=iota[:], in_=C.cst2[:, 512:576]), writes=[C.b("ret_iota")], dma="ld")
    P.op("sp", lambda h: h.dma_start(out=cm[:], in_=C.cst2[:, 0:512].rearrange("p (a b) -> p a b", b=128)), writes=[C.b("ret_cm")], dma="ld")
    SC = C.sb("ret_SC", [128, 16, 64], F32)
    for j in range(16):
        P.op("dve", lambda h, j=j: h.tensor_scalar(out=SC[:, j, :], in0=iota[:], scalar1=lg[:, j:j + 1], scalar2=None, op0=ALU.mult),
             reads=[C.b("ret_iota"), C.b("ret_lg")], writes=[C.b("ret_SC")])
    P.op("act", lambda h: h.activation(out=SC[:], in_=SC[:], func=AF.Exp), reads=[C.b("ret_SC")], writes=[C.b("ret_SC")])
    P.op("dve", lambda h: h.tensor_scalar(out=SC[:], in0=SC[:], scalar1=1.0 / 16, scalar2=None, op0=ALU.mult),
         reads=[C.b("ret_SC")], writes=[C.b("ret_SC")])
    BF = C.sb("ret_BF", [128, 8, 128], F32)
    BB = C.sb("ret_BB", [128, 8, 128], F32)
    DG = C.sb("ret_DG", [128, 8, 128], F32)
    tm = C.sb("ret_tm", [128, 128], F32)
    for hh in range(8):
        P.op("act", lambda h, hh=hh: h.activation(out=BF[:, hh, :], in_=cm[:, 0, :], func=AF.Exp, scale=lg[:, hh:hh + 1]),
             reads=[C.b("ret_cm"), C.b("ret_lg")], writes=[C.b("ret_BF")])
        P.op("act", lambda h, hh=hh: h.activation(out=BB[:, hh, :], in_=cm[:, 1, :], func=AF.Exp, scale=lg[:, 8 + hh:9 + hh]),
             reads=[C.b("ret_cm"), C.b("ret_lg")], writes=[C.b("ret_BB")])
        P.op("dve", lambda h, hh=hh: h.tensor_tensor(out=tm[:], in0=BF[:, hh, :], in1=cm[:, 2, :], op=ALU.mult),
             reads=[C.b("ret_BF"), C.b("ret_cm")], writes=[C.b("ret_tm")])
        P.op("dve", lambda h, hh=hh: h.tensor_tensor(out=DG[:, hh, :], in0=BB[:, hh, :], in1=cm[:, 3, :], op=ALU.mult),
             reads=[C.b("ret_BB"), C.b("ret_cm")], writes=[C.b("ret_DG")])
        P.op("dve", lambda h, hh=hh: h.tensor_tensor(out=DG[:, hh, :], in0=DG[:, hh, :], in1=tm[:], op=ALU.add),
             reads=[C.b("ret_DG"), C.b("ret_tm")], writes=[C.b("ret_DG")])
        P.op("dve", lambda h, hh=hh: h.tensor_scalar(out=DG[:, hh, :], in0=DG[:, hh, :], scalar1=1.0 / 16, scalar2=None, op0=ALU.mult),
             reads=[C.b("ret_DG")], writes=[C.b("ret_DG")])

    kT = C.sb("ret_kT", [128, 2, T], BF16)
    vv = C.sb("ret_v", [128, NT, 512], BF16)
    qT = [C.sb("ret_qT%d" % i, [128, 2, 512], BF16) for i in range(2)]
    pT = [C.sb("ret_pT%d" % i, [128, 512], BF16) for i in range(3)]
    sq = C.sb("ret_sq", [128, 512], BF16)
    ssa = C.sb("ret_ssa", [128, 512], F32)
    gt = [C.sb("ret_gt%d" % i, [128, 512], F32) for i in range(2)]
    oo = [C.sb("ret_oo%d" % i, [128, 512], BF16) for i in range(2)]
    acc = [C.ob, C.db, C.tpf, C.tpbf]
    accb = [C.b("ob"), C.b("db"), C.b("tpf"), C.b("tpb")]
    for hd in range(8):
        for ci in range(2):
            P.op("sp", lambda h, ci=ci, hd=hd: h.dma_start(out=kT[:, ci, :], in_=QK[(16 + hd * 2 + ci) * 128:(17 + hd * 2 + ci) * 128, :]),
                 reads=[C.b(("ret_QK", 16 + hd * 2 + ci, t)) for t in allt], writes=[C.b("ret_kT")], dma="ld")
        P.op("sp", lambda h, hd=hd: h.dma_start(out=vv[:], in_=V[:, hd * 512:(hd + 1) * 512].rearrange("(t p) c -> p t c", p=128)),
             reads=[C.b(("ret_V", t, hd)) for t in allt], writes=[C.b("ret_v")], dma="ld")
        for tiles in C.blocks():
            nq = len(tiles) * 128
            t0 = tiles[0]
            keys = [0, 1] if t0 < CTX_T else allt
            qi = C.next("ret_qT", 2)
            for ci in range(2):
                P.op("sp", lambda h, qi=qi, ci=ci, hd=hd, t0=t0, nq=nq: h.dma_start(
                    out=qT[qi][:, ci, 0:nq], in_=QK[(hd * 2 + ci) * 128:(hd * 2 + ci + 1) * 128, t0 * 128:t0 * 128 + nq]),
                    reads=[C.b(("ret_QK", hd * 2 + ci, t)) for t in tiles], writes=[C.b(("ret_qT", qi))], dma="ld")
            for ki, kt in enumerate(keys):
                bi = C.next("gb", 4)
                for ci in range(2):
                    P.op("pe", lambda h, bi=bi, kt=kt, qi=qi, nq=nq, ci=ci: h.matmul(
                        C.gb[bi][:, 0:nq], lhsT=kT[:, ci, kt * 128:(kt + 1) * 128], rhs=qT[qi][:, ci, 0:nq], start=(ci == 0), stop=(ci == 1)),
                        reads=[C.b("ret_kT"), C.b(("ret_qT", qi))], writes=[C.b(("gb", bi))])
                pi = C.next("ret_pT", 3)
                pbuf = C.b(("ret_pT", pi))
                for ql, qt in enumerate(tiles):
                    cs_ = slice(ql * 128, (ql + 1) * 128)
                    srcp = C.gb[bi][:, cs_]
                    dstp = pT[pi][:, cs_]
                    if kt == qt:
                        P.op("dve", lambda h, srcp=srcp, dstp=dstp, hd=hd: h.tensor_tensor(out=dstp, in0=srcp, in1=DG[:, hd, :], op=ALU.mult),
                             reads=[C.b(("gb", bi)), C.b("ret_DG")], writes=[pbuf])
                    elif kt < CTX_T and qt >= CTX_T:
                        da = qt - kt
                        db_ = NL + kt - qt + 2
                        P.op("dve", lambda h, hd=hd, da=da: h.tensor_scalar(out=tm[:], in0=BF[:, hd, :], scalar1=SC[:, hd, da:da + 1], scalar2=None, op0=ALU.mult),
                             reads=[C.b("ret_BF"), C.b("ret_SC")], writes=[C.b("ret_tm")])
                        P.op("dve", lambda h, hd=hd, db_=db_: h.scalar_tensor_tensor(out=tm[:], in0=BB[:, hd, :], scalar=SC[:, 8 + hd, db_:db_ + 1], in1=tm[:],
                                                                                   op0=ALU.mult, op1=ALU.add),
                             reads=[C.b("ret_BB"), C.b("ret_SC"), C.b("ret_tm")], writes=[C.b("ret_tm")])
                        P.op("dve", lambda h, srcp=srcp, dstp=dstp: h.tensor_tensor(out=dstp, in0=srcp, in1=tm[:], op=ALU.mult),
                             reads=[C.b(("gb", bi)), C.b("ret_tm")], writes=[pbuf])
                    elif kt < qt:
                        dd = qt - kt
                        P.op("dve", lambda h, srcp=srcp, dstp=dstp, hd=hd, dd=dd: h.scalar_tensor_tensor(
                            out=dstp, in0=srcp, scalar=SC[:, hd, dd:dd + 1], in1=BF[:, hd, :], op0=ALU.mult, op1=ALU.mult),
                            reads=[C.b(("gb", bi)), C.b("ret_SC"), C.b("ret_BF")], writes=[pbuf])
                    else:
                        dd = kt - qt
                        P.op("dve", lambda h, srcp=srcp, dstp=dstp, hd=hd, dd=dd: h.scalar_tensor_tensor(
                            out=dstp, in0=srcp, scalar=SC[:, 8 + hd, dd:dd + 1], in1=BB[:, hd, :], op0=ALU.mult, op1=ALU.mult),
                            reads=[C.b(("gb", bi)), C.b("ret_SC"), C.b("ret_BB")], writes=[pbuf])
                first, lastk = (ki == 0), (ki == len(keys) - 1)
                for j in range(4):
                    P.op("pe", lambda h, kt=kt, pi=pi, nq=nq, first=first, lastk=lastk, j=j: h.matmul(
                        acc[j][:, 0:nq], lhsT=vv[:, kt, j * 128:(j + 1) * 128], rhs=pT[pi][:, 0:nq], start=first, stop=lastk),
                        reads=[C.b("ret_v"), pbuf], writes=[accb[j]])
            for j in range(4):
                P.op("act", lambda h, j=j, nq=nq: h.activation(out=sq[:, 0:nq], in_=acc[j][:, 0:nq], func=AF.Square), reads=[accb[j]], writes=[C.b("ret_sq")])
                bi = C.next("gb", 4)
                P.op("pe", lambda h, bi=bi, nq=nq: h.matmul(C.gb[bi][:, 0:nq], lhsT=C.onesb[:], rhs=sq[:, 0:nq], start=True, stop=True),
                     reads=[C.b("ret_sq"), C.b("onesb")], writes=[C.b(("gb", bi))])
                if j == 0:
                    P.op("dve", lambda h, bi=bi, nq=nq: h.tensor_copy(out=ssa[:, 0:nq], in_=C.gb[bi][:, 0:nq]), reads=[C.b(("gb", bi))], writes=[C.b("ret_ssa")])
                else:
                    P.op("dve", lambda h, bi=bi, nq=nq: h.tensor_tensor(out=ssa[:, 0:nq], in0=C.gb[bi][:, 0:nq], in1=ssa[:, 0:nq], op=ALU.add),
                         reads=[C.b(("gb", bi)), C.b("ret_ssa")], writes=[C.b("ret_ssa")])
            P.op("act", lambda h, nq=nq: h.activation(out=ssa[:, 0:nq], in_=ssa[:, 0:nq], func=AF.Sqrt, scale=1.0 / 512, bias=C.epsc[:, 0:1]),
                 reads=[C.b("ret_ssa"), C.b("epsc")], writes=[C.b("ret_ssa")])
            P.op("dve", lambda h, nq=nq: h.reciprocal(out=ssa[:, 0:nq], in_=ssa[:, 0:nq]), reads=[C.b("ret_ssa")], writes=[C.b("ret_ssa")])
            for j in range(4):
                gi = C.next("ret_gt", 2)
                row = (hd * 4 + j) * 128
                P.op("sp", lambda h, gi=gi, row=row, t0=t0, nq=nq: h.dma_start(out=gt[gi][:, 0:nq], in_=G[row:row + 128, t0 * 128:t0 * 128 + nq]),
                     reads=[C.b(("ret_G", hd * 4 + j, t)) for t in tiles], writes=[C.b(("ret_gt", gi))], dma="ld")
                P.op("dve", lambda h, gi=gi, nq=nq: h.tensor_tensor(out=gt[gi][:, 0:nq], in0=gt[gi][:, 0:nq], in1=ssa[:, 0:nq], op=ALU.mult),
                     reads=[C.b(("ret_gt", gi)), C.b("ret_ssa")], writes=[C.b(("ret_gt", gi))])
                oi = C.next("ret_oo", 2)
                P.op("dve", lambda h, gi=gi, oi=oi, j=j, nq=nq: h.tensor_tensor(out=oo[oi][:, 0:nq], in0=acc[j][:, 0:nq], in1=gt[gi][:, 0:nq], op=ALU.mult),
                     reads=[accb[j], C.b(("ret_gt", gi))], writes=[C.b(("ret_oo", oi))])
                store_rows(C, OT, "ret_OT", hd * 4 + j, row, 128, t0, nq, oo[oi][:, 0:nq], C.b(("ret_oo", oi)))
    out_proj_fm(C, OT, "ret_OT", 32, Wo, Y)


def hgrn_mixer(C, l, X, Y, w_in, lb_raw, out_gain, w_out):
    P = C.P
    T = C.NT * 128
    NT, NL = C.NT, C.NL
    Win = cast_weight(C, "hg_in%d" % l, w_in, D, 10240)
    Wo = cast_weight(C, "hg_out%d" % l, w_out, D, D)
    QS = C.dram("hg_QS", [D, T], F32)
    KF = [C.dram("hg_K%d" % d_, [D, T], F32) for d_ in range(2)]
    LF = [C.dram("hg_LF%d" % d_, [D, T], F32) for d_ in range(2)]
    GS = C.dram("hg_GS", [D, T], F32)
    IV = C.dram("hg_I", [T, D], BF16)
    OF = C.dram("hg_OF", [D, T], F32)
    OT = C.dram("hg_OT", [D, T], BF16)
    lr = C.sb("hg_lr", [128, 4, KC], F32)
    lbv = C.sb("hg_lb", [128, 4, KC], F32)
    for j in range(4):
        P.op("sp", lambda h, j=j: h.dma_start(out=lr[:, j, :], in_=lb_raw[j].rearrange("(k p) -> p k", p=128), allow_slow_non_contiguous=True),
             writes=[C.b("hg_lr")], dma="ld")
    P.op("act", lambda h: h.activation(out=lr[:], in_=lr[:], func=AF.Exp), reads=[C.b("hg_lr")], writes=[C.b("hg_lr")])
    P.op("dve", lambda h: h.tensor_tensor(out=lbv[:, 2, :], in0=lr[:, 0, :], in1=lr[:, 1, :], op=ALU.add), reads=[C.b("hg_lr")], writes=[C.b("hg_lb")])
    P.op("dve", lambda h: h.tensor_tensor(out=lbv[:, 3, :], in0=lr[:, 2, :], in1=lr[:, 3, :], op=ALU.add), reads=[C.b("hg_lr")], writes=[C.b("hg_lb")])
    P.op("dve", lambda h: h.tensor_tensor(out=lbv[:, 2, :], in0=lbv[:, 2, :], in1=lbv[:, 3, :], op=ALU.add), reads=[C.b("hg_lb")], writes=[C.b("hg_lb")])
    P.op("dve", lambda h: h.reciprocal(out=lbv[:, 2, :], in_=lbv[:, 2, :]), reads=[C.b("hg_lb")], writes=[C.b("hg_lb")])
    P.op("dve", lambda h: h.memset(lbv[:, 0, :], 0.0), reads=[C.b("hg_lb")], writes=[C.b("hg_lb")])
    for j in range(1, l + 1):
        P.op("dve", lambda h, j=j: h.tensor_tensor(out=lbv[:, 0, :], in0=lbv[:, 0, :], in1=lr[:, j, :], op=ALU.add),
             reads=[C.b("hg_lb"), C.b("hg_lr")], writes=[C.b("hg_lb")])
    P.op("dve", lambda h: h.tensor_tensor(out=lbv[:, 0, :], in0=lbv[:, 0, :], in1=lbv[:, 2, :], op=ALU.mult), reads=[C.b("hg_lb")], writes=[C.b("hg_lb")])
    P.op("dve", lambda h: h.tensor_scalar(out=lbv[:, 1, :], in0=lbv[:, 0, :], scalar1=-1.0, scalar2=1.0, op0=ALU.mult, op1=ALU.add),
         reads=[C.b("hg_lb")], writes=[C.b("hg_lb")])

    def mk_act_store(dst, dname, func, m0, ntok, t0):
        def e(m, bank, bbuf):
            si = C.next("stgf", 4)
            sbuf = C.b(("stgf", si))
            P.op("act", lambda h: h.activation(out=C.stg_f[si][:, 0:ntok], in_=bank[:, 0:ntok], func=func), reads=[bbuf], writes=[sbuf])
            store_rows(C, dst, dname, m - m0, (m - m0) * 128, 128, t0, ntok, C.stg_f[si][:, 0:ntok], sbuf)
        return e

    mark_proj = C.top
    fg = C.sb("hg_fg", [128, 512], F32)
    kk = [C.sb("hg_kk%d" % i, [128, 512], F32) for i in range(2)]
    lff = [C.sb("hg_lff%d" % i, [128, 512], F32) for i in range(2)]

    def mk_forget(d_, m0, ntok, t0):
        def e(m, bank, bbuf):
            mm = m - m0
            P.op("act", lambda h: h.activation(out=fg[:, 0:ntok], in_=bank[:, 0:ntok], func=AF.Sigmoid), reads=[bbuf], writes=[C.b("hg_fg")])
            P.op("dve", lambda h: h.tensor_scalar(out=fg[:, 0:ntok], in0=fg[:, 0:ntok], scalar1=lbv[:, 1, mm:mm + 1], scalar2=lbv[:, 0, mm:mm + 1],
                                                  op0=ALU.mult, op1=ALU.add),
                 reads=[C.b("hg_fg"), C.b("hg_lb")], writes=[C.b("hg_fg")])
            i1 = C.next("hg_kk", 2)
            P.op("dve", lambda h: h.tensor_scalar(out=kk[i1][:, 0:ntok], in0=fg[:, 0:ntok], scalar1=-1.0, scalar2=1.0, op0=ALU.mult, op1=ALU.add),
                 reads=[C.b("hg_fg")], writes=[C.b(("hg_kk", i1))])
            store_rows(C, KF[d_], "hg_K%d" % d_, mm, mm * 128, 128, t0, ntok, kk[i1][:, 0:ntok], C.b(("hg_kk", i1)))
            i2 = C.next("hg_lff", 2)
            P.op("act", lambda h: h.activation(out=lff[i2][:, 0:ntok], in_=fg[:, 0:ntok], func=AF.Ln), reads=[C.b("hg_fg")], writes=[C.b(("hg_lff", i2))])
            store_rows(C, LF[d_], "hg_LF%d" % d_, mm, mm * 128, 128, t0, ntok, lff[i2][:, 0:ntok], C.b(("hg_lff", i2)))
        return e

    for tiles in C.blocks():
        r = 1 if tiles[0] < CTX_T else 0
        ntok = len(tiles) * 128
        t0 = tiles[0]
        load_hT_mod(C, X, "X", tiles, C.vecT[:, 1 * 2 + r, :], C.vecT[:, 0 * 2 + r, :], "vecT")
        linear_fm(C, Win, ntok, mk_act_store(QS, "hg_QS", AF.Silu, 0, ntok, t0), mchunks=list(range(0, 16)))
        linear_fm(C, Win, ntok, mk_forget(0, 16, ntok, t0), mchunks=list(range(16, 32)))
        linear_fm(C, Win, ntok, mk_forget(1, 32, ntok, t0), mchunks=list(range(32, 48)))
        eci = epi_copy(C, IV, "hg_I", BF16)
        linear(C, Win, tiles, lambda t, tl, j, bank, bbuf, eci=eci: eci(t, tl, j - 12, bank, bbuf), panels=list(range(12, 16)))
        linear_fm(C, Win, ntok, mk_act_store(GS, "hg_GS", AF.Silu, 64, ntok, t0), mchunks=list(range(64, 80)))

    P.barrier()
    C.top = mark_proj
    W_ = 16 * 128
    rm = C.sb("hg_rm", [128, W_], F32)
    msk = C.sb("hg_msk", [128, 2, 128], F32)
    gv = C.sb("hg_gv", [128, 1], F32)
    P.op("sp", lambda h: h.dma_start(out=rm[:], in_=C.cst3[:, 0:W_]), writes=[C.b("hg_rm")], dma="ld")
    P.op("sp", lambda h: h.dma_start(out=msk[:], in_=C.cst3[:, W_:W_ + 256].rearrange("p (a b) -> p a b", b=128)), writes=[C.b("hg_msk")], dma="ld")
    P.op("sp", lambda h: h.dma_start(out=gv[:], in_=out_gain.rearrange("a p -> p a"), allow_slow_non_contiguous=True), writes=[C.b("hg_gv")], dma="ld")
    qt_ = C.sb("hg_q", [128, W_], F32)
    kt_ = C.sb("hg_k", [128, W_], F32)
    lt_ = C.sb("hg_l", [128, W_], F32)
    cf = C.sb("hg_cf", [128, W_], F32)
    tS = C.xinb[0][:, 0:2 * W_].bitcast(F32)
    ex = C.sb("hg_ex", [128, W_], F32)
    ebl = C.sb("hg_ebl", [128, 32], F32)
    qin = C.sb("hg_qin", [128, W_], BF16)
    kout = C.sb("hg_kout", [128, W_], BF16)
    kd = C.sb("hg_kd", [128, W_], BF16)
    kdT = [C.sb("hg_kdT%d" % i, [128, 128], BF16) for i in range(2)]
    pm = [C.sb("hg_pm%d" % i, [128, 128], BF16) for i in range(2)]
    iv = C.sb("hg_iv", [128, D], BF16)
    St = C.sb("hg_S", [128, 16, 128], F32)
    Sb = C.sb("hg_Sb", [128, 16, 128], BF16)
    oall = C.sa[:].rearrange("p a b -> p (a b)")
    ofl, osq, gsl, ogb = lt_, kout, qt_, qin
    _alias = {"hg_ofl": "hg_l", "hg_osq": "hg_kout", "hg_gsl": "hg_q", "hg_ogb": "hg_qin", "hg_b": "hg_cf"}
    B = lambda k_: C.b(_alias.get(k_, k_) if isinstance(k_, str) else k_)
    c3 = lambda t_: t_[:].rearrange("p (a b) -> p a b", b=64)

    def fm_tile_ap(dr, t):
        return dr[:, t * 128:(t + 1) * 128].rearrange("(h p) t -> p h t", p=128)

    for d_ in range(2):
        P.op("dve", lambda h: h.memset(St[:], 0.0), reads=[B("hg_S")], writes=[B("hg_S")])
        P.op("dve", lambda h: h.memset(Sb[:], 0.0), reads=[B("hg_Sb")], writes=[B("hg_Sb")])
        order = list(range(NT)) if d_ == 0 else [1, 0] + list(range(NT - 1, CTX_T - 1, -1))
        for t in order:
            allm = list(range(16))
            P.op("sp", lambda h, t=t: h.dma_start(out=qt_[:].rearrange("p (h t) -> p h t", t=128), in_=fm_tile_ap(QS, t)),
                 reads=[B(("hg_QS", m, t)) for m in allm], writes=[B("hg_q")], dma="ld")
            P.op("sp", lambda h, t=t, d_=d_: h.dma_start(out=kt_[:].rearrange("p (h t) -> p h t", t=128), in_=fm_tile_ap(KF[d_], t)),
                 reads=[B(("hg_K%d" % d_, m, t)) for m in allm], writes=[B("hg_k")], dma="ld")
            P.op("sp", lambda h, t=t, d_=d_: h.dma_start(out=lt_[:].rearrange("p (h t) -> p h t", t=128), in_=fm_tile_ap(LF[d_], t)),
                 reads=[B(("hg_LF%d" % d_, m, t)) for m in allm], writes=[B("hg_l")], dma="ld")
            P.op("sp", lambda h, t=t: h.dma_start(out=iv[:], in_=IV[t * 128:(t + 1) * 128, :]),
                 reads=dbufs(C, "hg_I", t, 0, D), writes=[B("hg_iv")], dma="ld")
            P.op("dve", lambda h: h.tensor_tensor_scan(out=cf[:], data0=rm[:], data1=lt_[:], initial=0.0, op0=ALU.mult, op1=ALU.add),
                 reads=[B("hg_rm"), B("hg_l")], writes=[B("hg_cf")])
            P.op("dve", lambda h: h.tensor_copy(out=c3(tS), in_=c3(cf)[:, :, 63:64].to_broadcast([128, 32, 64])),
                 reads=[B("hg_cf")], writes=[B("hg_tS")])
            if d_ == 0:
                bsrc, bbuf_ = cf, B("hg_cf")
            else:
                P.op("dve", lambda h: h.tensor_tensor(out=cf[:], in0=lt_[:], in1=cf[:], op=ALU.subtract), reads=[B("hg_l"), B("hg_cf"), B("hg_tS")], writes=[B("hg_cf")])
                P.op("dve", lambda h: h.tensor_tensor(out=cf[:], in0=cf[:], in1=tS[:], op=ALU.add), reads=[B("hg_cf"), B("hg_tS")], writes=[B("hg_cf")])
                bsrc, bbuf_ = cf, B("hg_cf")
            P.op("act", lambda h, bsrc=bsrc: h.activation(out=ex[:], in_=bsrc[:], func=AF.Exp), reads=[bbuf_], writes=[B("hg_ex")])
            P.op("dve", lambda h: h.tensor_tensor(out=qin[:], in0=qt_[:], in1=ex[:], op=ALU.mult), reads=[B("hg_q"), B("hg_ex")], writes=[B("hg_qin")])
            P.op("act", lambda h, bsrc=bsrc: h.activation(out=ex[:], in_=bsrc[:], func=AF.Exp, scale=-1.0), reads=[bbuf_, B("hg_qin")], writes=[B("hg_ex")])
            P.op("dve", lambda h: h.tensor_tensor(out=kout[:], in0=kt_[:], in1=ex[:], op=ALU.mult), reads=[B("hg_k"), B("hg_ex")], writes=[B("hg_kout")])
            P.op("dve", lambda h, bsrc=bsrc: h.tensor_tensor(out=ex[:], in0=tS[:], in1=bsrc[:], op=ALU.subtract), reads=[B("hg_tS"), bbuf_, B("hg_kout")], writes=[B("hg_ex")])
            P.op("act", lambda h: h.activation(out=ex[:], in_=ex[:], func=AF.Exp), reads=[B("hg_ex")], writes=[B("hg_ex")])
            P.op("dve", lambda h: h.tensor_tensor(out=kd[:], in0=kt_[:], in1=ex[:], op=ALU.mult), reads=[B("hg_k"), B("hg_ex")], writes=[B("hg_kd")])
            P.op("act", lambda h: h.activation(out=ebl[:], in_=c3(tS)[:, :, 0], func=AF.Exp), reads=[B("hg_tS")], writes=[B("hg_ebl")])
            chunks = [0, 1] if d_ == 0 else [1, 0]
            for hd in range(16):
                hs = slice(hd * 128, (hd + 1) * 128)
                b1 = C.next("gb", 4)
                P.op("pe", lambda h, b1=b1, hs=hs: h.matmul(C.gb[b1][:, 0:128], lhsT=kout[:, hs], rhs=qin[:, hs], start=True, stop=True),
                     reads=[B("hg_kout"), B("hg_qin")], writes=[B(("gb", b1))])
                pi = C.next("hg_pm", 2)
                P.op("dve", lambda h, b1=b1, pi=pi, d_=d_: h.tensor_tensor(out=pm[pi][:], in0=C.gb[b1][:, 0:128], in1=msk[:, d_, :], op=ALU.mult),
                     reads=[B(("gb", b1)), B("hg_msk")], writes=[B(("hg_pm", pi))])
                P.op("pe", lambda h, hs=hs: h.transpose(C.tpb[:, 0:128], kd[:, hs], C.identb[:]), reads=[B("hg_kd"), B("identb")], writes=[B("tpb")])
                ki = C.next("hg_kdT", 2)
                P.op("act", lambda h, ki=ki: h.activation(out=kdT[ki][:], in_=C.tpb[:, 0:128], func=AF.Copy), reads=[B("tpb")], writes=[B(("hg_kdT", ki))])
                b2 = C.next("gb", 4)
                for ch in chunks:
                    cs_ = slice(ch * 64, ch * 64 + 64)
                    hcs = slice(hd * 128 + ch * 64, hd * 128 + ch * 64 + 64)
                    P.op("pe", lambda h, b2=b2, cs_=cs_, hcs=hcs, hd=hd: h.matmul(C.gb[b2][:, cs_], lhsT=Sb[:, hd, :], rhs=qin[:, hcs], start=True, stop=False),
                         reads=[B(("hg_Sb", hd)), B("hg_qin")], writes=[B(("gb", b2))])
                    P.op("pe", lambda h, b2=b2, cs_=cs_, pi=pi, hs=hs: h.matmul(C.gb[b2][:, cs_], lhsT=iv[:, hs], rhs=pm[pi][:, cs_], start=False, stop=True),
                         reads=[B("hg_iv"), B(("hg_pm", pi))], writes=[B(("gb", b2))])
                    b3 = C.next("gb", 4)
                    P.op("pe", lambda h, b3=b3, cs_=cs_, ki=ki, hs=hs: h.matmul(C.gb[b3][:, 0:128], lhsT=kdT[ki][cs_, :], rhs=iv[cs_, hs], start=True, stop=True),
                         reads=[B(("hg_kdT", ki)), B("hg_iv")], writes=[B(("gb", b3))])
                    ci = hd * 2 + ch
                    P.op("dve", lambda h, b3=b3, hd=hd, ci=ci: h.scalar_tensor_tensor(out=St[:, hd, :], in0=St[:, hd, :], scalar=ebl[:, ci:ci + 1], in1=C.gb[b3][:, 0:128],
                                                                                 op0=ALU.mult, op1=ALU.add),
                         reads=[B(("hg_S", hd)), B("hg_ebl"), B(("gb", b3))], writes=[B(("hg_S", hd))])
                    P.op("act", lambda h, hd=hd: h.activation(out=Sb[:, hd, :], in_=St[:, hd, :], func=AF.Copy),
                         reads=[B(("hg_S", hd))], writes=[B(("hg_Sb", hd))])
                evac(C, oall[:, hs], C.gb[b2][:, 0:128], [B(("gb", b2))], [B("hg_oall")])
            if d_ == 0:
                P.op("sp", lambda h, t=t: h.dma_start(out=fm_tile_ap(OF, t), in_=oall[:].rearrange("p (h t) -> p h t", t=128)),
                     reads=[B("hg_oall")], writes=[B(("hg_OF", t))], dma="st")
            else:
                P.op("sp", lambda h, t=t: h.dma_start(out=ofl[:].rearrange("p (h t) -> p h t", t=128), in_=fm_tile_ap(OF, t)),
                     reads=[B(("hg_OF", t))], writes=[B("hg_ofl")], dma="ld")
                P.op("sp", lambda h, t=t: h.dma_start(out=gsl[:].rearrange("p (h t) -> p h t", t=128), in_=fm_tile_ap(GS, t)),
                     reads=[B(("hg_GS", m, t)) for m in allm], writes=[B("hg_gsl")], dma="ld")
                P.op("dve", lambda h: h.tensor_tensor(out=oall[:], in0=oall[:], in1=ofl[:], op=ALU.add), reads=[B("hg_oall"), B("hg_ofl")], writes=[B("hg_oall")])
                P.op("act", lambda h: h.activation(out=osq[:], in_=oall[:], func=AF.Square), reads=[B("hg_oall")], writes=[B("hg_osq")])
                for q4 in range(4):
                    b4 = C.next("gb", 4)
                    P.op("pe", lambda h, b4=b4, q4=q4: h.matmul(C.gb[b4][:, :], lhsT=C.onesb[:], rhs=osq[:, q4 * 512:(q4 + 1) * 512], start=True, stop=True),
                         reads=[B("hg_osq"), B("onesb")], writes=[B(("gb", b4))])
                    P.op("act", lambda h, b4=b4, q4=q4: h.activation(out=ofl[:, q4 * 512:(q4 + 1) * 512], in_=C.gb[b4][:, :], func=AF.Sqrt, scale=1.0 / 128, bias=C.epsc[:, 0:1]),
                         reads=[B(("gb", b4)), B("epsc"), B("hg_ofl")], writes=[B("hg_ofl")])
                P.op("dve", lambda h: h.reciprocal(out=ofl[:], in_=ofl[:]), reads=[B("hg_ofl")], writes=[B("hg_ofl")])
                P.op("dve", lambda h: h.scalar_tensor_tensor(out=oall[:], in0=oall[:], scalar=gv[:, 0:1], in1=ofl[:], op0=ALU.mult, op1=ALU.mult),
                     reads=[B("hg_oall"), B("hg_gv"), B("hg_ofl")], writes=[B("hg_oall")])
                P.op("dve", lambda h: h.tensor_tensor(out=ogb[:], in0=oall[:], in1=gsl[:], op=ALU.mult), reads=[B("hg_oall"), B("hg_gsl")], writes=[B("hg_ogb")])
                P.op("sp", lambda h, t=t: h.dma_start(out=fm_tile_ap(OT, t), in_=ogb[:].rearrange("p (h t) -> p h t", t=128)),
                     reads=[B("hg_ogb")], writes=[B(("hg_OT", m, t)) for m in allm], dma="st")
    out_proj_fm(C, OT, "hg_OT", 16, Wo, Y)


def build(n_lat_tiles, layers, dbg=None):
    nc = bass.Bass("TRN2", target_bir_lowering=False)
    NL = n_lat_tiles
    T = (CTX_T + NL) * 128
    C = Ctx(nc, NL)
    P = C.P
    ein = lambda name, shape: nc.dram_tensor(name, shape, F32, kind="ExternalInput").ap()
    x_in = ein("x", [NL * 128, D])
    ctx_in = ein("ctx", [256, D])
    cvec = ein("cvec", [2, D])
    ada_w = ein("ada_w", [DEPTH, D, 6 * D])
    ada_b = ein("ada_b", [DEPTH, 6 * D])
    ln_g = ein("ln_g", [DEPTH, 2, D])
    ln_b = ein("ln_b", [DEPTH, 2, D])
    ffn_w_in = ein("ffn_w_in", [DEPTH, D, 2 * FFN_H])
    ffn_w_out = ein("ffn_w_out", [DEPTH, FFN_H, D])
    out = nc.dram_tensor("out", [NL * 128, D], F32, kind="ExternalOutput").ap()
    mixset = set(l % 4 for l in layers)
    C.cst2 = nc.dram_tensor("cst2", [128, 576], F32, kind="ExternalInput").ap()
    C.cst3 = nc.dram_tensor("cst3", [128, 2048 + 256], F32, kind="ExternalInput").ap()
    ein0 = ein
    ein_h = ein if 3 in mixset else (lambda name, shape: None)
    hg_in = {k: ein_h("hg_" + k, shp) for k, shp in (("w_in", [D, 10240]), ("lb_raw", [4, D]), ("gain", [1, 128]), ("w_out", [D, D]))}
    ein_r = ein if 0 in mixset else (lambda name, shape: None)
    ret_in = {k: ein_r("ret_" + k, shp) for k, shp in (("w_in", [D, 12288]), ("w_sw", [D, 4096]), ("decay", [1, 16]), ("rope", [2, 256, T]), ("w_out", [4096, D]))}
    if 1 not in mixset:
        ein = lambda name, shape: None
    gqa_w_in = ein("gqa_w_in", [D, 3072])
    gqa_w_sw = ein("gqa_w_sw", [D, 2560])
    gqa_gain = ein("gqa_gain", [4, 128])
    gqa_rope = ein("gqa_rope", [2, 128, T])
    gqa_w_out = ein("gqa_w_out", [D, D])
    ein = (lambda name, shape: nc.dram_tensor(name, shape, F32, kind="ExternalInput").ap()) if 2 in mixset else (lambda name, shape: None)
    mla_in = {k: ein("mla_" + k, shp) for k, shp in (("w_a", [D, 1024]), ("w_kr", [D, 64]), ("w_kr_sw", [D, 64]), ("w_qn", [512, 2048]),
                                                    ("w_qr", [512, 1024]), ("w_qr_sw", [512, 1024]), ("w_kn", [512, 2048]), ("w_v", [512, 2048]),
                                                    ("norms", [2, 512]), ("rope", [2, 64, T]), ("w_out", [D, D]))}

    C.cst = nc.dram_tensor("cst", [128, 258], F32, kind="ExternalInput").ap()
    setup_common(C)

    X = C.dram("X", [T, D], F32)
    Y = C.dram("Y", [T, D], F32)
    Y2 = C.dram("Y2", [T, D], F32)
    U = C.dram("U", [T, FFN_H], BF16)
    for t in range(C.NT):
        src = ctx_in[t * 128:(t + 1) * 128, :] if t < CTX_T else x_in[(t - CTX_T) * 128:(t - CTX_T + 1) * 128, :]
        P.op("sp", lambda h, t=t, src=src: h.dma_start(out=X[t * 128:(t + 1) * 128, :], in_=src),
             writes=dbufs(C, "X", t, 0, D), dma="st")

    adaln_all(C, cvec, ada_w, ada_b, layers)
    if dbg and dbg.get("stop_after_adaln"):
        o = nc.dram_tensor("dbg_MOD", [DEPTH, 2, 6 * D], F32, kind="ExternalOutput").ap()
        P.op("sp", lambda h: h.dma_start(out=o, in_=C.MOD), reads=[C.b(("MOD", l)) for l in layers], dma="st")
        o2 = nc.dram_tensor("dbg_csT", [128, KC * 2], BF16, kind="ExternalOutput").ap()
        P.op("sp", lambda h: h.dma_start(out=o2, in_=C.csT[:].rearrange("p k c -> p (k c)")), reads=[C.b("csT")], dma="st")
        P.emit()
        C.es.close()
        return nc, C

    for li, l in enumerate(layers):
        last = (li == len(layers) - 1)
        Wi = cast_weight(C, "ffi%d" % l, ffn_w_in[l], D, 2 * FFN_H)
        Wo = cast_weight(C, "ffo%d" % l, ffn_w_out[l], FFN_H, D)
        load_layer_vectors(C, l, ln_g, ln_b)
        mix = (l % 4) if (dbg is None or dbg.get("mixers", True)) else None
        with C.scope():
            if mix == 0:
                R_ = ret_in
                ret_mixer(C, l, X, Y, R_["w_in"], R_["w_sw"], R_["decay"], R_["rope"], R_["w_out"])
            if mix == 3:
                H_ = hg_in
                hgrn_mixer(C, l, X, Y, H_["w_in"], H_["lb_raw"], H_["gain"], H_["w_out"])
            if mix == 1:
                gqa_mixer(C, l, X, Y, gqa_w_in, gqa_w_sw, gqa_gain, gqa_rope, gqa_w_out)
            if mix == 2:
                M = mla_in
                mla_mixer(C, l, X, Y, M["w_a"], M["w_kr"], M["w_kr_sw"], M["w_qn"], M["w_qr"], M["w_qr_sw"], M["w_kn"], M["w_v"], M["norms"], M["rope"], M["w_out"])
        if mix in (0, 1, 2, 3):
            with C.scope():
                load_bcast(C, l, 0, ln_g, ln_b)
                resid_ln(C, X, "X", Y, "Y", X, "X")
        ffn(C, l, X, "X", Wi, Wo, U, Y2)
        with C.scope():
            load_bcast(C, l, 1, ln_g, ln_b)
            if last:
                resid_ln(C, X, "X", Y2, "Y2", out, "out", out_row0=0)
            else:
                resid_ln(C, X, "X", Y2, "Y2", X, "X")
    if dbg and dbg.get("dump"):
        for nm, (ap, shape, dt) in dict(MOD=(C.MOD, [DEPTH, 2, 6 * D], F32), U=(U, [T, FFN_H], BF16), Y2=(Y2, [T, D], F32),
                                        Y=(Y, [T, D], F32), X=(X, [T, D], F32)).items():
            if nm in dbg["dump"]:
                o = nc.dram_tensor("dbg_" + nm, shape, dt, kind="ExternalOutput").ap()
                allb = [b_ for k_, b_ in P.bufs.items() if isinstance(k_, tuple) and k_[0] == nm]
                P.op("sp", lambda h, o=o, ap=ap: h.dma_start(out=o, in_=ap), reads=allb, dma="st")
    P.emit()
    C.es.close()
    return nc, C


def make_consts2():
    c = np.zeros((128, 576), np.float32)
    s_ = np.arange(128)[:, None].astype(np.float32)
    c_ = np.arange(128)[None, :].astype(np.float32)
    c[:, 0:128] = c_ - s_
    c[:, 128:256] = s_ - c_
    c[:, 256:384] = (c_ >= s_)
    c[:, 384:512] = (c_ <= s_)
    c[:, 512:576] = 128.0 * np.arange(64)[None, :]
    return c


def ret_host_inputs(ret_w_in, ret_decay_fwd, ret_decay_bwd, ret_w_out, nl_tiles):
    w = ret_w_in[0]
    return {"ret_w_in": np.ascontiguousarray(w), "ret_w_sw": swap_halves_cols(w[:, :4096], 128, 64),
            "ret_decay": np.concatenate([ret_decay_fwd[0], ret_decay_bwd[0]])[None].astype(np.float32),
            "ret_rope": rope_tables(128, nl_tiles), "ret_w_out": np.ascontiguousarray(ret_w_out[0])}


def make_consts3():
    c = np.zeros((128, 2048 + 256), np.float32)
    col = np.arange(2048)
    c[:, 0:2048] = (col % 64 != 0).astype(np.float32)[None, :]
    s_ = np.arange(128)[:, None]
    c_ = np.arange(128)[None, :]
    same = (s_ // 64) == (c_ // 64)
    c[:, 2048:2176] = (same & (c_ >= s_))
    c[:, 2176:2304] = (same & (c_ <= s_))
    return c


def hgrn_host_inputs(hgrn_w_in, hgrn_lb_raw, hgrn_out_norm, hgrn_w_out):
    return {"hg_w_in": np.ascontiguousarray(hgrn_w_in[0]), "hg_lb_raw": np.ascontiguousarray(hgrn_lb_raw),
            "hg_gain": np.ascontiguousarray(hgrn_out_norm[0:1]), "hg_w_out": np.ascontiguousarray(hgrn_w_out[0])}


def make_consts():
    c = np.zeros((128, 258), np.float32)
    c[:, 0:128] = np.eye(128, dtype=np.float32)
    c[:, 128:256] = 1.0
    c[:, 256] = EPS
    return c


def rope_tables(half, nl_tiles, d_chunk=128):
    q = half // 2
    T = (CTX_T + nl_tiles) * 128
    pos = np.arange(nl_tiles * 128)
    row, col = pos // GRID_W, pos % GRID_W
    freqs = THETA ** (-np.arange(0, half, 2, dtype=np.float32) / half)
    cos = np.ones((4 * q, T), np.float32)
    sin = np.zeros((4 * q, T), np.float32)
    a_row = (row[None, :].astype(np.float32) * freqs[:, None]).astype(np.float32)
    a_col = (col[None, :].astype(np.float32) * freqs[:, None]).astype(np.float32)
    L0 = CTX_T * 128
    for blk, ang in ((0, a_row), (1, a_col)):
        c, s_ = np.cos(ang), np.sin(ang)
        cos[blk * 2 * q:blk * 2 * q + q, L0:] = c
        cos[blk * 2 * q + q:blk * 2 * q + 2 * q, L0:] = c
        sin[blk * 2 * q:blk * 2 * q + q, L0:] = -s_
        sin[blk * 2 * q + q:blk * 2 * q + 2 * q, L0:] = s_
    return np.stack([cos, sin]).astype(np.float32)


def swap_halves_cols(w, d, q):
    n = w.shape[-1]
    idx = np.arange(n)
    within = idx % d
    partner = np.where((within // q) % 2 == 0, idx + q, idx - q)
    return np.ascontiguousarray(w[..., partner])


N_LAT_TILES = 32
_NC_CACHE = {}


def kernel(x, c, ctx, c_ctx, ada_w, ada_b, ln_g, ln_b, ffn_w_in, ffn_w_out,
           ret_w_in, ret_decay_fwd, ret_decay_bwd, ret_w_out,
           gqa_w_in, gqa_q_norm, gqa_k_norm, gqa_w_out,
           mla_w_in, mla_q_norm, mla_w_q_up, mla_kv_norm, mla_w_kv_up, mla_w_out,
           hgrn_w_in, hgrn_lb_raw, hgrn_out_norm, hgrn_w_out):
    f = lambda a: np.ascontiguousarray(np.asarray(a, dtype=np.float32))
    x, c, ctx, c_ctx = f(x), f(c), f(ctx), f(c_ctx)
    if "nc" not in _NC_CACHE:
        _NC_CACHE["nc"] = build(N_LAT_TILES, [0, 1, 2, 3])[0]
    nc = _NC_CACHE["nc"]
    gw = f(gqa_w_in)[0]
    qg, kg = f(gqa_q_norm)[0], f(gqa_k_norm)[0]
    sw = lambda v: swap_halves_cols(v, 128, 32)
    shared = {"cst": make_consts(), "cst2": make_consts2(), "cst3": make_consts3(),
              "ada_w": f(ada_w), "ada_b": f(ada_b), "ln_g": f(ln_g), "ln_b": f(ln_b),
              "ffn_w_in": f(ffn_w_in), "ffn_w_out": f(ffn_w_out),
              "gqa_w_in": gw, "gqa_w_sw": sw(gw[:, :2560]), "gqa_gain": np.stack([qg, sw(qg), kg, sw(kg)]),
              "gqa_rope": rope_tables(64, N_LAT_TILES), "gqa_w_out": f(gqa_w_out)[0]}
    shared.update(ret_host_inputs(f(ret_w_in), f(ret_decay_fwd), f(ret_decay_bwd), f(ret_w_out), N_LAT_TILES))
    shared.update(mla_host_inputs(f(mla_w_in), f(mla_q_norm), f(mla_w_q_up), f(mla_kv_norm), f(mla_w_kv_up), f(mla_w_out), N_LAT_TILES))
    shared.update(hgrn_host_inputs(f(hgrn_w_in), f(hgrn_lb_raw), f(hgrn_out_norm), f(hgrn_w_out)))
    in_maps = []
    for core in range(8):
        b = core % 4
        m = dict(shared)
        m["x"] = x[b]
        m["ctx"] = ctx[b]
        m["cvec"] = np.stack([c[b], c_ctx])
        in_maps.append(m)
    res = run_bass_kernel_spmd(nc, in_maps, core_ids=list(range(8)))
    return np.stack([res.results[b]["out"] for b in range(4)]).astype(np.float32)


def mla_host_inputs(mla_w_in, mla_q_norm, mla_w_q_up, mla_kv_norm, mla_w_kv_up, mla_w_out, nl_tiles):
    w_in, wq, wkv = mla_w_in[0], mla_w_q_up[0], mla_w_kv_up[0]
    qh = wq.reshape(512, 16, 192)
    kvh = wkv.reshape(512, 16, 256)
    w_kr = np.ascontiguousarray(w_in[:, 1024:1088])
    w_qr = np.ascontiguousarray(qh[:, :, 128:].reshape(512, 1024))
    return {"mla_w_a": np.ascontiguousarray(w_in[:, :1024]), "mla_w_kr": w_kr, "mla_w_kr_sw": swap_halves_cols(w_kr, 64, 16),
            "mla_w_qn": np.ascontiguousarray(qh[:, :, :128].reshape(512, 2048)), "mla_w_qr": w_qr, "mla_w_qr_sw": swap_halves_cols(w_qr, 64, 16),
            "mla_w_kn": np.ascontiguousarray(kvh[:, :, :128].reshape(512, 2048)), "mla_w_v": np.ascontiguousarray(kvh[:, :, 128:].reshape(512, 2048)),
            "mla_norms": np.stack([mla_q_norm[0], mla_kv_norm[0]]), "mla_rope": rope_tables(32, nl_tiles), "mla_w_out": np.ascontiguousarray(mla_w_out[0])}
```

```python
import numpy as np
from contextlib import ExitStack
import concourse.bass as bass
import concourse.mybir as mybir
from concourse.alu_op_type import AluOpType as ALU
from concourse.bass_utils import run_bass_kernel_spmd

AF = mybir.ActivationFunctionType
F32 = mybir.dt.float32
BF16 = mybir.dt.bfloat16
AX = mybir.AxisListType

EPOCH = 32000
DMA_RING = 8
ARENA_BYTES = 207 * 1024
D = 2048
KC = 16
FFN_H = 5632
CTX_T = 2
GRID_W = 64
EPS = 1e-6
DEPTH = 4
ALPHA = (2 * DEPTH) ** 0.25
THETA = 10000.0


class Buf:
    __slots__ = ("key", "last_w", "readers")

    def __init__(self, key):
        self.key = key
        self.last_w = None
        self.readers = {}


class Op:
    __slots__ = ("idx", "eng", "fn", "deps", "stream", "tick", "signal", "isdma")


class Prog:
    ENG = ("pe", "act", "dve", "pool", "sp")

    def __init__(self, nc):
        self.nc = nc
        self.ops = []
        self.eng_ops = {e: [] for e in self.ENG}
        self.bufs = {}
        self.dma_cnt = {}
        self.last_on = {}
        self.last_all = {}

    def buf(self, key):
        b = self.bufs.get(key)
        if b is None:
            b = Buf(key)
            self.bufs[key] = b
        return b

    def op(self, eng, fn, reads=(), writes=(), dma=None):
        o = Op()
        o.idx = len(self.ops)
        o.eng = eng
        o.fn = fn
        o.isdma = dma is not None
        if dma is not None:
            n = self.dma_cnt.get(dma, 0)
            self.dma_cnt[dma] = n + 1
            o.stream = "dma_%s_%d" % (dma, n % DMA_RING)
        else:
            o.stream = eng
        o.signal = o.isdma
        o.tick = 0
        deps = {}
        ops = self.ops
        if o.isdma:
            prev = self.last_on.get(o.stream)
            if prev is not None:
                deps[o.stream] = prev
            self.last_on[o.stream] = o.idx

        def add(d):
            if d is None:
                return
            ps = ops[d].stream
            if ps == "pe" and o.stream == "pe":
                return
            if deps.get(ps, -1) < d:
                deps[ps] = d

        for b in reads:
            add(b.last_w)
        for b in writes:
            add(b.last_w)
            for d in b.readers.values():
                add(d)
        for b in writes:
            b.last_w = o.idx
            b.readers = {}
        wset = set(id(b) for b in writes)
        for b in reads:
            if id(b) not in wset:
                b.readers[o.stream] = o.idx
        o.deps = deps
        ops.append(o)
        self.eng_ops[eng].append(o)
        self.last_all[o.stream] = o.idx
        return o

    def barrier(self):
        snap = dict(self.last_all)
        for eng in self.ENG:
            o = Op()
            o.idx = len(self.ops)
            o.eng = eng
            o.fn = None
            o.isdma = False
            o.stream = eng
            o.signal = False
            o.tick = 0
            o.deps = {s_: d for s_, d in snap.items() if not (s_ == "pe" and eng == "pe")}
            self.ops.append(o)
            self.eng_ops[eng].append(o)

    def emit(self):
        nc = self.nc
        ops = self.ops
        for o in ops:
            for d in o.deps.values():
                ops[d].signal = True
        counters = {}
        for o in ops:
            if o.signal:
                counters[o.stream] = counters.get(o.stream, 0) + (16 if o.isdma else 1)
                o.tick = counters[o.stream]
        es = ExitStack()
        sems = {}
        for s, total in counters.items():
            n_ep = (total + EPOCH - 1) // EPOCH
            sems[s] = [es.enter_context(nc.semaphore("s_%s_%d" % (s, e))) for e in range(n_ep)]
        self.n_sems = sum(len(v) for v in sems.values())

        def sem_of(stream, tick):
            e = (tick - 1) // EPOCH
            return sems[stream][e], tick - e * EPOCH, e

        block = es.enter_context(nc.Block())

        def make_section(eng):
            my_ops = self.eng_ops[eng]

            def section(h):
                waited = {}
                ep_done = {}
                for o in my_ops:
                    for s, d in o.deps.items():
                        t = ops[d].tick
                        if waited.get(s, 0) >= t:
                            continue
                        sem, val, e = sem_of(s, t)
                        if s.startswith("dma_") and e > 0:
                            for pe_ in range(ep_done.get(s, 0), e):
                                h.wait_ge(sems[s][pe_], EPOCH)
                            ep_done[s] = max(ep_done.get(s, 0), e)
                        h.wait_ge(sem, val)
                        waited[s] = t
                    if o.fn is None:
                        continue
                    ins = o.fn(h)
                    if o.signal:
                        sem, val, e = sem_of(o.stream, o.tick)
                        ins.then_inc(sem, 16 if o.isdma else 1)
                mine = []
                for o in my_ops:
                    if o.isdma and o.stream not in mine:
                        mine.append(o.stream)
                for s in mine:
                    tot = counters[s]
                    for e in range(len(sems[s])):
                        h.wait_ge(sems[s][e], min(EPOCH, tot - e * EPOCH))
            return section

        for eng, reg in (("sp", block.sync), ("pe", block.tensor), ("act", block.scalar),
                         ("dve", block.vector), ("pool", block.gpsimd)):
            if self.eng_ops[eng]:
                reg(make_section(eng))
        es.close()


class Ctx:
    def __init__(self, nc, n_lat_tiles):
        self.nc = nc
        self.P = Prog(nc)
        self.es = ExitStack()
        self.NL = n_lat_tiles
        self.NT = CTX_T + n_lat_tiles
        self.cnt = 0
        self.rr = {}
        self.cap = ARENA_BYTES // 2
        self.arena = self.es.enter_context(nc.sbuf_tensor("arena", [128, self.cap], BF16))
        self.top = 0
        self.peak = 0

    def sb(self, name, shape, dt):
        n = 1
        for d_ in shape[1:]:
            n *= d_
        nb16 = n * (2 if dt == F32 else 1)
        off = self.top
        self.top += (nb16 + 15) // 16 * 16
        self.peak = max(self.peak, self.top)
        assert self.top <= self.cap, "SBUF arena overflow at %s: %d > %d" % (name, self.top * 2, self.cap * 2)
        v = self.arena[0:shape[0], off:off + nb16]
        if dt == F32:
            v = v.bitcast(F32)
        if len(shape) == 3:
            v = v.rearrange("p (a b) -> p a b", b=shape[2])
        return v

    def scope(self):
        C = self

        class _S:
            def __enter__(s_):
                s_.mark = C.top
                s_.rr = dict(C.rr)

            def __exit__(s_, *a):
                C.P.barrier()
                C.top = s_.mark
        return _S()

    def ps(self, name, shape, dt):
        return self.es.enter_context(self.nc.psum_tensor(name, shape, dt))

    def dram(self, name, shape, dt):
        return self.nc.dram_tensor(name, shape, dt, kind="Internal").ap()

    def b(self, key):
        return self.P.buf(key)

    def next(self, key, n):
        v = self.rr.get(key, 0)
        self.rr[key] = v + 1
        return v % n

    def blocks(self, tb=4):
        out = [[0, 1]]
        t = CTX_T
        while t < self.NT:
            out.append(list(range(t, min(t + tb, self.NT))))
            t += tb
        return out

    def hTv(self, kcn, ntok):
        return self.hT[:, 0:kcn * ntok].rearrange("p (k t) -> p k t", t=ntok)


def dbufs(C, name, t, c0, c1, gw=512):
    return [C.b((name, t, g)) for g in range(c0 // gw, (c1 - 1) // gw + 1)]


def setup_common(C):
    nc, P = C.nc, C.P
    C.identf = C.sb("identf", [128, 128], F32)
    C.identb = C.sb("identb", [128, 128], BF16)
    C.onesb = C.sb("onesb", [128, 128], BF16)
    P.op("sp", lambda h: h.dma_start(out=C.identf[:], in_=C.cst[:, 0:128]), writes=[C.b("identf")], dma="ld")
    P.op("dve", lambda h: h.tensor_copy(out=C.identb[:], in_=C.identf[:]), reads=[C.b("identf")], writes=[C.b("identb")])
    C.onesf = C.sb("onesf", [128, 128], F32)
    P.op("sp", lambda h: h.dma_start(out=C.onesf[:], in_=C.cst[:, 128:256]), writes=[C.b("onesf")], dma="ld")
    P.op("dve", lambda h: h.tensor_copy(out=C.onesb[:], in_=C.onesf[:]), reads=[C.b("onesf")], writes=[C.b("onesb")])
    C.epsc = C.sb("epsc", [128, 1], F32)
    P.op("sp", lambda h: h.dma_start(out=C.epsc[:], in_=C.cst[:, 256:257], allow_slow_non_contiguous=True), writes=[C.b("epsc")], dma="ld")
    C.gb = [C.ps("gb%d" % i, [128, 512], F32) for i in range(4)]
    C.ob = C.ps("ob", [128, 512], F32)
    C.db = C.ps("db", [128, 512], F32)
    C.tpf = C.ps("tpf", [128, 512], F32)
    C.tpbf = C.ps("tpb", [128, 512], F32)
    C.tpb = C.tpbf.bitcast(BF16)
    C.stg_f = [C.sb("stgf%d" % i, [128, 512], F32) for i in range(4)]
    C.stg_b = [C.sb("stgb%d" % i, [128, 512], BF16) for i in range(4)]
    C.hT = C.sb("hT", [128, 44 * 256], BF16)
    C.xin = [C.sb("xin%d" % i, [128, 2048], F32) for i in range(2)]
    C.xinb = [C.sb("xinb%d" % i, [128, 5632], BF16) for i in range(1)]
    C.wp = [C.sb("wp%d" % i, [128, 16, 512], BF16) for i in range(2)]
    C.sa = C.sb("sa", [128, 4, 512], F32)
    C.vecT = C.sb("vecT", [128, 8, KC], F32)


def cast_weight(C, name, w_ap, K, N, PW=512):
    kc = K // 128
    npan = N // PW
    wb = C.dram(name + "_bf", [npan, 128, kc, PW], BF16)
    for j in range(npan):
        src = w_ap[:, j * PW:(j + 1) * PW].rearrange("(k p) c -> p k c", p=128)
        C.P.op("pool", lambda h, j=j, src=src: h.dma_start(out=wb[j], in_=src),
               writes=[C.b((name, j))], dma="cast")
    return dict(ap=wb, name=name, kc=kc, npan=npan, pw=PW)


def evac(C, out_ap, in_ap, reads, writes, func=None, scale=1.0):
    if func is not None:
        C.P.op("act", lambda h: h.activation(out=out_ap, in_=in_ap, func=func, scale=scale), reads=reads, writes=writes)
        return
    if C.next("evac", 2) == 0:
        C.P.op("act", lambda h: h.activation(out=out_ap, in_=in_ap, func=AF.Copy), reads=reads, writes=writes)
    else:
        C.P.op("dve", lambda h: h.tensor_copy(out=out_ap, in_=in_ap), reads=reads, writes=writes)


def load_hT_mod(C, src, sname, tiles, scT, shT, vname):
    P = C.P
    for tl, t in enumerate(tiles):
        xi = C.next("xin", 2)
        xt = C.xin[xi]
        P.op("sp", lambda h, xt=xt, t=t: h.dma_start(out=xt[:], in_=src[t * 128:(t + 1) * 128, :]),
             reads=dbufs(C, sname, t, 0, D), writes=[C.b(("xin", xi))], dma="ld")
        for g in range(4):
            for q in range(4):
                kc = g * 4 + q
                P.op("pe", lambda h, xt=xt, kc=kc, q=q: h.transpose(C.tpf[:, q * 128:(q + 1) * 128],
                                                                   xt[:, kc * 128:(kc + 1) * 128], C.identf[:]),
                     reads=[C.b(("xin", xi)), C.b("identf")], writes=[C.b("tpf")])
            for q in range(4):
                kc = g * 4 + q
                P.op("act", lambda h, kc=kc, q=q, tl=tl: h.activation(
                    out=C.hTv(KC, 512)[:, kc, tl * 128:(tl + 1) * 128], in_=C.tpf[:, q * 128:(q + 1) * 128],
                    func=AF.Identity, scale=scT[:, kc:kc + 1], bias=shT[:, kc:kc + 1]),
                    reads=[C.b("tpf"), C.b(vname)], writes=[C.b(("hT", tl))])


NTOK_BIG = [False]


def ntok_for(K):
    return 512 if (K <= 2816 or NTOK_BIG[0]) else 256


def load_hT_bf(C, src, sname, tiles, K):
    P = C.P
    kcn = K // 128
    hv = C.hTv(kcn, ntok_for(K))
    for tl, t in enumerate(tiles):
        xi = C.next("xinb", 1)
        xt = C.xinb[xi]
        P.op("sp", lambda h, xt=xt, t=t: h.dma_start(out=xt[:, 0:K], in_=src[t * 128:(t + 1) * 128, :]),
             reads=dbufs(C, sname, t, 0, K), writes=[C.b(("xinb", xi))], dma="ld")
        for g0 in range(0, kcn, 8):
            n = min(8, kcn - g0)
            for q in range(n):
                kc = g0 + q
                P.op("pe", lambda h, xt=xt, kc=kc, q=q: h.transpose(C.tpb[:, q * 128:(q + 1) * 128],
                                                                   xt[:, kc * 128:(kc + 1) * 128], C.identb[:]),
                     reads=[C.b(("xinb", xi)), C.b("identb")], writes=[C.b("tpb")])
            src_ap = C.tpb[:, 0:n * 128].rearrange("p (k c) -> p k c", c=128)
            dst_ap = hv[:, g0:g0 + n, tl * 128:(tl + 1) * 128]
            evac(C, dst_ap, src_ap, [C.b("tpb")], [C.b(("hT", tl))])


def load_hT_fm(C, srcT, sname, tiles, K):
    kcn = K // 128
    t0 = tiles[0]
    n = len(tiles) * 128
    hv = C.hTv(kcn, ntok_for(K))
    C.P.op("sp", lambda h: h.dma_start(out=hv[:, 0:kcn, 0:n],
                                       in_=srcT[:, t0 * 128:t0 * 128 + n].rearrange("(k p) t -> p k t", p=128)),
           reads=[C.b((sname, t)) for t in tiles], writes=[C.b(("hT", tl)) for tl in range(len(tiles))], dma="ld")


def linear(C, W, tiles, epi, panels=None, hv=None, hbufs=None):
    P = C.P
    kc_tot, pw = W["kc"], W["pw"]
    panels = list(range(W["npan"])) if panels is None else panels
    kgroups = [(k0, min(16, kc_tot - k0)) for k0 in range(0, kc_tot, 16)]
    if hv is None:
        hv = C.hTv(kc_tot, ntok_for(kc_tot * 128))
        hbufs = [C.b(("hT", tl)) for tl in range(len(tiles))]
    items = [(j, gi, k0, kn) for j in panels for gi, (k0, kn) in enumerate(kgroups)]

    def issue(item):
        j, gi, k0, kn = item
        wi = C.next("wp", 2)
        wt = C.wp[wi]
        P.op("sp", lambda h, wt=wt, j=j, k0=k0, kn=kn: h.dma_start(out=wt[:, 0:kn, 0:pw], in_=W["ap"][j, :, k0:k0 + kn, :]),
             reads=[C.b((W["name"], j))], writes=[C.b(("wp", wi))], dma="ld")
        return wi
    pending = issue(items[0])
    banks = None
    for idx, (j, gi, k0, kn) in enumerate(items):
        wi = pending
        wt = C.wp[wi]
        if idx + 1 < len(items):
            pending = issue(items[idx + 1])
        if gi == 0:
            banks = [C.next("gb", 4) for _ in tiles]
        for tl, t in enumerate(tiles):
            bi = banks[tl]
            for k in range(kn):
                kk = k0 + k
                P.op("pe", lambda h, bi=bi, tl=tl, kk=kk, k=k, wt=wt: h.matmul(
                    C.gb[bi][:, 0:pw], lhsT=hv[:, kk, tl * 128:(tl + 1) * 128], rhs=wt[:, k, 0:pw],
                    start=(kk == 0), stop=(kk == kc_tot - 1)),
                    reads=[hbufs[tl], C.b(("wp", wi))], writes=[C.b(("gb", bi))])
        if gi == len(kgroups) - 1:
            for tl, t in enumerate(tiles):
                bi = banks[tl]
                epi(t, tl, j, C.gb[bi], C.b(("gb", bi)))


def store(C, dst, dname, t, c0, c1, src_ap, src_buf, gw=512):
    C.P.op("sp", lambda h: h.dma_start(out=dst[t * 128:(t + 1) * 128, c0:c1], in_=src_ap),
           reads=[src_buf], writes=dbufs(C, dname, t, c0, c1, gw), dma="st")


def epi_copy(C, dst, dname, dt, func=None):
    ring = C.stg_f if dt == F32 else C.stg_b
    rname = "stgf" if dt == F32 else "stgb"

    def epi(t, tl, j, bank, bbuf, pw=512):
        si = C.next(rname, 4)
        sbuf = C.b((rname, si))
        evac(C, ring[si][:, 0:pw], bank[:, 0:pw], [bbuf], [sbuf], func=func)
        store(C, dst, dname, t, j * pw, (j + 1) * pw, ring[si][:, 0:pw], sbuf)
    return epi


def adaln_all(C, cvec, ada_w, ada_b, layers):
    P = C.P
    cs = C.sb("cs", [2, D], F32)
    csT = C.sb("csT", [128, KC, 2], BF16)
    C.csT = csT
    P.op("sp", lambda h: h.dma_start(out=cs[:], in_=cvec), writes=[C.b("cs")], dma="ld")
    P.op("act", lambda h: h.activation(out=cs[:], in_=cs[:], func=AF.Silu), reads=[C.b("cs")], writes=[C.b("cs")])
    for kc in range(KC):
        P.op("pe", lambda h, kc=kc: h.transpose(C.tpf[:, kc * 2:kc * 2 + 2], cs[:, kc * 128:(kc + 1) * 128], C.identf[0:2, 0:2]),
             reads=[C.b("cs"), C.b("identf")], writes=[C.b("tpf")])
    P.op("dve", lambda h: h.tensor_copy(out=csT[:], in_=C.tpf[:, 0:KC * 2].rearrange("p (k c) -> p k c", c=2)),
         reads=[C.b("tpf")], writes=[C.b("csT")])
    MOD = C.dram("MOD", [DEPTH, 2, 6 * D], F32)
    C.MOD = MOD
    mrow = [C.sb("mrow%d" % i, [2, 512], F32) for i in range(2)]
    brow = [C.sb("brow%d" % i, [2, 512], F32) for i in range(2)]
    C.ada_state = (csT, mrow, brow, MOD, ada_w, ada_b)


def adaln_layer(C, l):
    P = C.P
    csT, mrow, brow, MOD, ada_w, ada_b = C.ada_state
    if True:
        for j in range(24):
            ri = C.next("mrow", 2)
            for r in range(2):
                P.op("sp", lambda h, l=l, r=r, j=j, ri=ri: h.dma_start(out=brow[ri][r:r + 1, :], in_=ada_b[l:l + 1, j * 512:(j + 1) * 512]),
                     writes=[C.b(("brow", ri))], dma="ld")
            wi = C.next("wp", 2)
            wt = C.wp[wi]
            src = ada_w[l][:, j * 512:(j + 1) * 512].rearrange("(k p) c -> p k c", p=128)
            P.op("pool", lambda h, wt=wt, src=src: h.dma_start(out=wt[:], in_=src), writes=[C.b(("wp", wi))], dma="cast")
            bi = C.next("gb", 4)
            for kc in range(KC):
                P.op("pe", lambda h, bi=bi, kc=kc, wt=wt: h.matmul(C.gb[bi][0:2, :], lhsT=csT[:, kc, :], rhs=wt[:, kc, :],
                                                                  start=(kc == 0), stop=(kc == KC - 1)),
                     reads=[C.b("csT"), C.b(("wp", wi))], writes=[C.b(("gb", bi))])
            P.op("dve", lambda h, bi=bi, ri=ri: h.tensor_tensor(out=mrow[ri][:], in0=C.gb[bi][0:2, :], in1=brow[ri][:], op=ALU.add),
                 reads=[C.b(("gb", bi)), C.b(("brow", ri))], writes=[C.b(("mrow", ri))])
            if 4 <= j < 8 or 16 <= j < 20:
                P.op("dve", lambda h, ri=ri: h.tensor_scalar_add(out=mrow[ri][:], in0=mrow[ri][:], scalar1=1.0),
                     reads=[C.b(("mrow", ri))], writes=[C.b(("mrow", ri))])
            P.op("sp", lambda h, l=l, j=j, ri=ri: h.dma_start(out=MOD[l, :, j * 512:(j + 1) * 512], in_=mrow[ri][:]),
                 reads=[C.b(("mrow", ri))], writes=[C.b(("MOD", l))], dma="st")


def load_layer_vectors(C, l, ln_g, ln_b):
    P = C.P
    for wi, off in enumerate((0, D, 3 * D, 4 * D)):
        for r in range(2):
            src = C.MOD[l, r, off:off + D].rearrange("(k p) -> p k", p=128)
            P.op("sp", lambda h, wi=wi, r=r, src=src: h.dma_start(out=C.vecT[:, wi * 2 + r, :], in_=src, allow_slow_non_contiguous=True),
                 reads=[C.b(("MOD", l))], writes=[C.b("vecT")], dma="ld")


def load_bcast(C, l, sub, ln_g, ln_b):
    P = C.P
    C.bc = [C.sb("bc%d" % i, [128, D], F32) for i in range(4)]
    goff = 2 * D if sub == 0 else 5 * D
    srcs = [C.MOD[l, 0:1, goff:goff + D], C.MOD[l, 1:2, goff:goff + D], ln_g[l, sub:sub + 1, :], ln_b[l, sub:sub + 1, :]]
    for i, s in enumerate(srcs):
        P.op("sp", lambda h, i=i, s=s: h.dma_start(out=C.bc[i][:], in_=s.partition_broadcast(128)),
             reads=[C.b(("MOD", l))], writes=[C.b(("bc", i))], dma="ld")


def resid_ln(C, X, xname, Y, yname, OUT, oname, out_row0=None):
    P = C.P
    C.rl_x = C.xin
    C.rl_y = [C.sb("rly%d" % i, [128, D], F32) for i in range(2)]
    C.rl_st = C.sb("rlst", [128, 4, 6], F32)
    C.rl_mv = C.sb("rlmv", [128, 4], F32)
    for t in range(C.NT):
        if out_row0 is not None and t < CTX_T:
            continue
        i = C.next("xin", 2)
        xt, yt = C.rl_x[i], C.rl_y[i]
        bx, by = C.b(("xin", i)), C.b(("rly", i))
        P.op("sp", lambda h, xt=xt, t=t: h.dma_start(out=xt[:], in_=X[t * 128:(t + 1) * 128, :]),
             reads=dbufs(C, xname, t, 0, D), writes=[bx], dma="ld")
        P.op("sp", lambda h, yt=yt, t=t: h.dma_start(out=yt[:], in_=Y[t * 128:(t + 1) * 128, :]),
             reads=dbufs(C, yname, t, 0, D), writes=[by], dma="ld")
        gi = 1 if t < CTX_T else 0
        P.op("dve", lambda h, yt=yt, gi=gi: h.tensor_tensor(out=yt[:], in0=yt[:], in1=C.bc[gi][:], op=ALU.mult),
             reads=[by, C.b(("bc", gi))], writes=[by])
        P.op("dve", lambda h, xt=xt, yt=yt: h.scalar_tensor_tensor(out=yt[:], in0=xt[:], scalar=ALPHA, in1=yt[:],
                                                                  op0=ALU.mult, op1=ALU.add),
             reads=[bx, by], writes=[by])
        for q in range(4):
            P.op("dve", lambda h, yt=yt, q=q: h.bn_stats(out=C.rl_st[:, q, :], in_=yt[:, q * 512:(q + 1) * 512]),
                 reads=[by], writes=[C.b("rlst")])
        P.op("dve", lambda h: h.bn_aggr(out=C.rl_mv[:, 0:2], in_=C.rl_st[:].rearrange("p a b -> p (a b)")),
             reads=[C.b("rlst")], writes=[C.b("rlmv")])
        P.op("act", lambda h: h.activation(out=C.rl_mv[:, 2:3], in_=C.rl_mv[:, 1:2], func=AF.Sqrt, bias=C.epsc[:, 0:1], scale=1.0),
             reads=[C.b("rlmv"), C.b("epsc")], writes=[C.b("rlmv")])
        P.op("dve", lambda h: h.reciprocal(out=C.rl_mv[:, 2:3], in_=C.rl_mv[:, 2:3]), reads=[C.b("rlmv")], writes=[C.b("rlmv")])
        P.op("dve", lambda h: h.scalar_tensor_tensor(out=C.rl_mv[:, 3:4], in0=C.rl_mv[:, 0:1], scalar=-1.0, in1=C.rl_mv[:, 2:3],
                                                     op0=ALU.mult, op1=ALU.mult),
             reads=[C.b("rlmv")], writes=[C.b("rlmv")])
        P.op("act", lambda h, xt=xt, yt=yt: h.activation(out=xt[:], in_=yt[:], func=AF.Identity, scale=C.rl_mv[:, 2:3], bias=C.rl_mv[:, 3:4]),
             reads=[by, C.b("rlmv")], writes=[bx])
        P.op("dve", lambda h, xt=xt: h.tensor_tensor(out=xt[:], in0=xt[:], in1=C.bc[2][:], op=ALU.mult),
             reads=[bx, C.b(("bc", 2))], writes=[bx])
        P.op("dve", lambda h, xt=xt: h.tensor_tensor(out=xt[:], in0=xt[:], in1=C.bc[3][:], op=ALU.add),
             reads=[bx, C.b(("bc", 3))], writes=[bx])
        if out_row0 is None:
            P.op("sp", lambda h, xt=xt, t=t: h.dma_start(out=OUT[t * 128:(t + 1) * 128, :], in_=xt[:]),
                 reads=[bx], writes=dbufs(C, oname, t, 0, D), dma="st")
        else:
            r0 = (t - CTX_T) * 128
            P.op("sp", lambda h, xt=xt, r0=r0: h.dma_start(out=OUT[r0:r0 + 128, :], in_=xt[:]),
                 reads=[bx], writes=[C.b((oname, t))], dma="st")


def ffn(C, l, X, xname, Wi, Wo, U, Y2):
    P = C.P

    for tiles in C.blocks():
        r = 1 if tiles[0] < CTX_T else 0
        load_hT_mod(C, X, xname, tiles, C.vecT[:, 3 * 2 + r, :], C.vecT[:, 2 * 2 + r, :], "vecT")

        def epi(t, tl, j, bank, bbuf):
            if j < 11:
                P.op("act", lambda h: h.activation(out=C.sa[:, tl, :], in_=bank[:], func=AF.Silu),
                     reads=[bbuf], writes=[C.b(("sa", tl))])
            else:
                si = C.next("stgb", 4)
                sbuf = C.b(("stgb", si))
                P.op("dve", lambda h: h.tensor_tensor(out=C.stg_b[si][:], in0=bank[:], in1=C.sa[:, tl, :], op=ALU.mult),
                     reads=[bbuf, C.b(("sa", tl))], writes=[sbuf])
                store(C, U, "U", t, (j - 11) * 512, (j - 10) * 512, C.stg_b[si][:], sbuf)
        order = []
        for j in range(11):
            order += [j, 11 + j]
        linear(C, Wi, tiles, epi, panels=order)
    with C.scope():
        old_hT = C.hT
        C.hT = C.sb("hTbig", [128, 44 * 512], BF16)
        NTOK_BIG[0] = True
        for tiles in C.blocks(4):
            load_hT_bf(C, U, "U", tiles, FFN_H)
            linear(C, Wo, tiles, epi_copy(C, Y2, "Y2", F32))
        NTOK_BIG[0] = False
        C.hT = old_hT


def out_proj_tok(C, SRC, sname, K, Wo, Y):
    for tiles in C.blocks(4 if K <= 2816 else 2):
        load_hT_bf(C, SRC, sname, tiles, K)
        linear(C, Wo, tiles, epi_copy(C, Y, "Y", F32))


def linear_fm(C, W, ntok, epi, mchunks=None, hv=None, hbufs=None):
    P = C.P
    kc_tot, pw = W["kc"], W["pw"]
    cw = min(128, pw)
    cpp = pw // cw
    if hv is None:
        hv = C.hTv(kc_tot, ntok_for(kc_tot * 128))
        hbufs = [C.b(("hT", tl)) for tl in range((ntok + 127) // 128)]
    items = []
    for j in range(W["npan"]):
        ms = [m for m in range(j * cpp, j * cpp + cpp) if mchunks is None or m in mchunks]
        if ms:
            items.append((j, ms))

    def issue(item):
        j = item[0]
        wi = C.next("wp", 2)
        wt = C.wp[wi]
        P.op("sp", lambda h, wt=wt, j=j: h.dma_start(out=wt[:, 0:kc_tot, 0:pw], in_=W["ap"][j, :, :, :]),
             reads=[C.b((W["name"], j))], writes=[C.b(("wp", wi))], dma="ld")
        return wi
    if not items:
        return
    pending = issue(items[0])
    for idx, (j, ms) in enumerate(items):
        wi = pending
        wt = C.wp[wi]
        if idx + 1 < len(items):
            pending = issue(items[idx + 1])
        for m in ms:
            sub = m % cpp
            bi = C.next("gb", 4)
            for k in range(kc_tot):
                P.op("pe", lambda h, bi=bi, k=k, wt=wt, sub=sub: h.matmul(
                    C.gb[bi][0:cw, 0:ntok], lhsT=wt[:, k, sub * cw:(sub + 1) * cw], rhs=hv[:, k, 0:ntok],
                    start=(k == 0), stop=(k == kc_tot - 1)),
                    reads=list(hbufs) + [C.b(("wp", wi))], writes=[C.b(("gb", bi))])
            epi(m, C.gb[bi], C.b(("gb", bi)))


def store_rows(C, dst, dname, key, r0, nrows, t0, ntok, src_ap, src_buf):
    tl = list(range(t0, t0 + (ntok + 127) // 128))
    C.P.op("sp", lambda h: h.dma_start(out=dst[r0:r0 + nrows, t0 * 128:t0 * 128 + ntok], in_=src_ap),
           reads=[src_buf], writes=[C.b((dname, key, t)) for t in tl], dma="st")


def store_fm(C, dst, dname, m, t0, ntok, src_ap, src_buf):
    tl = list(range(t0, t0 + (ntok + 127) // 128))
    C.P.op("sp", lambda h: h.dma_start(out=dst[m * 128:(m + 1) * 128, t0 * 128:t0 * 128 + ntok], in_=src_ap),
           reads=[src_buf], writes=[C.b((dname, m, t)) for t in tl], dma="st")


def attn_core(C, groups, scale, OT, oname):
    P = C.P
    T = C.NT * 128
    allt = list(range(C.NT))
    if True:
        C.att_kT = C.sb("att_kT", [128, 2, T], BF16)
        C.att_v = C.sb("att_v", [128, C.NT, 128], BF16)
        C.att_qT = [C.sb("att_qT%d" % i, [128, 2, 512], BF16) for i in range(2)]
        C.att_pT = [C.sb("att_pT%d" % i, [128, 512], BF16) for i in range(3)]
        C.att_rd = C.sb("att_rd", [128, 512], F32)
        C.att_oo = [C.sb("att_oo%d" % i, [128, 512], BF16) for i in range(2)]
    kT, vv, qT, pT, rd, oo = C.att_kT, C.att_v, C.att_qT, C.att_pT, C.att_rd, C.att_oo
    for grp in groups:
        nch = len(grp["k_chunks"])
        for ci, (ap, r0, nr, kp) in enumerate(grp["k_chunks"]):
            P.op("sp", lambda h, ci=ci, ap=ap, r0=r0, nr=nr: h.dma_start(out=kT[0:nr, ci, :], in_=ap[r0:r0 + nr, :]),
                 reads=[C.b(kp + (t,)) for t in allt], writes=[C.b("att_kT")], dma="ld")
        vap, vname, vc0, vg = grp["v"]
        P.op("sp", lambda h, vap=vap, vc0=vc0: h.dma_start(out=vv[:], in_=vap[:, vc0:vc0 + 128].rearrange("(t p) c -> p t c", p=128)),
             reads=[C.b((vname, t, (vc0 // 512))) for t in allt], writes=[C.b("att_v")], dma="ld")
        for hd in grp["heads"]:
            for tiles in C.blocks():
                nq = len(tiles) * 128
                t0 = tiles[0]
                keys = [0, 1] if t0 < CTX_T else allt
                qi = C.next("att_qT", 2)
                for ci, (ap, r0, nr, kp) in enumerate(hd["q_chunks"]):
                    P.op("sp", lambda h, qi=qi, ci=ci, ap=ap, r0=r0, nr=nr, t0=t0, nq=nq: h.dma_start(
                        out=qT[qi][0:nr, ci, 0:nq], in_=ap[r0:r0 + nr, t0 * 128:t0 * 128 + nq]),
                        reads=[C.b(kp + (t,)) for t in tiles], writes=[C.b(("att_qT", qi))], dma="ld")
                def pv(ki, kt, pi, nq=nq, nk=len(keys)):
                    first, lastk = (ki == 0), (ki == nk - 1)
                    P.op("pe", lambda h: h.matmul(C.ob[:, 0:nq], lhsT=vv[:, kt, :], rhs=pT[pi][:, 0:nq], start=first, stop=lastk),
                         reads=[C.b("att_v"), C.b(("att_pT", pi))], writes=[C.b("ob")])
                    P.op("pe", lambda h: h.matmul(C.db[:, 0:nq], lhsT=C.onesb[:], rhs=pT[pi][:, 0:nq], start=first, stop=lastk),
                         reads=[C.b("onesb"), C.b(("att_pT", pi))], writes=[C.b("db")])
                prev = None
                for ki, kt in enumerate(keys):
                    bi = C.next("gb", 4)
                    for ci, (ap, r0, nr, kp) in enumerate(grp["k_chunks"]):
                        P.op("pe", lambda h, bi=bi, kt=kt, qi=qi, nq=nq, ci=ci, nr=nr: h.matmul(
                            C.gb[bi][:, 0:nq], lhsT=kT[0:nr, ci, kt * 128:(kt + 1) * 128], rhs=qT[qi][0:nr, ci, 0:nq],
                            start=(ci == 0), stop=(ci == nch - 1)),
                            reads=[C.b("att_kT"), C.b(("att_qT", qi))], writes=[C.b(("gb", bi))])
                    pi = C.next("att_pT", 3)
                    P.op("act", lambda h, bi=bi, pi=pi, nq=nq: h.activation(out=pT[pi][:, 0:nq], in_=C.gb[bi][:, 0:nq], func=AF.Exp, scale=scale),
                         reads=[C.b(("gb", bi))], writes=[C.b(("att_pT", pi))])
                    if prev is not None:
                        pv(*prev)
                    prev = (ki, kt, pi)
                pv(*prev)
                P.op("dve", lambda h, nq=nq: h.reciprocal(out=rd[:, 0:nq], in_=C.db[:, 0:nq]), reads=[C.b("db")], writes=[C.b("att_rd")])
                oi = C.next("att_oo", 2)
                P.op("dve", lambda h, nq=nq, oi=oi: h.tensor_tensor(out=oo[oi][:, 0:nq], in0=C.ob[:, 0:nq], in1=rd[:, 0:nq], op=ALU.mult),
                     reads=[C.b("ob"), C.b("att_rd")], writes=[C.b(("att_oo", oi))])
                store_rows(C, OT, oname, hd["out"], hd["out"] * 128, 128, t0, nq, oo[oi][:, 0:nq], C.b(("att_oo", oi)))


def out_proj_fm(C, OT, oname, nchunks, Wo, Y):
    P = C.P
    K_ = nchunks * 128
    nt = ntok_for(K_)
    for tiles in C.blocks(nt // 128):
        hv = C.hTv(nchunks, nt)
        n = len(tiles) * 128
        t0 = tiles[0]
        P.op("sp", lambda h, hv=hv, n=n, t0=t0: h.dma_start(out=hv[:, 0:nchunks, 0:n], in_=OT[:, t0 * 128:t0 * 128 + n].rearrange("(k p) t -> p k t", p=128)),
             reads=[C.b((oname, m, t)) for m in range(nchunks) for t in tiles], writes=[C.b(("hT", tl)) for tl in range(len(tiles))], dma="ld")
        linear(C, Wo, tiles, epi_copy(C, Y, "Y", F32))


def gqa_mixer(C, l, X, Y, w_in, w_in_sw, qk_gain, rope_tab, w_out):
    P = C.P
    T = C.NT * 128
    Wq = cast_weight(C, "gqa_in%d" % l, w_in, D, 3072)
    Ws = cast_weight(C, "gqa_sw%d" % l, w_in_sw, D, 2560)
    Wo = cast_weight(C, "gqa_out%d" % l, w_out, D, D)
    QT = C.dram("gqa_QT", [2560, T], BF16)
    V = C.dram("gqa_V", [T, 512], BF16)
    OT = C.dram("gqa_OT", [D, T], BF16)
    tab = C.sb("ropetab", [128, 2, 512], F32)
    gq = C.sb("gqag", [128, 4], F32)
    P.op("sp", lambda h: h.dma_start(out=gq[:], in_=qk_gain.rearrange("a p -> p a"), allow_slow_non_contiguous=True),
         writes=[C.b("gqag")], dma="ld")
    sq = C.sb("gqa_sq", [128, 512], BF16)
    rs = C.sb("gqa_rs", [128, 4, 512], F32)
    t1 = C.sb("gqa_t1", [128, 4, 512], F32)
    t2 = C.sb("gqa_t2", [128, 512], F32)
    qo = [C.sb("gqa_qo%d" % i, [128, 512], BF16) for i in range(2)]

    def mk_epis(ntok, t0):
        def epi_a(m, bank, bbuf):
            w, sub = (0 if m < 16 else 1), m % 4
            P.op("act", lambda h: h.activation(out=sq[:, 0:ntok], in_=bank[:, 0:ntok], func=AF.Square), reads=[bbuf], writes=[C.b("gqa_sq")])
            P.op("pe", lambda h: h.matmul(C.db[:, 0:ntok], lhsT=C.onesb[:], rhs=sq[:, 0:ntok], start=True, stop=True),
                 reads=[C.b("gqa_sq"), C.b("onesb")], writes=[C.b("db")])
            P.op("act", lambda h: h.activation(out=rs[:, sub, 0:ntok], in_=C.db[:, 0:ntok], func=AF.Sqrt, scale=1.0 / 128, bias=C.epsc[:, 0:1]),
                 reads=[C.b("db"), C.b("epsc")], writes=[C.b(("gqa_rs", sub))])
            P.op("dve", lambda h: h.reciprocal(out=rs[:, sub, 0:ntok], in_=rs[:, sub, 0:ntok]), reads=[C.b(("gqa_rs", sub))], writes=[C.b(("gqa_rs", sub))])
            P.op("dve", lambda h: h.scalar_tensor_tensor(out=t1[:, sub, 0:ntok], in0=bank[:, 0:ntok], scalar=gq[:, 2 * w:2 * w + 1], in1=tab[:, 0, 0:ntok],
                                                         op0=ALU.mult, op1=ALU.mult),
                 reads=[bbuf, C.b("ropetab"), C.b("gqag")], writes=[C.b(("gqa_t1", sub))])

        def epi_b(m, bank, bbuf):
            w, sub = (0 if m < 16 else 1), m % 4
            P.op("dve", lambda h: h.scalar_tensor_tensor(out=t2[:, 0:ntok], in0=bank[:, 0:ntok], scalar=gq[:, 2 * w + 1:2 * w + 2], in1=tab[:, 1, 0:ntok],
                                                         op0=ALU.mult, op1=ALU.mult),
                 reads=[bbuf, C.b("ropetab"), C.b("gqag")], writes=[C.b("gqa_t2")])
            P.op("dve", lambda h: h.tensor_tensor(out=t2[:, 0:ntok], in0=t1[:, sub, 0:ntok], in1=t2[:, 0:ntok], op=ALU.add),
                 reads=[C.b(("gqa_t1", sub)), C.b("gqa_t2")], writes=[C.b("gqa_t2")])
            qi = C.next("gqa_qo", 2)
            P.op("dve", lambda h: h.tensor_tensor(out=qo[qi][:, 0:ntok], in0=t2[:, 0:ntok], in1=rs[:, sub, 0:ntok], op=ALU.mult),
                 reads=[C.b("gqa_t2"), C.b(("gqa_rs", sub))], writes=[C.b(("gqa_qo", qi))])
            store_rows(C, QT, "gqa_QT", m, m * 128, 128, t0, ntok, qo[qi][:, 0:ntok], C.b(("gqa_qo", qi)))
        return epi_a, epi_b

    for tiles in C.blocks():
        r = 1 if tiles[0] < CTX_T else 0
        ntok = len(tiles) * 128
        t0 = tiles[0]
        load_hT_mod(C, X, "X", tiles, C.vecT[:, 1 * 2 + r, :], C.vecT[:, 0 * 2 + r, :], "vecT")
        for i in range(2):
            P.op("sp", lambda h, i=i, t0=t0, ntok=ntok: h.dma_start(out=tab[:, i, 0:ntok], in_=rope_tab[i, :, t0 * 128:t0 * 128 + ntok]),
                 writes=[C.b("ropetab")], dma="ld")
        ea, eb = mk_epis(ntok, t0)
        for j in range(5):
            ms = list(range(j * 4, j * 4 + 4))
            linear_fm(C, Wq, ntok, ea, mchunks=ms)
            linear_fm(C, Ws, ntok, eb, mchunks=ms)
        ecv = epi_copy(C, V, "gqa_V", BF16)
        linear(C, Wq, tiles, lambda t, tl, j, bank, bbuf, ecv=ecv: ecv(t, tl, j - 5, bank, bbuf), panels=[5])

    groups = []
    for g in range(4):
        groups.append(dict(k_chunks=[(QT, (16 + g) * 128, 128, ("gqa_QT", 16 + g))], v=(V, "gqa_V", g * 128, 0),
                           heads=[dict(q_chunks=[(QT, (g * 4 + hh) * 128, 128, ("gqa_QT", g * 4 + hh))], out=g * 4 + hh) for hh in range(4)]))
    attn_core(C, groups, 128 ** -0.5, OT, "gqa_OT")
    out_proj_fm(C, OT, "gqa_OT", 16, Wo, Y)


def mla_mixer(C, l, X, Y, w_a, w_kr, w_kr_sw, w_qn, w_qr, w_qr_sw, w_kn, w_v, norms, rope_tab, w_out):
    P = C.P
    T = C.NT * 128
    Wa = cast_weight(C, "mla_a%d" % l, w_a, D, 1024)
    import os
    NC_ = int(os.environ.get("MLA_NCAST", "99"))
    specs = [("mla_kr", w_kr, D, 64, 64), ("mla_krs", w_kr_sw, D, 64, 64), ("mla_qn", w_qn, 512, 2048, 512), ("mla_qr", w_qr, 512, 1024, 64),
             ("mla_qrs", w_qr_sw, 512, 1024, 64), ("mla_kn", w_kn, 512, 2048, 512), ("mla_v", w_v, 512, 2048, 512), ("mla_out", w_out, D, D, 512)]
    Ws = []
    for i, (nm, w_, k_, n_, pw_) in enumerate(specs):
        if i >= NC_:
            return
        Ws.append(cast_weight(C, nm + "%d" % l, w_, k_, n_, PW=pw_))
    Wkr, Wkrs, Wqn, Wqr, Wqrs, Wkn, Wv, Wo = Ws
    KN = C.dram("mla_KN", [2048, T], BF16)
    KR = C.dram("mla_KR", [64, T], BF16)
    QN = C.dram("mla_QN", [2048, T], BF16)
    QR = C.dram("mla_QR", [1024, T], BF16)
    V = C.dram("mla_V", [T, 2048], BF16)
    OT = C.dram("mla_OT", [D, T], BF16)
    tab = C.sb("mla_tab", [64, 2, 512], F32)
    gn = C.sb("mla_gn", [128, 2, 4], F32)
    for i in range(2):
        P.op("sp", lambda h, i=i: h.dma_start(out=gn[:, i, :], in_=norms[i].rearrange("(k p) -> p k", p=128), allow_slow_non_contiguous=True),
             writes=[C.b("mla_gn")], dma="ld")
    sq = C.sb("mla_sq", [128, 512], BF16)
    rs = C.sb("mla_rs", [128, 512], F32)
    ssa = C.sb("mla_ssa", [128, 512], F32)
    raw = C.sb("mla_raw", [128, 4, 512], F32)
    cn = [C.sb("mla_cn%d" % i, [128, 4, 512], BF16) for i in range(2)]
    t1 = C.sb("mla_t1", [64, 512], F32)
    t2 = C.sb("mla_t2", [64, 512], F32)
    ro = [C.sb("mla_ro%d" % i, [64, 512], BF16) for i in range(2)]

    def mk_epi_a(ntok):
        def epi_a(m, bank, bbuf):
            grp, mm = m // 4, m % 4
            P.op("act", lambda h: h.activation(out=sq[:, 0:ntok], in_=bank[:, 0:ntok], func=AF.Square), reads=[bbuf], writes=[C.b("mla_sq")])
            P.op("pe", lambda h: h.matmul(C.db[:, 0:ntok], lhsT=C.onesb[:], rhs=sq[:, 0:ntok], start=True, stop=True),
                 reads=[C.b("mla_sq"), C.b("onesb")], writes=[C.b("db")])
            if mm == 0:
                P.op("dve", lambda h: h.tensor_copy(out=ssa[:, 0:ntok], in_=C.db[:, 0:ntok]), reads=[C.b("db")], writes=[C.b("mla_ssa")])
            else:
                P.op("dve", lambda h: h.tensor_tensor(out=ssa[:, 0:ntok], in0=C.db[:, 0:ntok], in1=ssa[:, 0:ntok], op=ALU.add),
                     reads=[C.b("db"), C.b("mla_ssa")], writes=[C.b("mla_ssa")])
            P.op("dve", lambda h: h.tensor_copy(out=raw[:, mm, 0:ntok], in_=bank[:, 0:ntok]), reads=[bbuf], writes=[C.b(("mla_raw", mm))])
            if mm == 3:
                P.op("act", lambda h: h.activation(out=rs[:, 0:ntok], in_=ssa[:, 0:ntok], func=AF.Sqrt, scale=1.0 / 512, bias=C.epsc[:, 0:1]),
                     reads=[C.b("mla_ssa"), C.b("epsc")], writes=[C.b("mla_rs")])
                P.op("dve", lambda h: h.reciprocal(out=rs[:, 0:ntok], in_=rs[:, 0:ntok]), reads=[C.b("mla_rs")], writes=[C.b("mla_rs")])
                for q in range(4):
                    P.op("dve", lambda h, q=q: h.scalar_tensor_tensor(out=cn[grp][:, q, 0:ntok], in0=raw[:, q, 0:ntok], scalar=gn[:, grp, q:q + 1],
                                                                     in1=rs[:, 0:ntok], op0=ALU.mult, op1=ALU.mult),
                         reads=[C.b(("mla_raw", q)), C.b("mla_rs"), C.b("mla_gn")], writes=[C.b(("mla_cn", grp))])
        return epi_a

    def mk_rope_epis(dst, dname, key, r0, ntok, t0):
        def ea(m, bank, bbuf):
            P.op("dve", lambda h: h.tensor_tensor(out=t1[:, 0:ntok], in0=bank[0:64, 0:ntok], in1=tab[:, 0, 0:ntok], op=ALU.mult),
                 reads=[bbuf, C.b("mla_tab")], writes=[C.b("mla_t1")])

        def eb(m, bank, bbuf):
            P.op("dve", lambda h: h.tensor_tensor(out=t2[:, 0:ntok], in0=bank[0:64, 0:ntok], in1=tab[:, 1, 0:ntok], op=ALU.mult),
                 reads=[bbuf, C.b("mla_tab")], writes=[C.b("mla_t2")])
            ri = C.next("mla_ro", 2)
            P.op("dve", lambda h: h.tensor_tensor(out=ro[ri][:, 0:ntok], in0=t1[:, 0:ntok], in1=t2[:, 0:ntok], op=ALU.add),
                 reads=[C.b("mla_t1"), C.b("mla_t2")], writes=[C.b(("mla_ro", ri))])
            store_rows(C, dst, dname, key, r0, 64, t0, ntok, ro[ri][:, 0:ntok], C.b(("mla_ro", ri)))
        return ea, eb

    def mk_copy_fm(dst, dname, ntok, t0):
        def e(m, bank, bbuf):
            si = C.next("stgb", 4)
            sbuf = C.b(("stgb", si))
            evac(C, C.stg_b[si][:, 0:ntok], bank[:, 0:ntok], [bbuf], [sbuf])
            store_rows(C, dst, dname, m, m * 128, 128, t0, ntok, C.stg_b[si][:, 0:ntok], sbuf)
        return e

    for tiles in C.blocks():
        r = 1 if tiles[0] < CTX_T else 0
        ntok = len(tiles) * 128
        t0 = tiles[0]
        load_hT_mod(C, X, "X", tiles, C.vecT[:, 1 * 2 + r, :], C.vecT[:, 0 * 2 + r, :], "vecT")
        for i in range(2):
            P.op("sp", lambda h, i=i, t0=t0, ntok=ntok: h.dma_start(out=tab[:, i, 0:ntok], in_=rope_tab[i, :, t0 * 128:t0 * 128 + ntok]),
                 writes=[C.b("mla_tab")], dma="ld")
        import os
        STOP = int(os.environ.get("MLA_STOP", "99"))
        if STOP < 1:
            continue
        linear_fm(C, Wa, ntok, mk_epi_a(ntok))
        if STOP < 2:
            continue
        ea, eb = mk_rope_epis(KR, "mla_KR", 0, 0, ntok, t0)
        linear_fm(C, Wkr, ntok, ea)
        linear_fm(C, Wkrs, ntok, eb)
        cqv, ckv = cn[0], cn[1]
        cqb, ckb = [C.b(("mla_cn", 0))] * 4, [C.b(("mla_cn", 1))] * 4
        if STOP < 3:
            continue
        linear_fm(C, Wqn, ntok, mk_copy_fm(QN, "mla_QN", ntok, t0), hv=cqv, hbufs=cqb[:1])
        if STOP < 4:
            continue
        for m in range(16):
            ea, eb = mk_rope_epis(QR, "mla_QR", m, m * 64, ntok, t0)
            linear_fm(C, Wqr, ntok, ea, mchunks=[m], hv=cqv, hbufs=cqb[:1])
            linear_fm(C, Wqrs, ntok, eb, mchunks=[m], hv=cqv, hbufs=cqb[:1])
        if STOP < 5:
            continue
        linear_fm(C, Wkn, ntok, mk_copy_fm(KN, "mla_KN", ntok, t0), hv=ckv, hbufs=ckb[:1])
        linear(C, Wv, tiles, epi_copy(C, V, "mla_V", BF16), hv=ckv, hbufs=ckb)
    if STOP < 6:
        return
    groups = []
    for hd in range(16):
        groups.append(dict(k_chunks=[(KN, hd * 128, 128, ("mla_KN", hd)), (KR, 0, 64, ("mla_KR", 0))], v=(V, "mla_V", hd * 128, 0),
                           heads=[dict(q_chunks=[(QN, hd * 128, 128, ("mla_QN", hd)), (QR, hd * 64, 64, ("mla_QR", hd))], out=hd)]))
    attn_core(C, groups, 192 ** -0.5, OT, "mla_OT")
    out_proj_fm(C, OT, "mla_OT", 16, Wo, Y)


def ret_mixer(C, l, X, Y, w_in, w_qk_sw, decay, rope_tab, w_out):
    P = C.P
    T = C.NT * 128
    NT, NL = C.NT, C.NL
    allt = list(range(NT))
    Win = cast_weight(C, "ret_in%d" % l, w_in, D, 12288)
    Wsw = cast_weight(C, "ret_sw%d" % l, w_qk_sw, D, 4096)
    Wo = cast_weight(C, "ret_out%d" % l, w_out, 4096, D)
    QK = C.dram("ret_QK", [4096, T], BF16)
    V = C.dram("ret_V", [T, 4096], BF16)
    G = C.dram("ret_G", [4096, T], F32)
    OT = C.dram("ret_OT", [4096, T], BF16)
    mark_proj = C.top
    tab = C.sb("ret_tab", [128, 4, 512], F32)
    t1 = C.sb("ret_t1", [128, 4, 512], F32)
    t2 = C.sb("ret_t2", [128, 512], F32)
    qo = [C.sb("ret_qo%d" % i, [128, 512], BF16) for i in range(2)]

    def mk_epis(ntok, t0):
        def ea(m, bank, bbuf):
            part, sub = m % 2, m % 4
            P.op("dve", lambda h: h.tensor_tensor(out=t1[:, sub, 0:ntok], in0=bank[:, 0:ntok], in1=tab[:, part, 0:ntok], op=ALU.mult),
                 reads=[bbuf, C.b("ret_tab")], writes=[C.b(("ret_t1", sub))])

        def eb(m, bank, bbuf):
            part, sub = m % 2, m % 4
            P.op("dve", lambda h: h.tensor_tensor(out=t2[:, 0:ntok], in0=bank[:, 0:ntok], in1=tab[:, 2 + part, 0:ntok], op=ALU.mult),
                 reads=[bbuf, C.b("ret_tab")], writes=[C.b("ret_t2")])
            qi = C.next("ret_qo", 2)
            P.op("dve", lambda h: h.tensor_tensor(out=qo[qi][:, 0:ntok], in0=t1[:, sub, 0:ntok], in1=t2[:, 0:ntok], op=ALU.add),
                 reads=[C.b(("ret_t1", sub)), C.b("ret_t2")], writes=[C.b(("ret_qo", qi))])
            store_rows(C, QK, "ret_QK", m, m * 128, 128, t0, ntok, qo[qi][:, 0:ntok], C.b(("ret_qo", qi)))
        return ea, eb

    def mk_g(ntok, t0):
        def e(m, bank, bbuf):
            si = C.next("stgf", 4)
            sbuf = C.b(("stgf", si))
            P.op("act", lambda h: h.activation(out=C.stg_f[si][:, 0:ntok], in_=bank[:, 0:ntok], func=AF.Silu), reads=[bbuf], writes=[sbuf])
            store_rows(C, G, "ret_G", m - 64, (m - 64) * 128, 128, t0, ntok, C.stg_f[si][:, 0:ntok], sbuf)
        return e

    for tiles in C.blocks():
        r = 1 if tiles[0] < CTX_T else 0
        ntok = len(tiles) * 128
        t0 = tiles[0]
        load_hT_mod(C, X, "X", tiles, C.vecT[:, 1 * 2 + r, :], C.vecT[:, 0 * 2 + r, :], "vecT")
        for i in range(2):
            for part in range(2):
                P.op("sp", lambda h, i=i, part=part, t0=t0, ntok=ntok: h.dma_start(
                    out=tab[:, i * 2 + part, 0:ntok], in_=rope_tab[i, part * 128:(part + 1) * 128, t0 * 128:t0 * 128 + ntok]),
                    writes=[C.b("ret_tab")], dma="ld")
        ea, eb = mk_epis(ntok, t0)
        for j in range(8):
            ms = list(range(j * 4, j * 4 + 4))
            linear_fm(C, Win, ntok, ea, mchunks=ms)
            linear_fm(C, Wsw, ntok, eb, mchunks=ms)
        ecv = epi_copy(C, V, "ret_V", BF16)
        linear(C, Win, tiles, lambda t, tl, j, bank, bbuf, ecv=ecv: ecv(t, tl, j - 8, bank, bbuf), panels=list(range(8, 16)))
        linear_fm(C, Win, ntok, mk_g(ntok, t0), mchunks=list(range(64, 96)))

    P.barrier()
    C.top = mark_proj
    lg = C.sb("ret_lg", [128, 16], F32)
    P.op("sp", lambda h: h.dma_start(out=lg[:], in_=decay.partition_broadcast(128)), writes=[C.b("ret_lg")], dma="ld")
    NSC = NT + 3
    iota = C.sb("ret_iota", [128, 64], F32)
    cm = C.sb("ret_cm", [128, 4, 128], F32)
    P.op("sp", lambda h: h.dma_start(out=iota[:], in_=C.cst2[:, 512:576]), writes=[C.b("ret_iota")], dma="ld")
    P.op("sp", lambda h: h.dma_start(out=cm[:], in_=C.cst2[:, 0:512].rearrange("p (a b) -> p a b", b=128)), writes=[C.b("ret_cm")], dma="ld")
    SC = C.sb("ret_SC", [128, 16, 64], F32)
    for j in range(16):
        P.op("dve", lambda h, j=j: h.tensor_scalar(out=SC[:, j, :], in0=iota[:], scalar1=lg[:, j:j + 1], scalar2=None, op0=ALU.mult),
             reads=[C.b("ret_iota"), C.b("ret_lg")], writes=[C.b("ret_SC")])
    P.op("act", lambda h: h.activation(out=SC[:], in_=SC[:], func=AF.Exp), reads=[C.b("ret_SC")], writes=[C.b("ret_SC")])
    P.op("dve", lambda h: h.tensor_scalar(out=SC[:], in0=SC[:], scalar1=1.0 / 16, scalar2=None, op0=ALU.mult),
         reads=[C.b("ret_SC")], writes=[C.b("ret_SC")])
    BF = C.sb("ret_BF", [128, 8, 128], F32)
    BB = C.sb("ret_BB", [128, 8, 128], F32)
    DG = C.sb("ret_DG", [128, 8, 128], F32)
    tm = C.sb("ret_tm", [128, 128], F32)
    for hh in range(8):
        P.op("act", lambda h, hh=hh: h.activation(out=BF[:, hh, :], in_=cm[:, 0, :], func=AF.Exp, scale=lg[:, hh:hh + 1]),
             reads=[C.b("ret_cm"), C.b("ret_lg")], writes=[C.b("ret_BF")])
        P.op("act", lambda h, hh=hh: h.activation(out=BB[:, hh, :], in_=cm[:, 1, :], func=AF.Exp, scale=lg[:, 8 + hh:9 + hh]),
             reads=[C.b("ret_cm"), C.b("ret_lg")], writes=[C.b("ret_BB")])
        P.op("dve", lambda h, hh=hh: h.tensor_tensor(out=tm[:], in0=BF[:, hh, :], in1=cm[:, 2, :], op=ALU.mult),
             reads=[C.b("ret_BF"), C.b("ret_cm")], writes=[C.b("ret_tm")])
        P.op("dve", lambda h, hh=hh: h.tensor_tensor(out=DG[:, hh, :], in0=BB[:, hh, :], in1=cm[:, 3, :], op=ALU.mult),
             reads=[C.b("ret_BB"), C.b("ret_cm")], writes=[C.b("ret_DG")])
        P.op("dve", lambda h, hh=hh: h.tensor_tensor(out=DG[:, hh, :], in0=DG[:, hh, :], in1=tm[:], op=ALU.add),
             reads=[C.b("ret_DG"), C.b("ret_tm")], writes=[C.b("ret_DG")])
        P.op("dve", lambda h, hh=hh: h.tensor_scalar(out=DG[:, hh, :], in0=DG[:, hh, :], scalar1=1.0 / 16, scalar2=None, op0=ALU.mult),
             reads=[C.b("ret_DG")], writes=[C.b("ret_DG")])

    kT = C.sb("ret_kT", [128, 2, T], BF16)
    vv = C.sb("ret_v", [128, NT, 512], BF16)
    qT = [C.sb("ret_qT%d" % i, [128, 2, 512], BF16) for i in range(2)]
    pT = [C.sb("ret_pT%d" % i, [128, 512], BF16) for i in range(3)]
    sq = C.sb("ret_sq", [128, 512], BF16)
    ssa = C.sb("ret_ssa", [128, 512], F32)
    gt = [C.sb("ret_gt%d" % i, [128, 512], F32) for i in range(2)]
    oo = [C.sb("ret_oo%d" % i, [128, 512], BF16) for i in range(2)]
    acc = [C.ob, C.db, C.tpf, C.tpbf]
    accb = [C.b("ob"), C.b("db"), C.b("tpf"), C.b("tpb")]
    for hd in range(8):
        for ci in range(2):
            P.op("sp", lambda h, ci=ci, hd=hd: h.dma_start(out=kT[:, ci, :], in_=QK[(16 + hd * 2 + ci) * 128:(17 + hd * 2 + ci) * 128, :]),
                 reads=[C.b(("ret_QK", 16 + hd * 2 + ci, t)) for t in allt], writes=[C.b("ret_kT")], dma="ld")
        P.op("sp", lambda h, hd=hd: h.dma_start(out=vv[:], in_=V[:, hd * 512:(hd + 1) * 512].rearrange("(t p) c -> p t c", p=128)),
             reads=[C.b(("ret_V", t, hd)) for t in allt], writes=[C.b("ret_v")], dma="ld")
        for tiles in C.blocks():
            nq = len(tiles) * 128
            t0 = tiles[0]
            keys = [0, 1] if t0 < CTX_T else allt
            qi = C.next("ret_qT", 2)
            for ci in range(2):
                P.op("sp", lambda h, qi=qi, ci=ci, hd=hd, t0=t0, nq=nq: h.dma_start(
                    out=qT[qi][:, ci, 0:nq], in_=QK[(hd * 2 + ci) * 128:(hd * 2 + ci + 1) * 128, t0 * 128:t0 * 128 + nq]),
                    reads=[C.b(("ret_QK", hd * 2 + ci, t)) for t in tiles], writes=[C.b(("ret_qT", qi))], dma="ld")
            def pv4(ki, kt, pi, pbuf, nq, nk):
                first, lastk = (ki == 0), (ki == nk - 1)
                for j in range(4):
                    P.op("pe", lambda h, j=j: h.matmul(acc[j][:, 0:nq], lhsT=vv[:, kt, j * 128:(j + 1) * 128], rhs=pT[pi][:, 0:nq], start=first, stop=lastk),
                         reads=[C.b("ret_v"), pbuf], writes=[accb[j]])
            prevr = None
            for ki, kt in enumerate(keys):
                bi = C.next("gb", 4)
                for ci in range(2):
                    P.op("pe", lambda h, bi=bi, kt=kt, qi=qi, nq=nq, ci=ci: h.matmul(
                        C.gb[bi][:, 0:nq], lhsT=kT[:, ci, kt * 128:(kt + 1) * 128], rhs=qT[qi][:, ci, 0:nq], start=(ci == 0), stop=(ci == 1)),
                        reads=[C.b("ret_kT"), C.b(("ret_qT", qi))], writes=[C.b(("gb", bi))])
                pi = C.next("ret_pT", 3)
                pbuf = C.b(("ret_pT", pi))
                for ql, qt in enumerate(tiles):
                    cs_ = slice(ql * 128, (ql + 1) * 128)
                    srcp = C.gb[bi][:, cs_]
                    dstp = pT[pi][:, cs_]
                    if kt == qt:
                        P.op("dve", lambda h, srcp=srcp, dstp=dstp, hd=hd: h.tensor_tensor(out=dstp, in0=srcp, in1=DG[:, hd, :], op=ALU.mult),
                             reads=[C.b(("gb", bi)), C.b("ret_DG")], writes=[pbuf])
                    elif kt < CTX_T and qt >= CTX_T:
                        da = qt - kt
                        db_ = NL + kt - qt + 2
                        P.op("dve", lambda h, hd=hd, da=da: h.tensor_scalar(out=tm[:], in0=BF[:, hd, :], scalar1=SC[:, hd, da:da + 1], scalar2=None, op0=ALU.mult),
                             reads=[C.b("ret_BF"), C.b("ret_SC")], writes=[C.b("ret_tm")])
                        P.op("dve", lambda h, hd=hd, db_=db_: h.scalar_tensor_tensor(out=tm[:], in0=BB[:, hd, :], scalar=SC[:, 8 + hd, db_:db_ + 1], in1=tm[:],
                                                                                   op0=ALU.mult, op1=ALU.add),
                             reads=[C.b("ret_BB"), C.b("ret_SC"), C.b("ret_tm")], writes=[C.b("ret_tm")])
                        P.op("dve", lambda h, srcp=srcp, dstp=dstp: h.tensor_tensor(out=dstp, in0=srcp, in1=tm[:], op=ALU.mult),
                             reads=[C.b(("gb", bi)), C.b("ret_tm")], writes=[pbuf])
                    elif kt < qt:
                        dd = qt - kt
                        P.op("dve", lambda h, srcp=srcp, dstp=dstp, hd=hd, dd=dd: h.scalar_tensor_tensor(
                            out=dstp, in0=srcp, scalar=SC[:, hd, dd:dd + 1], in1=BF[:, hd, :], op0=ALU.mult, op1=ALU.mult),
                            reads=[C.b(("gb", bi)), C.b("ret_SC"), C.b("ret_BF")], writes=[pbuf])
                    else:
                        dd = kt - qt
                        P.op("dve", lambda h, srcp=srcp, dstp=dstp, hd=hd, dd=dd: h.scalar_tensor_tensor(
                            out=dstp, in0=srcp, scalar=SC[:, 8 + hd, dd:dd + 1], in1=BB[:, hd, :], op0=ALU.mult, op1=ALU.mult),
                            reads=[C.b(("gb", bi)), C.b("ret_SC"), C.b("ret_BB")], writes=[pbuf])
                if prevr is not None:
                    pv4(*prevr)
                prevr = (ki, kt, pi, pbuf, nq, len(keys))
            pv4(*prevr)
            for j in range(4):
                P.op("act", lambda h, j=j, nq=nq: h.activation(out=sq[:, 0:nq], in_=acc[j][:, 0:nq], func=AF.Square), reads=[accb[j]], writes=[C.b("ret_sq")])
                bi = C.next("gb", 4)
                P.op("pe", lambda h, bi=bi, nq=nq: h.matmul(C.gb[bi][:, 0:nq], lhsT=C.onesb[:], rhs=sq[:, 0:nq], start=True, stop=True),
                     reads=[C.b("ret_sq"), C.b("onesb")], writes=[C.b(("gb", bi))])
                if j == 0:
                    P.op("dve", lambda h, bi=bi, nq=nq: h.tensor_copy(out=ssa[:, 0:nq], in_=C.gb[bi][:, 0:nq]), reads=[C.b(("gb", bi))], writes=[C.b("ret_ssa")])
                else:
                    P.op("dve", lambda h, bi=bi, nq=nq: h.tensor_tensor(out=ssa[:, 0:nq], in0=C.gb[bi][:, 0:nq], in1=ssa[:, 0:nq], op=ALU.add),
                         reads=[C.b(("gb", bi)), C.b("ret_ssa")], writes=[C.b("ret_ssa")])
            P.op("act", lambda h, nq=nq: h.activation(out=ssa[:, 0:nq], in_=ssa[:, 0:nq], func=AF.Sqrt, scale=1.0 / 512, bias=C.epsc[:, 0:1]),
                 reads=[C.b("ret_ssa"), C.b("epsc")], writes=[C.b("ret_ssa")])
            P.op("dve", lambda h, nq=nq: h.reciprocal(out=ssa[:, 0:nq], in_=ssa[:, 0:nq]), reads=[C.b("ret_ssa")], writes=[C.b("ret_ssa")])
            for j in range(4):
                gi = C.next("ret_gt", 2)
                row = (hd * 4 + j) * 128
                P.op("sp", lambda h, gi=gi, row=row, t0=t0, nq=nq: h.dma_start(out=gt[gi][:, 0:nq], in_=G[row:row + 128, t0 * 128:t0 * 128 + nq]),
                     reads=[C.b(("ret_G", hd * 4 + j, t)) for t in tiles], writes=[C.b(("ret_gt", gi))], dma="ld")
                P.op("dve", lambda h, gi=gi, nq=nq: h.tensor_tensor(out=gt[gi][:, 0:nq], in0=gt[gi][:, 0:nq], in1=ssa[:, 0:nq], op=ALU.mult),
                     reads=[C.b(("ret_gt", gi)), C.b("ret_ssa")], writes=[C.b(("ret_gt", gi))])
                oi = C.next("ret_oo", 2)
                P.op("dve", lambda h, gi=gi, oi=oi, j=j, nq=nq: h.tensor_tensor(out=oo[oi][:, 0:nq], in0=acc[j][:, 0:nq], in1=gt[gi][:, 0:nq], op=ALU.mult),
                     reads=[accb[j], C.b(("ret_gt", gi))], writes=[C.b(("ret_oo", oi))])
                store_rows(C, OT, "ret_OT", hd * 4 + j, row, 128, t0, nq, oo[oi][:, 0:nq], C.b(("ret_oo", oi)))
    out_proj_fm(C, OT, "ret_OT", 32, Wo, Y)


def hgrn_mixer(C, l, X, Y, w_in, lb_raw, out_gain, w_out):
    P = C.P
    T = C.NT * 128
    NT, NL = C.NT, C.NL
    Win = cast_weight(C, "hg_in%d" % l, w_in, D, 10240)
    Wo = cast_weight(C, "hg_out%d" % l, w_out, D, D)
    QS = C.dram("hg_QS", [D, T], F32)
    KF = [C.dram("hg_K%d" % d_, [D, T], F32) for d_ in range(2)]
    LF = [C.dram("hg_LF%d" % d_, [D, T], F32) for d_ in range(2)]
    GS = C.dram("hg_GS", [D, T], F32)
    IV = C.dram("hg_I", [T, D], BF16)
    OF = C.dram("hg_OF", [D, T], F32)
    OT = C.dram("hg_OT", [D, T], BF16)
    lr = C.sb("hg_lr", [128, 4, KC], F32)
    lbv = C.sb("hg_lb", [128, 4, KC], F32)
    for j in range(4):
        P.op("sp", lambda h, j=j: h.dma_start(out=lr[:, j, :], in_=lb_raw[j].rearrange("(k p) -> p k", p=128), allow_slow_non_contiguous=True),
             writes=[C.b("hg_lr")], dma="ld")
    P.op("act", lambda h: h.activation(out=lr[:], in_=lr[:], func=AF.Exp), reads=[C.b("hg_lr")], writes=[C.b("hg_lr")])
    P.op("dve", lambda h: h.tensor_tensor(out=lbv[:, 2, :], in0=lr[:, 0, :], in1=lr[:, 1, :], op=ALU.add), reads=[C.b("hg_lr")], writes=[C.b("hg_lb")])
    P.op("dve", lambda h: h.tensor_tensor(out=lbv[:, 3, :], in0=lr[:, 2, :], in1=lr[:, 3, :], op=ALU.add), reads=[C.b("hg_lr")], writes=[C.b("hg_lb")])
    P.op("dve", lambda h: h.tensor_tensor(out=lbv[:, 2, :], in0=lbv[:, 2, :], in1=lbv[:, 3, :], op=ALU.add), reads=[C.b("hg_lb")], writes=[C.b("hg_lb")])
    P.op("dve", lambda h: h.reciprocal(out=lbv[:, 2, :], in_=lbv[:, 2, :]), reads=[C.b("hg_lb")], writes=[C.b("hg_lb")])
    P.op("dve", lambda h: h.memset(lbv[:, 0, :], 0.0), reads=[C.b("hg_lb")], writes=[C.b("hg_lb")])
    for j in range(1, l + 1):
        P.op("dve", lambda h, j=j: h.tensor_tensor(out=lbv[:, 0, :], in0=lbv[:, 0, :], in1=lr[:, j, :], op=ALU.add),
             reads=[C.b("hg_lb"), C.b("hg_lr")], writes=[C.b("hg_lb")])
    P.op("dve", lambda h: h.tensor_tensor(out=lbv[:, 0, :], in0=lbv[:, 0, :], in1=lbv[:, 2, :], op=ALU.mult), reads=[C.b("hg_lb")], writes=[C.b("hg_lb")])
    P.op("dve", lambda h: h.tensor_scalar(out=lbv[:, 1, :], in0=lbv[:, 0, :], scalar1=-1.0, scalar2=1.0, op0=ALU.mult, op1=ALU.add),
         reads=[C.b("hg_lb")], writes=[C.b("hg_lb")])

    def mk_act_store(dst, dname, func, m0, ntok, t0):
        def e(m, bank, bbuf):
            si = C.next("stgf", 4)
            sbuf = C.b(("stgf", si))
            P.op("act", lambda h: h.activation(out=C.stg_f[si][:, 0:ntok], in_=bank[:, 0:ntok], func=func), reads=[bbuf], writes=[sbuf])
            store_rows(C, dst, dname, m - m0, (m - m0) * 128, 128, t0, ntok, C.stg_f[si][:, 0:ntok], sbuf)
        return e

    mark_proj = C.top
    fg = C.sb("hg_fg", [128, 512], F32)
    kk = [C.sb("hg_kk%d" % i, [128, 512], F32) for i in range(2)]
    lff = [C.sb("hg_lff%d" % i, [128, 512], F32) for i in range(2)]

    def mk_forget(d_, m0, ntok, t0):
        def e(m, bank, bbuf):
            mm = m - m0
            P.op("act", lambda h: h.activation(out=fg[:, 0:ntok], in_=bank[:, 0:ntok], func=AF.Sigmoid), reads=[bbuf], writes=[C.b("hg_fg")])
            P.op("dve", lambda h: h.tensor_scalar(out=fg[:, 0:ntok], in0=fg[:, 0:ntok], scalar1=lbv[:, 1, mm:mm + 1], scalar2=lbv[:, 0, mm:mm + 1],
                                                  op0=ALU.mult, op1=ALU.add),
                 reads=[C.b("hg_fg"), C.b("hg_lb")], writes=[C.b("hg_fg")])
            i1 = C.next("hg_kk", 2)
            P.op("dve", lambda h: h.tensor_scalar(out=kk[i1][:, 0:ntok], in0=fg[:, 0:ntok], scalar1=-1.0, scalar2=1.0, op0=ALU.mult, op1=ALU.add),
                 reads=[C.b("hg_fg")], writes=[C.b(("hg_kk", i1))])
            store_rows(C, KF[d_], "hg_K%d" % d_, mm, mm * 128, 128, t0, ntok, kk[i1][:, 0:ntok], C.b(("hg_kk", i1)))
            i2 = C.next("hg_lff", 2)
            P.op("act", lambda h: h.activation(out=lff[i2][:, 0:ntok], in_=fg[:, 0:ntok], func=AF.Ln), reads=[C.b("hg_fg")], writes=[C.b(("hg_lff", i2))])
            store_rows(C, LF[d_], "hg_LF%d" % d_, mm, mm * 128, 128, t0, ntok, lff[i2][:, 0:ntok], C.b(("hg_lff", i2)))
        return e

    for tiles in C.blocks():
        r = 1 if tiles[0] < CTX_T else 0
        ntok = len(tiles) * 128
        t0 = tiles[0]
        load_hT_mod(C, X, "X", tiles, C.vecT[:, 1 * 2 + r, :], C.vecT[:, 0 * 2 + r, :], "vecT")
        linear_fm(C, Win, ntok, mk_act_store(QS, "hg_QS", AF.Silu, 0, ntok, t0), mchunks=list(range(0, 16)))
        linear_fm(C, Win, ntok, mk_forget(0, 16, ntok, t0), mchunks=list(range(16, 32)))
        linear_fm(C, Win, ntok, mk_forget(1, 32, ntok, t0), mchunks=list(range(32, 48)))
        eci = epi_copy(C, IV, "hg_I", BF16)
        linear(C, Win, tiles, lambda t, tl, j, bank, bbuf, eci=eci: eci(t, tl, j - 12, bank, bbuf), panels=list(range(12, 16)))
        linear_fm(C, Win, ntok, mk_act_store(GS, "hg_GS", AF.Silu, 64, ntok, t0), mchunks=list(range(64, 80)))

    P.barrier()
    C.top = mark_proj
    W_ = 16 * 128
    rm = C.sb("hg_rm", [128, W_], F32)
    msk = C.sb("hg_msk", [128, 2, 128], F32)
    gv = C.sb("hg_gv", [128, 1], F32)
    P.op("sp", lambda h: h.dma_start(out=rm[:], in_=C.cst3[:, 0:W_]), writes=[C.b("hg_rm")], dma="ld")
    P.op("sp", lambda h: h.dma_start(out=msk[:], in_=C.cst3[:, W_:W_ + 256].rearrange("p (a b) -> p a b", b=128)), writes=[C.b("hg_msk")], dma="ld")
    P.op("sp", lambda h: h.dma_start(out=gv[:], in_=out_gain.rearrange("a p -> p a"), allow_slow_non_contiguous=True), writes=[C.b("hg_gv")], dma="ld")
    qt_ = C.sb("hg_q", [128, W_], F32)
    kt_ = C.sb("hg_k", [128, W_], F32)
    lt_ = C.sb("hg_l", [128, W_], F32)
    cf = C.sb("hg_cf", [128, W_], F32)
    tS = C.xinb[0][:, 0:2 * W_].bitcast(F32)
    ex = C.sb("hg_ex", [128, W_], F32)
    ebl = C.sb("hg_ebl", [128, 32], F32)
    qin = C.sb("hg_qin", [128, W_], BF16)
    kout = C.sb("hg_kout", [128, W_], BF16)
    kd = C.sb("hg_kd", [128, W_], BF16)
    kdT = [C.sb("hg_kdT%d" % i, [128, 128], BF16) for i in range(2)]
    pm = [C.sb("hg_pm%d" % i, [128, 128], BF16) for i in range(2)]
    iv = C.sb("hg_iv", [128, D], BF16)
    St = C.sb("hg_S", [128, 16, 128], F32)
    Sb = C.sb("hg_Sb", [128, 16, 128], BF16)
    oall = C.sa[:].rearrange("p a b -> p (a b)")
    ofl, osq, gsl, ogb = lt_, kout, qt_, qin
    _alias = {"hg_ofl": "hg_l", "hg_osq": "hg_kout", "hg_gsl": "hg_q", "hg_ogb": "hg_qin", "hg_b": "hg_cf"}
    B = lambda k_: C.b(_alias.get(k_, k_) if isinstance(k_, str) else k_)
    c3 = lambda t_: t_[:].rearrange("p (a b) -> p a b", b=64)

    def fm_tile_ap(dr, t):
        return dr[:, t * 128:(t + 1) * 128].rearrange("(h p) t -> p h t", p=128)

    for d_ in range(2):
        P.op("dve", lambda h: h.memset(St[:], 0.0), reads=[B("hg_S")], writes=[B("hg_S")])
        P.op("dve", lambda h: h.memset(Sb[:], 0.0), reads=[B("hg_Sb")], writes=[B("hg_Sb")])
        order = list(range(NT)) if d_ == 0 else [1, 0] + list(range(NT - 1, CTX_T - 1, -1))
        for t in order:
            allm = list(range(16))
            P.op("sp", lambda h, t=t: h.dma_start(out=qt_[:].rearrange("p (h t) -> p h t", t=128), in_=fm_tile_ap(QS, t)),
                 reads=[B(("hg_QS", m, t)) for m in allm], writes=[B("hg_q")], dma="ld")
            P.op("sp", lambda h, t=t, d_=d_: h.dma_start(out=kt_[:].rearrange("p (h t) -> p h t", t=128), in_=fm_tile_ap(KF[d_], t)),
                 reads=[B(("hg_K%d" % d_, m, t)) for m in allm], writes=[B("hg_k")], dma="ld")
            P.op("sp", lambda h, t=t, d_=d_: h.dma_start(out=lt_[:].rearrange("p (h t) -> p h t", t=128), in_=fm_tile_ap(LF[d_], t)),
                 reads=[B(("hg_LF%d" % d_, m, t)) for m in allm], writes=[B("hg_l")], dma="ld")
            P.op("sp", lambda h, t=t: h.dma_start(out=iv[:], in_=IV[t * 128:(t + 1) * 128, :]),
                 reads=dbufs(C, "hg_I", t, 0, D), writes=[B("hg_iv")], dma="ld")
            P.op("dve", lambda h: h.tensor_tensor_scan(out=cf[:], data0=rm[:], data1=lt_[:], initial=0.0, op0=ALU.mult, op1=ALU.add),
                 reads=[B("hg_rm"), B("hg_l")], writes=[B("hg_cf")])
            P.op("dve", lambda h: h.tensor_copy(out=c3(tS), in_=c3(cf)[:, :, 63:64].to_broadcast([128, 32, 64])),
                 reads=[B("hg_cf")], writes=[B("hg_tS")])
            if d_ == 0:
                bsrc, bbuf_ = cf, B("hg_cf")
            else:
                P.op("dve", lambda h: h.tensor_tensor(out=cf[:], in0=lt_[:], in1=cf[:], op=ALU.subtract), reads=[B("hg_l"), B("hg_cf"), B("hg_tS")], writes=[B("hg_cf")])
                P.op("dve", lambda h: h.tensor_tensor(out=cf[:], in0=cf[:], in1=tS[:], op=ALU.add), reads=[B("hg_cf"), B("hg_tS")], writes=[B("hg_cf")])
                bsrc, bbuf_ = cf, B("hg_cf")
            P.op("act", lambda h, bsrc=bsrc: h.activation(out=ex[:], in_=bsrc[:], func=AF.Exp), reads=[bbuf_], writes=[B("hg_ex")])
            P.op("dve", lambda h: h.tensor_tensor(out=qin[:], in0=qt_[:], in1=ex[:], op=ALU.mult), reads=[B("hg_q"), B("hg_ex")], writes=[B("hg_qin")])
            P.op("act", lambda h, bsrc=bsrc: h.activation(out=ex[:], in_=bsrc[:], func=AF.Exp, scale=-1.0), reads=[bbuf_, B("hg_qin")], writes=[B("hg_ex")])
            P.op("dve", lambda h: h.tensor_tensor(out=kout[:], in0=kt_[:], in1=ex[:], op=ALU.mult), reads=[B("hg_k"), B("hg_ex")], writes=[B("hg_kout")])
            P.op("dve", lambda h, bsrc=bsrc: h.tensor_tensor(out=ex[:], in0=tS[:], in1=bsrc[:], op=ALU.subtract), reads=[B("hg_tS"), bbuf_, B("hg_kout")], writes=[B("hg_ex")])
            P.op("act", lambda h: h.activation(out=ex[:], in_=ex[:], func=AF.Exp), reads=[B("hg_ex")], writes=[B("hg_ex")])
            P.op("dve", lambda h: h.tensor_tensor(out=kd[:], in0=kt_[:], in1=ex[:], op=ALU.mult), reads=[B("hg_k"), B("hg_ex")], writes=[B("hg_kd")])
            P.op("act", lambda h: h.activation(out=ebl[:], in_=c3(tS)[:, :, 0], func=AF.Exp), reads=[B("hg_tS")], writes=[B("hg_ebl")])
            chunks = [0, 1] if d_ == 0 else [1, 0]
            for hd in range(16):
                hs = slice(hd * 128, (hd + 1) * 128)
                b1 = C.next("gb", 4)
                P.op("pe", lambda h, b1=b1, hs=hs: h.matmul(C.gb[b1][:, 0:128], lhsT=kout[:, hs], rhs=qin[:, hs], start=True, stop=True),
                     reads=[B("hg_kout"), B("hg_qin")], writes=[B(("gb", b1))])
                pi = C.next("hg_pm", 2)
                P.op("dve", lambda h, b1=b1, pi=pi, d_=d_: h.tensor_tensor(out=pm[pi][:], in0=C.gb[b1][:, 0:128], in1=msk[:, d_, :], op=ALU.mult),
                     reads=[B(("gb", b1)), B("hg_msk")], writes=[B(("hg_pm", pi))])
                P.op("pe", lambda h, hs=hs: h.transpose(C.tpb[:, 0:128], kd[:, hs], C.identb[:]), reads=[B("hg_kd"), B("identb")], writes=[B("tpb")])
                ki = C.next("hg_kdT", 2)
                P.op("act", lambda h, ki=ki: h.activation(out=kdT[ki][:], in_=C.tpb[:, 0:128], func=AF.Copy), reads=[B("tpb")], writes=[B(("hg_kdT", ki))])
                b2 = C.next("gb", 4)
                for ch in chunks:
                    cs_ = slice(ch * 64, ch * 64 + 64)
                    hcs = slice(hd * 128 + ch * 64, hd * 128 + ch * 64 + 64)
                    P.op("pe", lambda h, b2=b2, cs_=cs_, hcs=hcs, hd=hd: h.matmul(C.gb[b2][:, cs_], lhsT=Sb[:, hd, :], rhs=qin[:, hcs], start=True, stop=False),
                         reads=[B(("hg_Sb", hd)), B("hg_qin")], writes=[B(("gb", b2))])
                    P.op("pe", lambda h, b2=b2, cs_=cs_, pi=pi, hs=hs: h.matmul(C.gb[b2][:, cs_], lhsT=iv[:, hs], rhs=pm[pi][:, cs_], start=False, stop=True),
                         reads=[B("hg_iv"), B(("hg_pm", pi))], writes=[B(("gb", b2))])
                    b3 = C.next("gb", 4)
                    P.op("pe", lambda h, b3=b3, cs_=cs_, ki=ki, hs=hs: h.matmul(C.gb[b3][:, 0:128], lhsT=kdT[ki][cs_, :], rhs=iv[cs_, hs], start=True, stop=True),
                         reads=[B(("hg_kdT", ki)), B("hg_iv")], writes=[B(("gb", b3))])
                    ci = hd * 2 + ch
                    P.op("dve", lambda h, b3=b3, hd=hd, ci=ci: h.scalar_tensor_tensor(out=St[:, hd, :], in0=St[:, hd, :], scalar=ebl[:, ci:ci + 1], in1=C.gb[b3][:, 0:128],
                                                                                 op0=ALU.mult, op1=ALU.add),
                         reads=[B(("hg_S", hd)), B("hg_ebl"), B(("gb", b3))], writes=[B(("hg_S", hd))])
                    P.op("act", lambda h, hd=hd: h.activation(out=Sb[:, hd, :], in_=St[:, hd, :], func=AF.Copy),
                         reads=[B(("hg_S", hd))], writes=[B(("hg_Sb", hd))])
                evac(C, oall[:, hs], C.gb[b2][:, 0:128], [B(("gb", b2))], [B("hg_oall")])
            if d_ == 0:
                P.op("sp", lambda h, t=t: h.dma_start(out=fm_tile_ap(OF, t), in_=oall[:].rearrange("p (h t) -> p h t", t=128)),
                     reads=[B("hg_oall")], writes=[B(("hg_OF", t))], dma="st")
            else:
                P.op("sp", lambda h, t=t: h.dma_start(out=ofl[:].rearrange("p (h t) -> p h t", t=128), in_=fm_tile_ap(OF, t)),
                     reads=[B(("hg_OF", t))], writes=[B("hg_ofl")], dma="ld")
                P.op("sp", lambda h, t=t: h.dma_start(out=gsl[:].rearrange("p (h t) -> p h t", t=128), in_=fm_tile_ap(GS, t)),
                     reads=[B(("hg_GS", m, t)) for m in allm], writes=[B("hg_gsl")], dma="ld")
                P.op("dve", lambda h: h.tensor_tensor(out=oall[:], in0=oall[:], in1=ofl[:], op=ALU.add), reads=[B("hg_oall"), B("hg_ofl")], writes=[B("hg_oall")])
                P.op("act", lambda h: h.activation(out=osq[:], in_=oall[:], func=AF.Square), reads=[B("hg_oall")], writes=[B("hg_osq")])
                for q4 in range(4):
                    b4 = C.next("gb", 4)
                    P.op("pe", lambda h, b4=b4, q4=q4: h.matmul(C.gb[b4][:, :], lhsT=C.onesb[:], rhs=osq[:, q4 * 512:(q4 + 1) * 512], start=True, stop=True),
                         reads=[B("hg_osq"), B("onesb")], writes=[B(("gb", b4))])
                    P.op("act", lambda h, b4=b4, q4=q4: h.activation(out=ofl[:, q4 * 512:(q4 + 1) * 512], in_=C.gb[b4][:, :], func=AF.Sqrt, scale=1.0 / 128, bias=C.epsc[:, 0:1]),
                         reads=[B(("gb", b4)), B("epsc"), B("hg_ofl")], writes=[B("hg_ofl")])
                P.op("dve", lambda h: h.reciprocal(out=ofl[:], in_=ofl[:]), reads=[B("hg_ofl")], writes=[B("hg_ofl")])
                P.op("dve", lambda h: h.scalar_tensor_tensor(out=oall[:], in0=oall[:], scalar=gv[:, 0:1], in1=ofl[:], op0=ALU.mult, op1=ALU.mult),
                     reads=[B("hg_oall"), B("hg_gv"), B("hg_ofl")], writes=[B("hg_oall")])
                P.op("dve", lambda h: h.tensor_tensor(out=ogb[:], in0=oall[:], in1=gsl[:], op=ALU.mult), reads=[B("hg_oall"), B("hg_gsl")], writes=[B("hg_ogb")])
                P.op("sp", lambda h, t=t: h.dma_start(out=fm_tile_ap(OT, t), in_=ogb[:].rearrange("p (h t) -> p h t", t=128)),
                     reads=[B("hg_ogb")], writes=[B(("hg_OT", m, t)) for m in allm], dma="st")
    out_proj_fm(C, OT, "hg_OT", 16, Wo, Y)


def build(n_lat_tiles, layers, dbg=None):
    nc = bass.Bass("TRN2", target_bir_lowering=False)
    NL = n_lat_tiles
    T = (CTX_T + NL) * 128
    C = Ctx(nc, NL)
    P = C.P
    ein = lambda name, shape: nc.dram_tensor(name, shape, F32, kind="ExternalInput").ap()
    x_in = ein("x", [NL * 128, D])
    ctx_in = ein("ctx", [256, D])
    cvec = ein("cvec", [2, D])
    ada_w = ein("ada_w", [DEPTH, D, 6 * D])
    ada_b = ein("ada_b", [DEPTH, 6 * D])
    ln_g = ein("ln_g", [DEPTH, 2, D])
    ln_b = ein("ln_b", [DEPTH, 2, D])
    ffn_w_in = ein("ffn_w_in", [DEPTH, D, 2 * FFN_H])
    ffn_w_out = ein("ffn_w_out", [DEPTH, FFN_H, D])
    out = nc.dram_tensor("out", [NL * 128, D], F32, kind="ExternalOutput").ap()
    mixset = set(l % 4 for l in layers)
    C.cst2 = nc.dram_tensor("cst2", [128, 576], F32, kind="ExternalInput").ap()
    C.cst3 = nc.dram_tensor("cst3", [128, 2048 + 256], F32, kind="ExternalInput").ap()
    ein0 = ein
    ein_h = ein if 3 in mixset else (lambda name, shape: None)
    hg_in = {k: ein_h("hg_" + k, shp) for k, shp in (("w_in", [D, 10240]), ("lb_raw", [4, D]), ("gain", [1, 128]), ("w_out", [D, D]))}
    ein_r = ein if 0 in mixset else (lambda name, shape: None)
    ret_in = {k: ein_r("ret_" + k, shp) for k, shp in (("w_in", [D, 12288]), ("w_sw", [D, 4096]), ("decay", [1, 16]), ("rope", [2, 256, T]), ("w_out", [4096, D]))}
    if 1 not in mixset:
        ein = lambda name, shape: None
    gqa_w_in = ein("gqa_w_in", [D, 3072])
    gqa_w_sw = ein("gqa_w_sw", [D, 2560])
    gqa_gain = ein("gqa_gain", [4, 128])
    gqa_rope = ein("gqa_rope", [2, 128, T])
    gqa_w_out = ein("gqa_w_out", [D, D])
    ein = (lambda name, shape: nc.dram_tensor(name, shape, F32, kind="ExternalInput").ap()) if 2 in mixset else (lambda name, shape: None)
    mla_in = {k: ein("mla_" + k, shp) for k, shp in (("w_a", [D, 1024]), ("w_kr", [D, 64]), ("w_kr_sw", [D, 64]), ("w_qn", [512, 2048]),
                                                    ("w_qr", [512, 1024]), ("w_qr_sw", [512, 1024]), ("w_kn", [512, 2048]), ("w_v", [512, 2048]),
                                                    ("norms", [2, 512]), ("rope", [2, 64, T]), ("w_out", [D, D]))}

    C.cst = nc.dram_tensor("cst", [128, 258], F32, kind="ExternalInput").ap()
    setup_common(C)

    X = C.dram("X", [T, D], F32)
    Y = C.dram("Y", [T, D], F32)
    Y2 = C.dram("Y2", [T, D], F32)
    U = C.dram("U", [T, FFN_H], BF16)
    for t in range(C.NT):
        src = ctx_in[t * 128:(t + 1) * 128, :] if t < CTX_T else x_in[(t - CTX_T) * 128:(t - CTX_T + 1) * 128, :]
        P.op("sp", lambda h, t=t, src=src: h.dma_start(out=X[t * 128:(t + 1) * 128, :], in_=src),
             writes=dbufs(C, "X", t, 0, D), dma="st")

    adaln_all(C, cvec, ada_w, ada_b, layers)
    adaln_layer(C, layers[0])
    if dbg and dbg.get("stop_after_adaln"):
        o = nc.dram_tensor("dbg_MOD", [DEPTH, 2, 6 * D], F32, kind="ExternalOutput").ap()
        P.op("sp", lambda h: h.dma_start(out=o, in_=C.MOD), reads=[C.b(("MOD", l)) for l in layers], dma="st")
        o2 = nc.dram_tensor("dbg_csT", [128, KC * 2], BF16, kind="ExternalOutput").ap()
        P.op("sp", lambda h: h.dma_start(out=o2, in_=C.csT[:].rearrange("p k c -> p (k c)")), reads=[C.b("csT")], dma="st")
        P.emit()
        C.es.close()
        return nc, C

    for li, l in enumerate(layers):
        last = (li == len(layers) - 1)
        load_layer_vectors(C, l, ln_g, ln_b)
        mix = (l % 4) if (dbg is None or dbg.get("mixers", True)) else None
        with C.scope():
            if mix == 0:
                R_ = ret_in
                ret_mixer(C, l, X, Y, R_["w_in"], R_["w_sw"], R_["decay"], R_["rope"], R_["w_out"])
            if mix == 3:
                H_ = hg_in
                hgrn_mixer(C, l, X, Y, H_["w_in"], H_["lb_raw"], H_["gain"], H_["w_out"])
            if mix == 1:
                gqa_mixer(C, l, X, Y, gqa_w_in, gqa_w_sw, gqa_gain, gqa_rope, gqa_w_out)
            if mix == 2:
                M = mla_in
                mla_mixer(C, l, X, Y, M["w_a"], M["w_kr"], M["w_kr_sw"], M["w_qn"], M["w_qr"], M["w_qr_sw"], M["w_kn"], M["w_v"], M["norms"], M["rope"], M["w_out"])
        if mix in (0, 1, 2, 3):
            with C.scope():
                load_bcast(C, l, 0, ln_g, ln_b)
                resid_ln(C, X, "X", Y, "Y", X, "X")
        Wi = cast_weight(C, "ffi%d" % l, ffn_w_in[l], D, 2 * FFN_H)
        Wo = cast_weight(C, "ffo%d" % l, ffn_w_out[l], FFN_H, D)
        if not last:
            adaln_layer(C, layers[li + 1])
        ffn(C, l, X, "X", Wi, Wo, U, Y2)
        with C.scope():
            load_bcast(C, l, 1, ln_g, ln_b)
            if last:
                resid_ln(C, X, "X", Y2, "Y2", out, "out", out_row0=0)
            else:
                resid_ln(C, X, "X", Y2, "Y2", X, "X")
    if dbg and dbg.get("dump"):
        for nm, (ap, shape, dt) in dict(MOD=(C.MOD, [DEPTH, 2, 6 * D], F32), U=(U, [T, FFN_H], BF16), Y2=(Y2, [T, D], F32),
                                        Y=(Y, [T, D], F32), X=(X, [T, D], F32)).items():
            if nm in dbg["dump"]:
                o = nc.dram_tensor("dbg_" + nm, shape, dt, kind="ExternalOutput").ap()
                allb = [b_ for k_, b_ in P.bufs.items() if isinstance(k_, tuple) and k_[0] == nm]
                P.op("sp", lambda h, o=o, ap=ap: h.dma_start(out=o, in_=ap), reads=allb, dma="st")
    P.emit()
    C.es.close()
    return nc, C


def make_consts2():
    c = np.zeros((128, 576), np.float32)
    s_ = np.arange(128)[:, None].astype(np.float32)
    c_ = np.arange(128)[None, :].astype(np.float32)
    c[:, 0:128] = c_ - s_
    c[:, 128:256] = s_ - c_
    c[:, 256:384] = (c_ >= s_)
    c[:, 384:512] = (c_ <= s_)
    c[:, 512:576] = 128.0 * np.arange(64)[None, :]
    return c


def ret_host_inputs(ret_w_in, ret_decay_fwd, ret_decay_bwd, ret_w_out, nl_tiles):
    w = ret_w_in[0]
    return {"ret_w_in": np.ascontiguousarray(w), "ret_w_sw": swap_halves_cols(w[:, :4096], 128, 64),
            "ret_decay": np.concatenate([ret_decay_fwd[0], ret_decay_bwd[0]])[None].astype(np.float32),
            "ret_rope": rope_tables(128, nl_tiles), "ret_w_out": np.ascontiguousarray(ret_w_out[0])}


def make_consts3():
    c = np.zeros((128, 2048 + 256), np.float32)
    col = np.arange(2048)
    c[:, 0:2048] = (col % 64 != 0).astype(np.float32)[None, :]
    s_ = np.arange(128)[:, None]
    c_ = np.arange(128)[None, :]
    same = (s_ // 64) == (c_ // 64)
    c[:, 2048:2176] = (same & (c_ >= s_))
    c[:, 2176:2304] = (same & (c_ <= s_))
    return c


def hgrn_host_inputs(hgrn_w_in, hgrn_lb_raw, hgrn_out_norm, hgrn_w_out):
    return {"hg_w_in": np.ascontiguousarray(hgrn_w_in[0]), "hg_lb_raw": np.ascontiguousarray(hgrn_lb_raw),
            "hg_gain": np.ascontiguousarray(hgrn_out_norm[0:1]), "hg_w_out": np.ascontiguousarray(hgrn_w_out[0])}


def make_consts():
    c = np.zeros((128, 258), np.float32)
    c[:, 0:128] = np.eye(128, dtype=np.float32)
    c[:, 128:256] = 1.0
    c[:, 256] = EPS
    return c


def rope_tables(half, nl_tiles, d_chunk=128):
    q = half // 2
    T = (CTX_T + nl_tiles) * 128
    pos = np.arange(nl_tiles * 128)
    row, col = pos // GRID_W, pos % GRID_W
    freqs = THETA ** (-np.arange(0, half, 2, dtype=np.float32) / half)
    cos = np.ones((4 * q, T), np.float32)
    sin = np.zeros((4 * q, T), np.float32)
    a_row = (row[None, :].astype(np.float32) * freqs[:, None]).astype(np.float32)
    a_col = (col[None, :].astype(np.float32) * freqs[:, None]).astype(np.float32)
    L0 = CTX_T * 128
    for blk, ang in ((0, a_row), (1, a_col)):
        c, s_ = np.cos(ang), np.sin(ang)
        cos[blk * 2 * q:blk * 2 * q + q, L0:] = c
        cos[blk * 2 * q + q:blk * 2 * q + 2 * q, L0:] = c
        sin[blk * 2 * q:blk * 2 * q + q, L0:] = -s_
        sin[blk * 2 * q + q:blk * 2 * q + 2 * q, L0:] = s_
    return np.stack([cos, sin]).astype(np.float32)


def swap_halves_cols(w, d, q):
    n = w.shape[-1]
    idx = np.arange(n)
    within = idx % d
    partner = np.where((within // q) % 2 == 0, idx + q, idx - q)
    return np.ascontiguousarray(w[..., partner])


N_LAT_TILES = 32
_NC_CACHE = {}


def kernel(x, c, ctx, c_ctx, ada_w, ada_b, ln_g, ln_b, ffn_w_in, ffn_w_out,
           ret_w_in, ret_decay_fwd, ret_decay_bwd, ret_w_out,
           gqa_w_in, gqa_q_norm, gqa_k_norm, gqa_w_out,
           mla_w_in, mla_q_norm, mla_w_q_up, mla_kv_norm, mla_w_kv_up, mla_w_out,
           hgrn_w_in, hgrn_lb_raw, hgrn_out_norm, hgrn_w_out):
    f = lambda a: np.ascontiguousarray(np.asarray(a, dtype=np.float32))
    x, c, ctx, c_ctx = f(x), f(c), f(ctx), f(c_ctx)
    if "nc" not in _NC_CACHE:
        _NC_CACHE["nc"] = build(N_LAT_TILES, [0, 1, 2, 3])[0]
    nc = _NC_CACHE["nc"]
    gw = f(gqa_w_in)[0]
    qg, kg = f(gqa_q_norm)[0], f(gqa_k_norm)[0]
    sw = lambda v: swap_halves_cols(v, 128, 32)
    shared = {"cst": make_consts(), "cst2": make_consts2(), "cst3": make_consts3(),
              "ada_w": f(ada_w), "ada_b": f(ada_b), "ln_g": f(ln_g), "ln_b": f(ln_b),
              "ffn_w_in": f(ffn_w_in), "ffn_w_out": f(ffn_w_out),
              "gqa_w_in": gw, "gqa_w_sw": sw(gw[:, :2560]), "gqa_gain": np.stack([qg, sw(qg), kg, sw(kg)]),
              "gqa_rope": rope_tables(64, N_LAT_TILES), "gqa_w_out": f(gqa_w_out)[0]}
    shared.update(ret_host_inputs(f(ret_w_in), f(ret_decay_fwd), f(ret_decay_bwd), f(ret_w_out), N_LAT_TILES))
    shared.update(mla_host_inputs(f(mla_w_in), f(mla_q_norm), f(mla_w_q_up), f(mla_kv_norm), f(mla_w_kv_up), f(mla_w_out), N_LAT_TILES))
    shared.update(hgrn_host_inputs(f(hgrn_w_in), f(hgrn_lb_raw), f(hgrn_out_norm), f(hgrn_w_out)))
    work = {0: 0, 1: 1, 4: 2, 5: 3}
    zeros = {k: np.zeros_like(v) for k, v in shared.items()}
    zx, zc, zv = np.zeros_like(x[0]), np.zeros_like(ctx[0]), np.zeros((2, D), np.float32)
    in_maps = []
    for core in range(8):
        if core in work:
            b = work[core]
            m = dict(shared)
            m["x"], m["ctx"], m["cvec"] = x[b], ctx[b], np.stack([c[b], c_ctx])
        else:
            m = dict(zeros)
            m["x"], m["ctx"], m["cvec"] = zx, zc, zv
        in_maps.append(m)
    res = run_bass_kernel_spmd(nc, in_maps, core_ids=list(range(8)))
    return np.stack([res.results[core]["out"] for core in (0, 1, 4, 5)]).astype(np.float32)


def mla_host_inputs(mla_w_in, mla_q_norm, mla_w_q_up, mla_kv_norm, mla_w_kv_up, mla_w_out, nl_tiles):
    w_in, wq, wkv = mla_w_in[0], mla_w_q_up[0], mla_w_kv_up[0]
    qh = wq.reshape(512, 16, 192)
    kvh = wkv.reshape(512, 16, 256)
    w_kr = np.ascontiguousarray(w_in[:, 1024:1088])
    w_qr = np.ascontiguousarray(qh[:, :, 128:].reshape(512, 1024))
    return {"mla_w_a": np.ascontiguousarray(w_in[:, :1024]), "mla_w_kr": w_kr, "mla_w_kr_sw": swap_halves_cols(w_kr, 64, 16),
            "mla_w_qn": np.ascontiguousarray(qh[:, :, :128].reshape(512, 2048)), "mla_w_qr": w_qr, "mla_w_qr_sw": swap_halves_cols(w_qr, 64, 16),
            "mla_w_kn": np.ascontiguousarray(kvh[:, :, :128].reshape(512, 2048)), "mla_w_v": np.ascontiguousarray(kvh[:, :, 128:].reshape(512, 2048)),
            "mla_norms": np.stack([mla_q_norm[0], mla_kv_norm[0]]), "mla_rope": rope_tables(32, nl_tiles), "mla_w_out": np.ascontiguousarray(mla_w_out[0])}
```

```python
import numpy as np
from contextlib import ExitStack
import concourse.bass as bass
import concourse.mybir as mybir
from concourse.alu_op_type import AluOpType as ALU
from concourse.bass_utils import run_bass_kernel_spmd

AF = mybir.ActivationFunctionType
F32 = mybir.dt.float32
BF16 = mybir.dt.bfloat16
AX = mybir.AxisListType

EPOCH = 32000
DMA_RING = 8
ARENA_BYTES = 207 * 1024
D = 2048
KC = 16
FFN_H = 5632
CTX_T = 2
GRID_W = 64
EPS = 1e-6
DEPTH = 4
ALPHA = (2 * DEPTH) ** 0.25
THETA = 10000.0


class Buf:
    __slots__ = ("key", "last_w", "readers")

    def __init__(self, key):
        self.key = key
        self.last_w = None
        self.readers = {}


class Op:
    __slots__ = ("idx", "eng", "fn", "deps", "stream", "tick", "signal", "isdma")


class Prog:
    ENG = ("pe", "act", "dve", "pool", "sp")

    def __init__(self, nc):
        self.nc = nc
        self.ops = []
        self.eng_ops = {e: [] for e in self.ENG}
        self.bufs = {}
        self.dma_cnt = {}
        self.last_on = {}
        self.last_all = {}

    def buf(self, key):
        b = self.bufs.get(key)
        if b is None:
            b = Buf(key)
            self.bufs[key] = b
        return b

    def op(self, eng, fn, reads=(), writes=(), dma=None):
        o = Op()
        o.idx = len(self.ops)
        o.eng = eng
        o.fn = fn
        o.isdma = dma is not None
        if dma is not None:
            n = self.dma_cnt.get(dma, 0)
            self.dma_cnt[dma] = n + 1
            o.stream = "dma_%s_%d" % (dma, n % DMA_RING)
        else:
            o.stream = eng
        o.signal = o.isdma
        o.tick = 0
        deps = {}
        ops = self.ops
        if o.isdma:
            prev = self.last_on.get(o.stream)
            if prev is not None:
                deps[o.stream] = prev
            self.last_on[o.stream] = o.idx

        def add(d):
            if d is None:
                return
            ps = ops[d].stream
            if ps == "pe" and o.stream == "pe":
                return
            if deps.get(ps, -1) < d:
                deps[ps] = d

        for b in reads:
            add(b.last_w)
        for b in writes:
            add(b.last_w)
            for d in b.readers.values():
                add(d)
        for b in writes:
            b.last_w = o.idx
            b.readers = {}
        wset = set(id(b) for b in writes)
        for b in reads:
            if id(b) not in wset:
                b.readers[o.stream] = o.idx
        o.deps = deps
        ops.append(o)
        self.eng_ops[eng].append(o)
        self.last_all[o.stream] = o.idx
        return o

    def barrier(self):
        snap = dict(self.last_all)
        for eng in self.ENG:
            o = Op()
            o.idx = len(self.ops)
            o.eng = eng
            o.fn = None
            o.isdma = False
            o.stream = eng
            o.signal = False
            o.tick = 0
            o.deps = {s_: d for s_, d in snap.items() if not (s_ == "pe" and eng == "pe")}
            self.ops.append(o)
            self.eng_ops[eng].append(o)

    def emit(self):
        nc = self.nc
        ops = self.ops
        for o in ops:
            for d in o.deps.values():
                ops[d].signal = True
        counters = {}
        for o in ops:
            if o.signal:
                counters[o.stream] = counters.get(o.stream, 0) + (16 if o.isdma else 1)
                o.tick = counters[o.stream]
        es = ExitStack()
        sems = {}
        for s, total in counters.items():
            n_ep = (total + EPOCH - 1) // EPOCH
            sems[s] = [es.enter_context(nc.semaphore("s_%s_%d" % (s, e))) for e in range(n_ep)]
        self.n_sems = sum(len(v) for v in sems.values())

        def sem_of(stream, tick):
            e = (tick - 1) // EPOCH
            return sems[stream][e], tick - e * EPOCH, e

        block = es.enter_context(nc.Block())

        def make_section(eng):
            my_ops = self.eng_ops[eng]

            def section(h):
                waited = {}
                ep_done = {}
                for o in my_ops:
                    for s, d in o.deps.items():
                        t = ops[d].tick
                        if waited.get(s, 0) >= t:
                            continue
                        sem, val, e = sem_of(s, t)
                        if s.startswith("dma_") and e > 0:
                            for pe_ in range(ep_done.get(s, 0), e):
                                h.wait_ge(sems[s][pe_], EPOCH)
                            ep_done[s] = max(ep_done.get(s, 0), e)
                        h.wait_ge(sem, val)
                        waited[s] = t
                    if o.fn is None:
                        continue
                    ins = o.fn(h)
                    if o.signal:
                        sem, val, e = sem_of(o.stream, o.tick)
                        ins.then_inc(sem, 16 if o.isdma else 1)
                mine = []
                for o in my_ops:
                    if o.isdma and o.stream not in mine:
                        mine.append(o.stream)
                for s in mine:
                    tot = counters[s]
                    for e in range(len(sems[s])):
                        h.wait_ge(sems[s][e], min(EPOCH, tot - e * EPOCH))
            return section

        for eng, reg in (("sp", block.sync), ("pe", block.tensor), ("act", block.scalar),
                         ("dve", block.vector), ("pool", block.gpsimd)):
            if self.eng_ops[eng]:
                reg(make_section(eng))
        es.close()


class Ctx:
    def __init__(self, nc, n_lat_tiles):
        self.nc = nc
        self.P = Prog(nc)
        self.es = ExitStack()
        self.NL = n_lat_tiles
        self.NT = CTX_T + n_lat_tiles
        self.cnt = 0
        self.rr = {}
        self.cap = ARENA_BYTES // 2
        self.arena = self.es.enter_context(nc.sbuf_tensor("arena", [128, self.cap], BF16))
        self.top = 0
        self.peak = 0

    def sb(self, name, shape, dt):
        n = 1
        for d_ in shape[1:]:
            n *= d_
        nb16 = n * (2 if dt == F32 else 1)
        off = self.top
        self.top += (nb16 + 15) // 16 * 16
        self.peak = max(self.peak, self.top)
        assert self.top <= self.cap, "SBUF arena overflow at %s: %d > %d" % (name, self.top * 2, self.cap * 2)
        v = self.arena[0:shape[0], off:off + nb16]
        if dt == F32:
            v = v.bitcast(F32)
        if len(shape) == 3:
            v = v.rearrange("p (a b) -> p a b", b=shape[2])
        return v

    def scope(self):
        C = self

        class _S:
            def __enter__(s_):
                s_.mark = C.top
                s_.rr = dict(C.rr)

            def __exit__(s_, *a):
                C.P.barrier()
                C.top = s_.mark
        return _S()

    def ps(self, name, shape, dt):
        return self.es.enter_context(self.nc.psum_tensor(name, shape, dt))

    def dram(self, name, shape, dt):
        return self.nc.dram_tensor(name, shape, dt, kind="Internal").ap()

    def b(self, key):
        return self.P.buf(key)

    def next(self, key, n):
        v = self.rr.get(key, 0)
        self.rr[key] = v + 1
        return v % n

    def blocks(self, tb=4):
        out = [[0, 1]]
        t = CTX_T
        while t < self.NT:
            out.append(list(range(t, min(t + tb, self.NT))))
            t += tb
        return out

    def hTv(self, kcn, ntok):
        return self.hT[:, 0:kcn * ntok].rearrange("p (k t) -> p k t", t=ntok)


def dbufs(C, name, t, c0, c1, gw=512):
    return [C.b((name, t, g)) for g in range(c0 // gw, (c1 - 1) // gw + 1)]


def setup_common(C):
    nc, P = C.nc, C.P
    C.identf = C.sb("identf", [128, 128], F32)
    C.identb = C.sb("identb", [128, 128], BF16)
    C.onesb = C.sb("onesb", [128, 128], BF16)
    P.op("sp", lambda h: h.dma_start(out=C.identf[:], in_=C.cst[:, 0:128]), writes=[C.b("identf")], dma="ld")
    P.op("dve", lambda h: h.tensor_copy(out=C.identb[:], in_=C.identf[:]), reads=[C.b("identf")], writes=[C.b("identb")])
    C.onesf = C.sb("onesf", [128, 128], F32)
    P.op("sp", lambda h: h.dma_start(out=C.onesf[:], in_=C.cst[:, 128:256]), writes=[C.b("onesf")], dma="ld")
    P.op("dve", lambda h: h.tensor_copy(out=C.onesb[:], in_=C.onesf[:]), reads=[C.b("onesf")], writes=[C.b("onesb")])
    C.epsc = C.sb("epsc", [128, 1], F32)
    P.op("sp", lambda h: h.dma_start(out=C.epsc[:], in_=C.cst[:, 256:257], allow_slow_non_contiguous=True), writes=[C.b("epsc")], dma="ld")
    C.gb = [C.ps("gb%d" % i, [128, 512], F32) for i in range(4)]
    C.ob = C.ps("ob", [128, 512], F32)
    C.db = C.ps("db", [128, 512], F32)
    C.tpf = C.ps("tpf", [128, 512], F32)
    C.tpbf = C.ps("tpb", [128, 512], F32)
    C.tpb = C.tpbf.bitcast(BF16)
    C.stg_f = [C.sb("stgf%d" % i, [128, 512], F32) for i in range(4)]
    C.stg_b = [C.sb("stgb%d" % i, [128, 512], BF16) for i in range(4)]
    C.hT = C.sb("hT", [128, 44 * 256], BF16)
    C.xin = [C.sb("xin%d" % i, [128, 2048], F32) for i in range(2)]
    C.xinb = [C.sb("xinb%d" % i, [128, 5632], BF16) for i in range(1)]
    C.wp = [C.sb("wp%d" % i, [128, 16, 512], BF16) for i in range(2)]
    C.sa = C.sb("sa", [128, 4, 512], F32)
    C.vecT = C.sb("vecT", [128, 8, KC], F32)


def cast_weight(C, name, w_ap, K, N, PW=512):
    kc = K // 128
    npan = N // PW
    wb = C.dram(name + "_bf", [npan, 128, kc, PW], BF16)
    for j in range(npan):
        src = w_ap[:, j * PW:(j + 1) * PW].rearrange("(k p) c -> p k c", p=128)
        C.P.op("pool", lambda h, j=j, src=src: h.dma_start(out=wb[j], in_=src),
               writes=[C.b((name, j))], dma="cast")
    return dict(ap=wb, name=name, kc=kc, npan=npan, pw=PW)


def evac(C, out_ap, in_ap, reads, writes, func=None, scale=1.0):
    if func is not None:
        C.P.op("act", lambda h: h.activation(out=out_ap, in_=in_ap, func=func, scale=scale), reads=reads, writes=writes)
        return
    if C.next("evac", 2) == 0:
        C.P.op("act", lambda h: h.activation(out=out_ap, in_=in_ap, func=AF.Copy), reads=reads, writes=writes)
    else:
        C.P.op("dve", lambda h: h.tensor_copy(out=out_ap, in_=in_ap), reads=reads, writes=writes)


def load_hT_mod(C, src, sname, tiles, scT, shT, vname):
    P = C.P
    for tl, t in enumerate(tiles):
        xi = C.next("xin", 2)
        xt = C.xin[xi]
        P.op("sp", lambda h, xt=xt, t=t: h.dma_start(out=xt[:], in_=src[t * 128:(t + 1) * 128, :]),
             reads=dbufs(C, sname, t, 0, D), writes=[C.b(("xin", xi))], dma="ld")
        for g in range(4):
            for q in range(4):
                kc = g * 4 + q
                P.op("pe", lambda h, xt=xt, kc=kc, q=q: h.transpose(C.tpf[:, q * 128:(q + 1) * 128],
                                                                   xt[:, kc * 128:(kc + 1) * 128], C.identf[:]),
                     reads=[C.b(("xin", xi)), C.b("identf")], writes=[C.b("tpf")])
            for q in range(4):
                kc = g * 4 + q
                P.op("act", lambda h, kc=kc, q=q, tl=tl: h.activation(
                    out=C.hTv(KC, 512)[:, kc, tl * 128:(tl + 1) * 128], in_=C.tpf[:, q * 128:(q + 1) * 128],
                    func=AF.Identity, scale=scT[:, kc:kc + 1], bias=shT[:, kc:kc + 1]),
                    reads=[C.b("tpf"), C.b(vname)], writes=[C.b(("hT", tl))])


NTOK_BIG = [False]


def ntok_for(K):
    return 512 if (K <= 2816 or NTOK_BIG[0]) else 256


def load_hT_bf(C, src, sname, tiles, K):
    P = C.P
    kcn = K // 128
    hv = C.hTv(kcn, ntok_for(K))
    for tl, t in enumerate(tiles):
        xi = C.next("xinb", 1)
        xt = C.xinb[xi]
        P.op("sp", lambda h, xt=xt, t=t: h.dma_start(out=xt[:, 0:K], in_=src[t * 128:(t + 1) * 128, :]),
             reads=dbufs(C, sname, t, 0, K), writes=[C.b(("xinb", xi))], dma="ld")
        for g0 in range(0, kcn, 8):
            n = min(8, kcn - g0)
            for q in range(n):
                kc = g0 + q
                P.op("pe", lambda h, xt=xt, kc=kc, q=q: h.transpose(C.tpb[:, q * 128:(q + 1) * 128],
                                                                   xt[:, kc * 128:(kc + 1) * 128], C.identb[:]),
                     reads=[C.b(("xinb", xi)), C.b("identb")], writes=[C.b("tpb")])
            src_ap = C.tpb[:, 0:n * 128].rearrange("p (k c) -> p k c", c=128)
            dst_ap = hv[:, g0:g0 + n, tl * 128:(tl + 1) * 128]
            evac(C, dst_ap, src_ap, [C.b("tpb")], [C.b(("hT", tl))])


def load_hT_fm(C, srcT, sname, tiles, K):
    kcn = K // 128
    t0 = tiles[0]
    n = len(tiles) * 128
    hv = C.hTv(kcn, ntok_for(K))
    C.P.op("sp", lambda h: h.dma_start(out=hv[:, 0:kcn, 0:n],
                                       in_=srcT[:, t0 * 128:t0 * 128 + n].rearrange("(k p) t -> p k t", p=128)),
           reads=[C.b((sname, t)) for t in tiles], writes=[C.b(("hT", tl)) for tl in range(len(tiles))], dma="ld")


def linear(C, W, tiles, epi, panels=None, hv=None, hbufs=None):
    P = C.P
    kc_tot, pw = W["kc"], W["pw"]
    panels = list(range(W["npan"])) if panels is None else panels
    kgroups = [(k0, min(16, kc_tot - k0)) for k0 in range(0, kc_tot, 16)]
    if hv is None:
        hv = C.hTv(kc_tot, ntok_for(kc_tot * 128))
        hbufs = [C.b(("hT", tl)) for tl in range(len(tiles))]
    items = [(j, gi, k0, kn) for j in panels for gi, (k0, kn) in enumerate(kgroups)]

    def issue(item):
        j, gi, k0, kn = item
        wi = C.next("wp", 2)
        wt = C.wp[wi]
        P.op("sp", lambda h, wt=wt, j=j, k0=k0, kn=kn: h.dma_start(out=wt[:, 0:kn, 0:pw], in_=W["ap"][j, :, k0:k0 + kn, :]),
             reads=[C.b((W["name"], j))], writes=[C.b(("wp", wi))], dma="ld")
        return wi
    pending = issue(items[0])
    banks = None
    for idx, (j, gi, k0, kn) in enumerate(items):
        wi = pending
        wt = C.wp[wi]
        if idx + 1 < len(items):
            pending = issue(items[idx + 1])
        if gi == 0:
            banks = [C.next("gb", 4) for _ in tiles]
        for tl, t in enumerate(tiles):
            bi = banks[tl]
            for k in range(kn):
                kk = k0 + k
                P.op("pe", lambda h, bi=bi, tl=tl, kk=kk, k=k, wt=wt: h.matmul(
                    C.gb[bi][:, 0:pw], lhsT=hv[:, kk, tl * 128:(tl + 1) * 128], rhs=wt[:, k, 0:pw],
                    start=(kk == 0), stop=(kk == kc_tot - 1)),
                    reads=[hbufs[tl], C.b(("wp", wi))], writes=[C.b(("gb", bi))])
        if gi == len(kgroups) - 1:
            for tl, t in enumerate(tiles):
                bi = banks[tl]
                epi(t, tl, j, C.gb[bi], C.b(("gb", bi)))


def store(C, dst, dname, t, c0, c1, src_ap, src_buf, gw=512):
    C.P.op("sp", lambda h: h.dma_start(out=dst[t * 128:(t + 1) * 128, c0:c1], in_=src_ap),
           reads=[src_buf], writes=dbufs(C, dname, t, c0, c1, gw), dma="st")


def epi_copy(C, dst, dname, dt, func=None):
    ring = C.stg_f if dt == F32 else C.stg_b
    rname = "stgf" if dt == F32 else "stgb"

    def epi(t, tl, j, bank, bbuf, pw=512):
        si = C.next(rname, 4)
        sbuf = C.b((rname, si))
        evac(C, ring[si][:, 0:pw], bank[:, 0:pw], [bbuf], [sbuf], func=func)
        store(C, dst, dname, t, j * pw, (j + 1) * pw, ring[si][:, 0:pw], sbuf)
    return epi


def adaln_all(C, cvec, ada_w, ada_b, layers):
    P = C.P
    cs = C.sb("cs", [2, D], F32)
    csT = C.sb("csT", [128, KC, 2], BF16)
    C.csT = csT
    P.op("sp", lambda h: h.dma_start(out=cs[:], in_=cvec), writes=[C.b("cs")], dma="ld")
    P.op("act", lambda h: h.activation(out=cs[:], in_=cs[:], func=AF.Silu), reads=[C.b("cs")], writes=[C.b("cs")])
    for kc in range(KC):
        P.op("pe", lambda h, kc=kc: h.transpose(C.tpf[:, kc * 2:kc * 2 + 2], cs[:, kc * 128:(kc + 1) * 128], C.identf[0:2, 0:2]),
             reads=[C.b("cs"), C.b("identf")], writes=[C.b("tpf")])
    P.op("dve", lambda h: h.tensor_copy(out=csT[:], in_=C.tpf[:, 0:KC * 2].rearrange("p (k c) -> p k c", c=2)),
         reads=[C.b("tpf")], writes=[C.b("csT")])
    MOD = C.dram("MOD", [DEPTH, 2, 6 * D], F32)
    C.MOD = MOD
    mrow = [C.sb("mrow%d" % i, [2, 512], F32) for i in range(2)]
    brow = [C.sb("brow%d" % i, [2, 512], F32) for i in range(2)]
    C.ada_state = (csT, mrow, brow, MOD, ada_w, ada_b)


def adaln_layer(C, l):
    P = C.P
    csT, mrow, brow, MOD, ada_w, ada_b = C.ada_state
    if True:
        for j in range(24):
            ri = C.next("mrow", 2)
            for r in range(2):
                P.op("sp", lambda h, l=l, r=r, j=j, ri=ri: h.dma_start(out=brow[ri][r:r + 1, :], in_=ada_b[l:l + 1, j * 512:(j + 1) * 512]),
                     writes=[C.b(("brow", ri))], dma="ld")
            wi = C.next("wp", 2)
            wt = C.wp[wi]
            src = ada_w[l][:, j * 512:(j + 1) * 512].rearrange("(k p) c -> p k c", p=128)
            P.op("pool", lambda h, wt=wt, src=src: h.dma_start(out=wt[:], in_=src), writes=[C.b(("wp", wi))], dma="cast")
            bi = C.next("gb", 4)
            for kc in range(KC):
                P.op("pe", lambda h, bi=bi, kc=kc, wt=wt: h.matmul(C.gb[bi][0:2, :], lhsT=csT[:, kc, :], rhs=wt[:, kc, :],
                                                                  start=(kc == 0), stop=(kc == KC - 1)),
                     reads=[C.b("csT"), C.b(("wp", wi))], writes=[C.b(("gb", bi))])
            P.op("dve", lambda h, bi=bi, ri=ri: h.tensor_tensor(out=mrow[ri][:], in0=C.gb[bi][0:2, :], in1=brow[ri][:], op=ALU.add),
                 reads=[C.b(("gb", bi)), C.b(("brow", ri))], writes=[C.b(("mrow", ri))])
            if 4 <= j < 8 or 16 <= j < 20:
                P.op("dve", lambda h, ri=ri: h.tensor_scalar_add(out=mrow[ri][:], in0=mrow[ri][:], scalar1=1.0),
                     reads=[C.b(("mrow", ri))], writes=[C.b(("mrow", ri))])
            P.op("sp", lambda h, l=l, j=j, ri=ri: h.dma_start(out=MOD[l, :, j * 512:(j + 1) * 512], in_=mrow[ri][:]),
                 reads=[C.b(("mrow", ri))], writes=[C.b(("MOD", l))], dma="st")


def load_layer_vectors(C, l, ln_g, ln_b):
    P = C.P
    for wi, off in enumerate((0, D, 3 * D, 4 * D)):
        for r in range(2):
            src = C.MOD[l, r, off:off + D].rearrange("(k p) -> p k", p=128)
            P.op("sp", lambda h, wi=wi, r=r, src=src: h.dma_start(out=C.vecT[:, wi * 2 + r, :], in_=src, allow_slow_non_contiguous=True),
                 reads=[C.b(("MOD", l))], writes=[C.b("vecT")], dma="ld")


def load_bcast(C, l, sub, ln_g, ln_b):
    P = C.P
    C.bc = [C.sb("bc%d" % i, [128, D], F32) for i in range(4)]
    goff = 2 * D if sub == 0 else 5 * D
    srcs = [C.MOD[l, 0:1, goff:goff + D], C.MOD[l, 1:2, goff:goff + D], ln_g[l, sub:sub + 1, :], ln_b[l, sub:sub + 1, :]]
    for i, s in enumerate(srcs):
        P.op("sp", lambda h, i=i, s=s: h.dma_start(out=C.bc[i][:], in_=s.partition_broadcast(128)),
             reads=[C.b(("MOD", l))], writes=[C.b(("bc", i))], dma="ld")


def resid_ln(C, X, xname, Y, yname, OUT, oname, out_row0=None):
    P = C.P
    C.rl_x = C.xin
    C.rl_y = [C.sb("rly%d" % i, [128, D], F32) for i in range(2)]
    C.rl_st = C.sb("rlst", [128, 4, 6], F32)
    C.rl_mv = C.sb("rlmv", [128, 4], F32)
    for t in range(C.NT):
        if out_row0 is not None and t < CTX_T:
            continue
        i = C.next("xin", 2)
        xt, yt = C.rl_x[i], C.rl_y[i]
        bx, by = C.b(("xin", i)), C.b(("rly", i))
        P.op("sp", lambda h, xt=xt, t=t: h.dma_start(out=xt[:], in_=X[t * 128:(t + 1) * 128, :]),
             reads=dbufs(C, xname, t, 0, D), writes=[bx], dma="ld")
        P.op("sp", lambda h, yt=yt, t=t: h.dma_start(out=yt[:], in_=Y[t * 128:(t + 1) * 128, :]),
             reads=dbufs(C, yname, t, 0, D), writes=[by], dma="ld")
        gi = 1 if t < CTX_T else 0
        P.op("dve", lambda h, yt=yt, gi=gi: h.tensor_tensor(out=yt[:], in0=yt[:], in1=C.bc[gi][:], op=ALU.mult),
             reads=[by, C.b(("bc", gi))], writes=[by])
        P.op("dve", lambda h, xt=xt, yt=yt: h.scalar_tensor_tensor(out=yt[:], in0=xt[:], scalar=ALPHA, in1=yt[:],
                                                                  op0=ALU.mult, op1=ALU.add),
             reads=[bx, by], writes=[by])
        for q in range(4):
            P.op("dve", lambda h, yt=yt, q=q: h.bn_stats(out=C.rl_st[:, q, :], in_=yt[:, q * 512:(q + 1) * 512]),
                 reads=[by], writes=[C.b("rlst")])
        P.op("dve", lambda h: h.bn_aggr(out=C.rl_mv[:, 0:2], in_=C.rl_st[:].rearrange("p a b -> p (a b)")),
             reads=[C.b("rlst")], writes=[C.b("rlmv")])
        P.op("act", lambda h: h.activation(out=C.rl_mv[:, 2:3], in_=C.rl_mv[:, 1:2], func=AF.Sqrt, bias=C.epsc[:, 0:1], scale=1.0),
             reads=[C.b("rlmv"), C.b("epsc")], writes=[C.b("rlmv")])
        P.op("dve", lambda h: h.reciprocal(out=C.rl_mv[:, 2:3], in_=C.rl_mv[:, 2:3]), reads=[C.b("rlmv")], writes=[C.b("rlmv")])
        P.op("dve", lambda h: h.scalar_tensor_tensor(out=C.rl_mv[:, 3:4], in0=C.rl_mv[:, 0:1], scalar=-1.0, in1=C.rl_mv[:, 2:3],
                                                     op0=ALU.mult, op1=ALU.mult),
             reads=[C.b("rlmv")], writes=[C.b("rlmv")])
        P.op("act", lambda h, xt=xt, yt=yt: h.activation(out=xt[:], in_=yt[:], func=AF.Identity, scale=C.rl_mv[:, 2:3], bias=C.rl_mv[:, 3:4]),
             reads=[by, C.b("rlmv")], writes=[bx])
        P.op("dve", lambda h, xt=xt: h.tensor_tensor(out=xt[:], in0=xt[:], in1=C.bc[2][:], op=ALU.mult),
             reads=[bx, C.b(("bc", 2))], writes=[bx])
        P.op("dve", lambda h, xt=xt: h.tensor_tensor(out=xt[:], in0=xt[:], in1=C.bc[3][:], op=ALU.add),
             reads=[bx, C.b(("bc", 3))], writes=[bx])
        if out_row0 is None:
            P.op("sp", lambda h, xt=xt, t=t: h.dma_start(out=OUT[t * 128:(t + 1) * 128, :], in_=xt[:]),
                 reads=[bx], writes=dbufs(C, oname, t, 0, D), dma="st")
        else:
            r0 = (t - CTX_T) * 128
            P.op("sp", lambda h, xt=xt, r0=r0: h.dma_start(out=OUT[r0:r0 + 128, :], in_=xt[:]),
                 reads=[bx], writes=[C.b((oname, t))], dma="st")


def ffn(C, l, X, xname, Wi, Wo, U, Y2):
    P = C.P

    for tiles in C.blocks():
        r = 1 if tiles[0] < CTX_T else 0
        load_hT_mod(C, X, xname, tiles, C.vecT[:, 3 * 2 + r, :], C.vecT[:, 2 * 2 + r, :], "vecT")

        def epi(t, tl, j, bank, bbuf):
            if j < 11:
                P.op("act", lambda h: h.activation(out=C.sa[:, tl, :], in_=bank[:], func=AF.Silu),
                     reads=[bbuf], writes=[C.b(("sa", tl))])
            else:
                si = C.next("stgb", 4)
                sbuf = C.b(("stgb", si))
                P.op("dve", lambda h: h.tensor_tensor(out=C.stg_b[si][:], in0=bank[:], in1=C.sa[:, tl, :], op=ALU.mult),
                     reads=[bbuf, C.b(("sa", tl))], writes=[sbuf])
                store(C, U, "U", t, (j - 11) * 512, (j - 10) * 512, C.stg_b[si][:], sbuf)
        order = []
        for j in range(11):
            order += [j, 11 + j]
        linear(C, Wi, tiles, epi, panels=order)
    with C.scope():
        old_hT = C.hT
        C.hT = C.sb("hTbig", [128, 44 * 512], BF16)
        NTOK_BIG[0] = True
        for tiles in C.blocks(4):
            load_hT_bf(C, U, "U", tiles, FFN_H)
            linear(C, Wo, tiles, epi_copy(C, Y2, "Y2", F32))
        NTOK_BIG[0] = False
        C.hT = old_hT


def out_proj_tok(C, SRC, sname, K, Wo, Y):
    for tiles in C.blocks(4 if K <= 2816 else 2):
        load_hT_bf(C, SRC, sname, tiles, K)
        linear(C, Wo, tiles, epi_copy(C, Y, "Y", F32))


def linear_fm(C, W, ntok, epi, mchunks=None, hv=None, hbufs=None):
    P = C.P
    kc_tot, pw = W["kc"], W["pw"]
    cw = min(128, pw)
    cpp = pw // cw
    if hv is None:
        hv = C.hTv(kc_tot, ntok_for(kc_tot * 128))
        hbufs = [C.b(("hT", tl)) for tl in range((ntok + 127) // 128)]
    items = []
    for j in range(W["npan"]):
        ms = [m for m in range(j * cpp, j * cpp + cpp) if mchunks is None or m in mchunks]
        if ms:
            items.append((j, ms))

    def issue(item):
        j = item[0]
        wi = C.next("wp", 2)
        wt = C.wp[wi]
        P.op("sp", lambda h, wt=wt, j=j: h.dma_start(out=wt[:, 0:kc_tot, 0:pw], in_=W["ap"][j, :, :, :]),
             reads=[C.b((W["name"], j))], writes=[C.b(("wp", wi))], dma="ld")
        return wi
    if not items:
        return
    pending = issue(items[0])
    for idx, (j, ms) in enumerate(items):
        wi = pending
        wt = C.wp[wi]
        if idx + 1 < len(items):
            pending = issue(items[idx + 1])
        for m in ms:
            sub = m % cpp
            bi = C.next("gb", 4)
            for k in range(kc_tot):
                P.op("pe", lambda h, bi=bi, k=k, wt=wt, sub=sub: h.matmul(
                    C.gb[bi][0:cw, 0:ntok], lhsT=wt[:, k, sub * cw:(sub + 1) * cw], rhs=hv[:, k, 0:ntok],
                    start=(k == 0), stop=(k == kc_tot - 1)),
                    reads=list(hbufs) + [C.b(("wp", wi))], writes=[C.b(("gb", bi))])
            epi(m, C.gb[bi], C.b(("gb", bi)))


def store_rows(C, dst, dname, key, r0, nrows, t0, ntok, src_ap, src_buf):
    tl = list(range(t0, t0 + (ntok + 127) // 128))
    C.P.op("sp", lambda h: h.dma_start(out=dst[r0:r0 + nrows, t0 * 128:t0 * 128 + ntok], in_=src_ap),
           reads=[src_buf], writes=[C.b((dname, key, t)) for t in tl], dma="st")


def store_fm(C, dst, dname, m, t0, ntok, src_ap, src_buf):
    tl = list(range(t0, t0 + (ntok + 127) // 128))
    C.P.op("sp", lambda h: h.dma_start(out=dst[m * 128:(m + 1) * 128, t0 * 128:t0 * 128 + ntok], in_=src_ap),
           reads=[src_buf], writes=[C.b((dname, m, t)) for t in tl], dma="st")


def attn_core(C, groups, scale, OT, oname):
    P = C.P
    T = C.NT * 128
    allt = list(range(C.NT))
    if True:
        C.att_kT = C.sb("att_kT", [128, 2, T], BF16)
        C.att_v = C.sb("att_v", [128, C.NT, 128], BF16)
        C.att_qT = [C.sb("att_qT%d" % i, [128, 2, 512], BF16) for i in range(2)]
        C.att_pT = [C.sb("att_pT%d" % i, [128, 512], BF16) for i in range(4)]
        C.att_rd = C.sb("att_rd", [128, 512], F32)
        C.att_oo = [C.sb("att_oo%d" % i, [128, 512], BF16) for i in range(2)]
    kT, vv, qT, pT, rd, oo = C.att_kT, C.att_v, C.att_qT, C.att_pT, C.att_rd, C.att_oo
    for grp in groups:
        nch = len(grp["k_chunks"])
        for ci, (ap, r0, nr, kp) in enumerate(grp["k_chunks"]):
            P.op("sp", lambda h, ci=ci, ap=ap, r0=r0, nr=nr: h.dma_start(out=kT[0:nr, ci, :], in_=ap[r0:r0 + nr, :]),
                 reads=[C.b(kp + (t,)) for t in allt], writes=[C.b("att_kT")], dma="ld")
        vap, vname, vc0, vg = grp["v"]
        P.op("sp", lambda h, vap=vap, vc0=vc0: h.dma_start(out=vv[:], in_=vap[:, vc0:vc0 + 128].rearrange("(t p) c -> p t c", p=128)),
             reads=[C.b((vname, t, (vc0 // 512))) for t in allt], writes=[C.b("att_v")], dma="ld")
        for hd in grp["heads"]:
            for tiles in C.blocks():
                nq = len(tiles) * 128
                t0 = tiles[0]
                keys = [0, 1] if t0 < CTX_T else allt
                qi = C.next("att_qT", 2)
                for ci, (ap, r0, nr, kp) in enumerate(hd["q_chunks"]):
                    P.op("sp", lambda h, qi=qi, ci=ci, ap=ap, r0=r0, nr=nr, t0=t0, nq=nq: h.dma_start(
                        out=qT[qi][0:nr, ci, 0:nq], in_=ap[r0:r0 + nr, t0 * 128:t0 * 128 + nq]),
                        reads=[C.b(kp + (t,)) for t in tiles], writes=[C.b(("att_qT", qi))], dma="ld")
                def pv(ki, kt, pi, nq=nq, nk=len(keys)):
                    first, lastk = (ki == 0), (ki == nk - 1)
                    P.op("pe", lambda h: h.matmul(C.ob[:, 0:nq], lhsT=vv[:, kt, :], rhs=pT[pi][:, 0:nq], start=first, stop=lastk),
                         reads=[C.b("att_v"), C.b(("att_pT", pi))], writes=[C.b("ob")])
                    P.op("pe", lambda h: h.matmul(C.db[:, 0:nq], lhsT=C.onesb[:], rhs=pT[pi][:, 0:nq], start=first, stop=lastk),
                         reads=[C.b("onesb"), C.b(("att_pT", pi))], writes=[C.b("db")])
                pend = []
                for ki, kt in enumerate(keys):
                    bi = C.next("gb", 4)
                    for ci, (ap, r0, nr, kp) in enumerate(grp["k_chunks"]):
                        P.op("pe", lambda h, bi=bi, kt=kt, qi=qi, nq=nq, ci=ci, nr=nr: h.matmul(
                            C.gb[bi][:, 0:nq], lhsT=kT[0:nr, ci, kt * 128:(kt + 1) * 128], rhs=qT[qi][0:nr, ci, 0:nq],
                            start=(ci == 0), stop=(ci == nch - 1)),
                            reads=[C.b("att_kT"), C.b(("att_qT", qi))], writes=[C.b(("gb", bi))])
                    pi = C.next("att_pT", 4)
                    P.op("act", lambda h, bi=bi, pi=pi, nq=nq: h.activation(out=pT[pi][:, 0:nq], in_=C.gb[bi][:, 0:nq], func=AF.Exp, scale=scale),
                         reads=[C.b(("gb", bi))], writes=[C.b(("att_pT", pi))])
                    pend.append((ki, kt, pi))
                    if len(pend) > 2:
                        pv(*pend.pop(0))
                while pend:
                    pv(*pend.pop(0))
                P.op("dve", lambda h, nq=nq: h.reciprocal(out=rd[:, 0:nq], in_=C.db[:, 0:nq]), reads=[C.b("db")], writes=[C.b("att_rd")])
                oi = C.next("att_oo", 2)
                P.op("dve", lambda h, nq=nq, oi=oi: h.tensor_tensor(out=oo[oi][:, 0:nq], in0=C.ob[:, 0:nq], in1=rd[:, 0:nq], op=ALU.mult),
                     reads=[C.b("ob"), C.b("att_rd")], writes=[C.b(("att_oo", oi))])
                store_rows(C, OT, oname, hd["out"], hd["out"] * 128, 128, t0, nq, oo[oi][:, 0:nq], C.b(("att_oo", oi)))


def out_proj_fm(C, OT, oname, nchunks, Wo, Y):
    P = C.P
    K_ = nchunks * 128
    nt = ntok_for(K_)
    for tiles in C.blocks(nt // 128):
        hv = C.hTv(nchunks, nt)
        n = len(tiles) * 128
        t0 = tiles[0]
        P.op("sp", lambda h, hv=hv, n=n, t0=t0: h.dma_start(out=hv[:, 0:nchunks, 0:n], in_=OT[:, t0 * 128:t0 * 128 + n].rearrange("(k p) t -> p k t", p=128)),
             reads=[C.b((oname, m, t)) for m in range(nchunks) for t in tiles], writes=[C.b(("hT", tl)) for tl in range(len(tiles))], dma="ld")
        linear(C, Wo, tiles, epi_copy(C, Y, "Y", F32))


def gqa_mixer(C, l, X, Y, w_in, w_in_sw, qk_gain, rope_tab, w_out):
    P = C.P
    T = C.NT * 128
    Wq = cast_weight(C, "gqa_in%d" % l, w_in, D, 3072)
    Ws = cast_weight(C, "gqa_sw%d" % l, w_in_sw, D, 2560)
    Wo = cast_weight(C, "gqa_out%d" % l, w_out, D, D)
    QT = C.dram("gqa_QT", [2560, T], BF16)
    V = C.dram("gqa_V", [T, 512], BF16)
    OT = C.dram("gqa_OT", [D, T], BF16)
    tab = C.sb("ropetab", [128, 2, 512], F32)
    gq = C.sb("gqag", [128, 4], F32)
    P.op("sp", lambda h: h.dma_start(out=gq[:], in_=qk_gain.rearrange("a p -> p a"), allow_slow_non_contiguous=True),
         writes=[C.b("gqag")], dma="ld")
    sq = C.sb("gqa_sq", [128, 512], BF16)
    rs = C.sb("gqa_rs", [128, 4, 512], F32)
    t1 = C.sb("gqa_t1", [128, 4, 512], F32)
    t2 = C.sb("gqa_t2", [128, 512], F32)
    qo = [C.sb("gqa_qo%d" % i, [128, 512], BF16) for i in range(2)]

    def mk_epis(ntok, t0):
        def epi_a(m, bank, bbuf):
            w, sub = (0 if m < 16 else 1), m % 4
            P.op("act", lambda h: h.activation(out=sq[:, 0:ntok], in_=bank[:, 0:ntok], func=AF.Square), reads=[bbuf], writes=[C.b("gqa_sq")])
            P.op("pe", lambda h: h.matmul(C.db[:, 0:ntok], lhsT=C.onesb[:], rhs=sq[:, 0:ntok], start=True, stop=True),
                 reads=[C.b("gqa_sq"), C.b("onesb")], writes=[C.b("db")])
            P.op("act", lambda h: h.activation(out=rs[:, sub, 0:ntok], in_=C.db[:, 0:ntok], func=AF.Sqrt, scale=1.0 / 128, bias=C.epsc[:, 0:1]),
                 reads=[C.b("db"), C.b("epsc")], writes=[C.b(("gqa_rs", sub))])
            P.op("dve", lambda h: h.reciprocal(out=rs[:, sub, 0:ntok], in_=rs[:, sub, 0:ntok]), reads=[C.b(("gqa_rs", sub))], writes=[C.b(("gqa_rs", sub))])
            P.op("dve", lambda h: h.scalar_tensor_tensor(out=t1[:, sub, 0:ntok], in0=bank[:, 0:ntok], scalar=gq[:, 2 * w:2 * w + 1], in1=tab[:, 0, 0:ntok],
                                                         op0=ALU.mult, op1=ALU.mult),
                 reads=[bbuf, C.b("ropetab"), C.b("gqag")], writes=[C.b(("gqa_t1", sub))])

        def epi_b(m, bank, bbuf):
            w, sub = (0 if m < 16 else 1), m % 4
            P.op("dve", lambda h: h.scalar_tensor_tensor(out=t2[:, 0:ntok], in0=bank[:, 0:ntok], scalar=gq[:, 2 * w + 1:2 * w + 2], in1=tab[:, 1, 0:ntok],
                                                         op0=ALU.mult, op1=ALU.mult),
                 reads=[bbuf, C.b("ropetab"), C.b("gqag")], writes=[C.b("gqa_t2")])
            P.op("dve", lambda h: h.tensor_tensor(out=t2[:, 0:ntok], in0=t1[:, sub, 0:ntok], in1=t2[:, 0:ntok], op=ALU.add),
                 reads=[C.b(("gqa_t1", sub)), C.b("gqa_t2")], writes=[C.b("gqa_t2")])
            qi = C.next("gqa_qo", 2)
            P.op("dve", lambda h: h.tensor_tensor(out=qo[qi][:, 0:ntok], in0=t2[:, 0:ntok], in1=rs[:, sub, 0:ntok], op=ALU.mult),
                 reads=[C.b("gqa_t2"), C.b(("gqa_rs", sub))], writes=[C.b(("gqa_qo", qi))])
            store_rows(C, QT, "gqa_QT", m, m * 128, 128, t0, ntok, qo[qi][:, 0:ntok], C.b(("gqa_qo", qi)))
        return epi_a, epi_b

    for tiles in C.blocks():
        r = 1 if tiles[0] < CTX_T else 0
        ntok = len(tiles) * 128
        t0 = tiles[0]
        load_hT_mod(C, X, "X", tiles, C.vecT[:, 1 * 2 + r, :], C.vecT[:, 0 * 2 + r, :], "vecT")
        for i in range(2):
            P.op("sp", lambda h, i=i, t0=t0, ntok=ntok: h.dma_start(out=tab[:, i, 0:ntok], in_=rope_tab[i, :, t0 * 128:t0 * 128 + ntok]),
                 writes=[C.b("ropetab")], dma="ld")
        ea, eb = mk_epis(ntok, t0)
        for j in range(5):
            ms = list(range(j * 4, j * 4 + 4))
            linear_fm(C, Wq, ntok, ea, mchunks=ms)
            linear_fm(C, Ws, ntok, eb, mchunks=ms)
        ecv = epi_copy(C, V, "gqa_V", BF16)
        linear(C, Wq, tiles, lambda t, tl, j, bank, bbuf, ecv=ecv: ecv(t, tl, j - 5, bank, bbuf), panels=[5])

    groups = []
    for g in range(4):
        groups.append(dict(k_chunks=[(QT, (16 + g) * 128, 128, ("gqa_QT", 16 + g))], v=(V, "gqa_V", g * 128, 0),
                           heads=[dict(q_chunks=[(QT, (g * 4 + hh) * 128, 128, ("gqa_QT", g * 4 + hh))], out=g * 4 + hh) for hh in range(4)]))
    attn_core(C, groups, 128 ** -0.5, OT, "gqa_OT")
    out_proj_fm(C, OT, "gqa_OT", 16, Wo, Y)


def mla_mixer(C, l, X, Y, w_a, w_kr, w_kr_sw, w_qn, w_qr, w_qr_sw, w_kn, w_v, norms, rope_tab, w_out):
    P = C.P
    T = C.NT * 128
    Wa = cast_weight(C, "mla_a%d" % l, w_a, D, 1024)
    import os
    NC_ = int(os.environ.get("MLA_NCAST", "99"))
    specs = [("mla_kr", w_kr, D, 64, 64), ("mla_krs", w_kr_sw, D, 64, 64), ("mla_qn", w_qn, 512, 2048, 512), ("mla_qr", w_qr, 512, 1024, 64),
             ("mla_qrs", w_qr_sw, 512, 1024, 64), ("mla_kn", w_kn, 512, 2048, 512), ("mla_v", w_v, 512, 2048, 512), ("mla_out", w_out, D, D, 512)]
    Ws = []
    for i, (nm, w_, k_, n_, pw_) in enumerate(specs):
        if i >= NC_:
            return
        Ws.append(cast_weight(C, nm + "%d" % l, w_, k_, n_, PW=pw_))
    Wkr, Wkrs, Wqn, Wqr, Wqrs, Wkn, Wv, Wo = Ws
    KN = C.dram("mla_KN", [2048, T], BF16)
    KR = C.dram("mla_KR", [64, T], BF16)
    QN = C.dram("mla_QN", [2048, T], BF16)
    QR = C.dram("mla_QR", [1024, T], BF16)
    V = C.dram("mla_V", [T, 2048], BF16)
    OT = C.dram("mla_OT", [D, T], BF16)
    tab = C.sb("mla_tab", [64, 2, 512], F32)
    gn = C.sb("mla_gn", [128, 2, 4], F32)
    for i in range(2):
        P.op("sp", lambda h, i=i: h.dma_start(out=gn[:, i, :], in_=norms[i].rearrange("(k p) -> p k", p=128), allow_slow_non_contiguous=True),
             writes=[C.b("mla_gn")], dma="ld")
    sq = C.sb("mla_sq", [128, 512], BF16)
    rs = C.sb("mla_rs", [128, 512], F32)
    ssa = C.sb("mla_ssa", [128, 512], F32)
    raw = C.sb("mla_raw", [128, 4, 512], F32)
    cn = [C.sb("mla_cn%d" % i, [128, 4, 512], BF16) for i in range(2)]
    t1 = C.sb("mla_t1", [64, 512], F32)
    t2 = C.sb("mla_t2", [64, 512], F32)
    ro = [C.sb("mla_ro%d" % i, [64, 512], BF16) for i in range(2)]

    def mk_epi_a(ntok):
        def epi_a(m, bank, bbuf):
            grp, mm = m // 4, m % 4
            P.op("act", lambda h: h.activation(out=sq[:, 0:ntok], in_=bank[:, 0:ntok], func=AF.Square), reads=[bbuf], writes=[C.b("mla_sq")])
            P.op("pe", lambda h: h.matmul(C.db[:, 0:ntok], lhsT=C.onesb[:], rhs=sq[:, 0:ntok], start=True, stop=True),
                 reads=[C.b("mla_sq"), C.b("onesb")], writes=[C.b("db")])
            if mm == 0:
                P.op("dve", lambda h: h.tensor_copy(out=ssa[:, 0:ntok], in_=C.db[:, 0:ntok]), reads=[C.b("db")], writes=[C.b("mla_ssa")])
            else:
                P.op("dve", lambda h: h.tensor_tensor(out=ssa[:, 0:ntok], in0=C.db[:, 0:ntok], in1=ssa[:, 0:ntok], op=ALU.add),
                     reads=[C.b("db"), C.b("mla_ssa")], writes=[C.b("mla_ssa")])
            P.op("dve", lambda h: h.tensor_copy(out=raw[:, mm, 0:ntok], in_=bank[:, 0:ntok]), reads=[bbuf], writes=[C.b(("mla_raw", mm))])
            if mm == 3:
                P.op("act", lambda h: h.activation(out=rs[:, 0:ntok], in_=ssa[:, 0:ntok], func=AF.Sqrt, scale=1.0 / 512, bias=C.epsc[:, 0:1]),
                     reads=[C.b("mla_ssa"), C.b("epsc")], writes=[C.b("mla_rs")])
                P.op("dve", lambda h: h.reciprocal(out=rs[:, 0:ntok], in_=rs[:, 0:ntok]), reads=[C.b("mla_rs")], writes=[C.b("mla_rs")])
                for q in range(4):
                    P.op("dve", lambda h, q=q: h.scalar_tensor_tensor(out=cn[grp][:, q, 0:ntok], in0=raw[:, q, 0:ntok], scalar=gn[:, grp, q:q + 1],
                                                                     in1=rs[:, 0:ntok], op0=ALU.mult, op1=ALU.mult),
                         reads=[C.b(("mla_raw", q)), C.b("mla_rs"), C.b("mla_gn")], writes=[C.b(("mla_cn", grp))])
        return epi_a

    def mk_rope_epis(dst, dname, key, r0, ntok, t0):
        def ea(m, bank, bbuf):
            P.op("dve", lambda h: h.tensor_tensor(out=t1[:, 0:ntok], in0=bank[0:64, 0:ntok], in1=tab[:, 0, 0:ntok], op=ALU.mult),
                 reads=[bbuf, C.b("mla_tab")], writes=[C.b("mla_t1")])

        def eb(m, bank, bbuf):
            P.op("dve", lambda h: h.tensor_tensor(out=t2[:, 0:ntok], in0=bank[0:64, 0:ntok], in1=tab[:, 1, 0:ntok], op=ALU.mult),
                 reads=[bbuf, C.b("mla_tab")], writes=[C.b("mla_t2")])
            ri = C.next("mla_ro", 2)
            P.op("dve", lambda h: h.tensor_tensor(out=ro[ri][:, 0:ntok], in0=t1[:, 0:ntok], in1=t2[:, 0:ntok], op=ALU.add),
                 reads=[C.b("mla_t1"), C.b("mla_t2")], writes=[C.b(("mla_ro", ri))])
            store_rows(C, dst, dname, key, r0, 64, t0, ntok, ro[ri][:, 0:ntok], C.b(("mla_ro", ri)))
        return ea, eb

    def mk_copy_fm(dst, dname, ntok, t0):
        def e(m, bank, bbuf):
            si = C.next("stgb", 4)
            sbuf = C.b(("stgb", si))
            evac(C, C.stg_b[si][:, 0:ntok], bank[:, 0:ntok], [bbuf], [sbuf])
            store_rows(C, dst, dname, m, m * 128, 128, t0, ntok, C.stg_b[si][:, 0:ntok], sbuf)
        return e

    for tiles in C.blocks():
        r = 1 if tiles[0] < CTX_T else 0
        ntok = len(tiles) * 128
        t0 = tiles[0]
        load_hT_mod(C, X, "X", tiles, C.vecT[:, 1 * 2 + r, :], C.vecT[:, 0 * 2 + r, :], "vecT")
        for i in range(2):
            P.op("sp", lambda h, i=i, t0=t0, ntok=ntok: h.dma_start(out=tab[:, i, 0:ntok], in_=rope_tab[i, :, t0 * 128:t0 * 128 + ntok]),
                 writes=[C.b("mla_tab")], dma="ld")
        import os
        STOP = int(os.environ.get("MLA_STOP", "99"))
        if STOP < 1:
            continue
        linear_fm(C, Wa, ntok, mk_epi_a(ntok))
        if STOP < 2:
            continue
        ea, eb = mk_rope_epis(KR, "mla_KR", 0, 0, ntok, t0)
        linear_fm(C, Wkr, ntok, ea)
        linear_fm(C, Wkrs, ntok, eb)
        cqv, ckv = cn[0], cn[1]
        cqb, ckb = [C.b(("mla_cn", 0))] * 4, [C.b(("mla_cn", 1))] * 4
        if STOP < 3:
            continue
        linear_fm(C, Wqn, ntok, mk_copy_fm(QN, "mla_QN", ntok, t0), hv=cqv, hbufs=cqb[:1])
        if STOP < 4:
            continue
        for m in range(16):
            ea, eb = mk_rope_epis(QR, "mla_QR", m, m * 64, ntok, t0)
            linear_fm(C, Wqr, ntok, ea, mchunks=[m], hv=cqv, hbufs=cqb[:1])
            linear_fm(C, Wqrs, ntok, eb, mchunks=[m], hv=cqv, hbufs=cqb[:1])
        if STOP < 5:
            continue
        linear_fm(C, Wkn, ntok, mk_copy_fm(KN, "mla_KN", ntok, t0), hv=ckv, hbufs=ckb[:1])
        linear(C, Wv, tiles, epi_copy(C, V, "mla_V", BF16), hv=ckv, hbufs=ckb)
    if STOP < 6:
        return
    groups = []
    for hd in range(16):
        groups.append(dict(k_chunks=[(KN, hd * 128, 128, ("mla_KN", hd)), (KR, 0, 64, ("mla_KR", 0))], v=(V, "mla_V", hd * 128, 0),
                           heads=[dict(q_chunks=[(QN, hd * 128, 128, ("mla_QN", hd)), (QR, hd * 64, 64, ("mla_QR", hd))], out=hd)]))
    attn_core(C, groups, 192 ** -0.5, OT, "mla_OT")
    out_proj_fm(C, OT, "mla_OT", 16, Wo, Y)


def ret_mixer(C, l, X, Y, w_in, w_qk_sw, decay, rope_tab, w_out):
    P = C.P
    T = C.NT * 128
    NT, NL = C.NT, C.NL
    allt = list(range(NT))
    Win = cast_weight(C, "ret_in%d" % l, w_in, D, 12288)
    Wsw = cast_weight(C, "ret_sw%d" % l, w_qk_sw, D, 4096)
    Wo = cast_weight(C, "ret_out%d" % l, w_out, 4096, D)
    QK = C.dram("ret_QK", [4096, T], BF16)
    V = C.dram("ret_V", [T, 4096], BF16)
    G = C.dram("ret_G", [4096, T], F32)
    OT = C.dram("ret_OT", [4096, T], BF16)
    mark_proj = C.top
    tab = C.sb("ret_tab", [128, 4, 512], F32)
    t1 = C.sb("ret_t1", [128, 4, 512], F32)
    t2 = C.sb("ret_t2", [128, 512], F32)
    qo = [C.sb("ret_qo%d" % i, [128, 512], BF16) for i in range(2)]

    def mk_epis(ntok, t0):
        def ea(m, bank, bbuf):
            part, sub = m % 2, m % 4
            P.op("dve", lambda h: h.tensor_tensor(out=t1[:, sub, 0:ntok], in0=bank[:, 0:ntok], in1=tab[:, part, 0:ntok], op=ALU.mult),
                 reads=[bbuf, C.b("ret_tab")], writes=[C.b(("ret_t1", sub))])

        def eb(m, bank, bbuf):
            part, sub = m % 2, m % 4
            P.op("dve", lambda h: h.tensor_tensor(out=t2[:, 0:ntok], in0=bank[:, 0:ntok], in1=tab[:, 2 + part, 0:ntok], op=ALU.mult),
                 reads=[bbuf, C.b("ret_tab")], writes=[C.b("ret_t2")])
            qi = C.next("ret_qo", 2)
            P.op("dve", lambda h: h.tensor_tensor(out=qo[qi][:, 0:ntok], in0=t1[:, sub, 0:ntok], in1=t2[:, 0:ntok], op=ALU.add),
                 reads=[C.b(("ret_t1", sub)), C.b("ret_t2")], writes=[C.b(("ret_qo", qi))])
            store_rows(C, QK, "ret_QK", m, m * 128, 128, t0, ntok, qo[qi][:, 0:ntok], C.b(("ret_qo", qi)))
        return ea, eb

    def mk_g(ntok, t0):
        def e(m, bank, bbuf):
            si = C.next("stgf", 4)
            sbuf = C.b(("stgf", si))
            P.op("act", lambda h: h.activation(out=C.stg_f[si][:, 0:ntok], in_=bank[:, 0:ntok], func=AF.Silu), reads=[bbuf], writes=[sbuf])
            store_rows(C, G, "ret_G", m - 64, (m - 64) * 128, 128, t0, ntok, C.stg_f[si][:, 0:ntok], sbuf)
        return e

    for tiles in C.blocks():
        r = 1 if tiles[0] < CTX_T else 0
        ntok = len(tiles) * 128
        t0 = tiles[0]
        load_hT_mod(C, X, "X", tiles, C.vecT[:, 1 * 2 + r, :], C.vecT[:, 0 * 2 + r, :], "vecT")
        for i in range(2):
            for part in range(2):
                P.op("sp", lambda h, i=i, part=part, t0=t0, ntok=ntok: h.dma_start(
                    out=tab[:, i * 2 + part, 0:ntok], in_=rope_tab[i, part * 128:(part + 1) * 128, t0 * 128:t0 * 128 + ntok]),
                    writes=[C.b("ret_tab")], dma="ld")
        ea, eb = mk_epis(ntok, t0)
        for j in range(8):
            ms = list(range(j * 4, j * 4 + 4))
            linear_fm(C, Win, ntok, ea, mchunks=ms)
            linear_fm(C, Wsw, ntok, eb, mchunks=ms)
        ecv = epi_copy(C, V, "ret_V", BF16)
        linear(C, Win, tiles, lambda t, tl, j, bank, bbuf, ecv=ecv: ecv(t, tl, j - 8, bank, bbuf), panels=list(range(8, 16)))
        linear_fm(C, Win, ntok, mk_g(ntok, t0), mchunks=list(range(64, 96)))

    P.barrier()
    C.top = mark_proj
    lg = C.sb("ret_lg", [128, 16], F32)
    P.op("sp", lambda h: h.dma_start(out=lg[:], in_=decay.partition_broadcast(128)), writes=[C.b("ret_lg")], dma="ld")
    NSC = NT + 3
    iota = C.sb("ret_iota", [128, 64], F32)
    cm = C.sb("ret_cm", [128, 4, 128], F32)
    P.op("sp", lambda h: h.dma_start(out=iota[:], in_=C.cst2[:, 512:576]), writes=[C.b("ret_iota")], dma="ld")
    P.op("sp", lambda h: h.dma_start(out=cm[:], in_=C.cst2[:, 0:512].rearrange("p (a b) -> p a b", b=128)), writes=[C.b("ret_cm")], dma="ld")
    SC = C.sb("ret_SC", [128, 16, 64], F32)
    for j in range(16):
        P.op("dve", lambda h, j=j: h.tensor_scalar(out=SC[:, j, :], in0=iota[:], scalar1=lg[:, j:j + 1], scalar2=None, op0=ALU.mult),
             reads=[C.b("ret_iota"), C.b("ret_lg")], writes=[C.b("ret_SC")])
    P.op("act", lambda h: h.activation(out=SC[:], in_=SC[:], func=AF.Exp), reads=[C.b("ret_SC")], writes=[C.b("ret_SC")])
    P.op("dve", lambda h: h.tensor_scalar(out=SC[:], in0=SC[:], scalar1=1.0 / 16, scalar2=None, op0=ALU.mult),
         reads=[C.b("ret_SC")], writes=[C.b("ret_SC")])
    BF = C.sb("ret_BF", [128, 8, 128], F32)
    BB = C.sb("ret_BB", [128, 8, 128], F32)
    DG = C.sb("ret_DG", [128, 8, 128], F32)
    tm = C.sb("ret_tm", [128, 128], F32)
    for hh in range(8):
        P.op("act", lambda h, hh=hh: h.activation(out=BF[:, hh, :], in_=cm[:, 0, :], func=AF.Exp, scale=lg[:, hh:hh + 1]),
             reads=[C.b("ret_cm"), C.b("ret_lg")], writes=[C.b("ret_BF")])
        P.op("act", lambda h, hh=hh: h.activation(out=BB[:, hh, :], in_=cm[:, 1, :], func=AF.Exp, scale=lg[:, 8 + hh:9 + hh]),
             reads=[C.b("ret_cm"), C.b("ret_lg")], writes=[C.b("ret_BB")])
        P.op("dve", lambda h, hh=hh: h.tensor_tensor(out=tm[:], in0=BF[:, hh, :], in1=cm[:, 2, :], op=ALU.mult),
             reads=[C.b("ret_BF"), C.b("ret_cm")], writes=[C.b("ret_tm")])
        P.op("dve", lambda h, hh=hh: h.tensor_tensor(out=DG[:, hh, :], in0=BB[:, hh, :], in1=cm[:, 3, :], op=ALU.mult),
             reads=[C.b("ret_BB"), C.b("ret_cm")], writes=[C.b("ret_DG")])
        P.op("dve", lambda h, hh=hh: h.tensor_tensor(out=DG[:, hh, :], in0=DG[:, hh, :], in1=tm[:], op=ALU.add),
             reads=[C.b("ret_DG"), C.b("ret_tm")], writes=[C.b("ret_DG")])
        P.op("dve", lambda h, hh=hh: h.tensor_scalar(out=DG[:, hh, :], in0=DG[:, hh, :], scalar1=1.0 / 16, scalar2=None, op0=ALU.mult),
             reads=[C.b("ret_DG")], writes=[C.b("ret_DG")])

    kT = C.sb("ret_kT", [128, 2, T], BF16)
    vv = C.sb("ret_v", [128, NT, 512], BF16)
    qT = [C.sb("ret_qT%d" % i, [128, 2, 512], BF16) for i in range(2)]
    pT = [C.sb("ret_pT%d" % i, [128, 512], BF16) for i in range(4)]
    sq = C.sb("ret_sq", [128, 512], BF16)
    ssa = C.sb("ret_ssa", [128, 512], F32)
    gt = [C.sb("ret_gt%d" % i, [128, 512], F32) for i in range(2)]
    oo = [C.sb("ret_oo%d" % i, [128, 512], BF16) for i in range(2)]
    acc = [C.ob, C.db, C.tpf, C.tpbf]
    accb = [C.b("ob"), C.b("db"), C.b("tpf"), C.b("tpb")]
    for hd in range(8):
        for ci in range(2):
            P.op("sp", lambda h, ci=ci, hd=hd: h.dma_start(out=kT[:, ci, :], in_=QK[(16 + hd * 2 + ci) * 128:(17 + hd * 2 + ci) * 128, :]),
                 reads=[C.b(("ret_QK", 16 + hd * 2 + ci, t)) for t in allt], writes=[C.b("ret_kT")], dma="ld")
        P.op("sp", lambda h, hd=hd: h.dma_start(out=vv[:], in_=V[:, hd * 512:(hd + 1) * 512].rearrange("(t p) c -> p t c", p=128)),
             reads=[C.b(("ret_V", t, hd)) for t in allt], writes=[C.b("ret_v")], dma="ld")
        for tiles in C.blocks():
            nq = len(tiles) * 128
            t0 = tiles[0]
            keys = [0, 1] if t0 < CTX_T else allt
            qi = C.next("ret_qT", 2)
            for ci in range(2):
                P.op("sp", lambda h, qi=qi, ci=ci, hd=hd, t0=t0, nq=nq: h.dma_start(
                    out=qT[qi][:, ci, 0:nq], in_=QK[(hd * 2 + ci) * 128:(hd * 2 + ci + 1) * 128, t0 * 128:t0 * 128 + nq]),
                    reads=[C.b(("ret_QK", hd * 2 + ci, t)) for t in tiles], writes=[C.b(("ret_qT", qi))], dma="ld")
            def pv4(ki, kt, pi, pbuf, nq, nk):
                first, lastk = (ki == 0), (ki == nk - 1)
                for j in range(4):
                    P.op("pe", lambda h, j=j: h.matmul(acc[j][:, 0:nq], lhsT=vv[:, kt, j * 128:(j + 1) * 128], rhs=pT[pi][:, 0:nq], start=first, stop=lastk),
                         reads=[C.b("ret_v"), pbuf], writes=[accb[j]])
            pendr = []
            for ki, kt in enumerate(keys):
                bi = C.next("gb", 4)
                for ci in range(2):
                    P.op("pe", lambda h, bi=bi, kt=kt, qi=qi, nq=nq, ci=ci: h.matmul(
                        C.gb[bi][:, 0:nq], lhsT=kT[:, ci, kt * 128:(kt + 1) * 128], rhs=qT[qi][:, ci, 0:nq], start=(ci == 0), stop=(ci == 1)),
                        reads=[C.b("ret_kT"), C.b(("ret_qT", qi))], writes=[C.b(("gb", bi))])
                pi = C.next("ret_pT", 4)
                pbuf = C.b(("ret_pT", pi))
                for ql, qt in enumerate(tiles):
                    cs_ = slice(ql * 128, (ql + 1) * 128)
                    srcp = C.gb[bi][:, cs_]
                    dstp = pT[pi][:, cs_]
                    if kt == qt:
                        P.op("dve", lambda h, srcp=srcp, dstp=dstp, hd=hd: h.tensor_tensor(out=dstp, in0=srcp, in1=DG[:, hd, :], op=ALU.mult),
                             reads=[C.b(("gb", bi)), C.b("ret_DG")], writes=[pbuf])
                    elif kt < CTX_T and qt >= CTX_T:
                        da = qt - kt
                        db_ = NL + kt - qt + 2
                        P.op("dve", lambda h, hd=hd, da=da: h.tensor_scalar(out=tm[:], in0=BF[:, hd, :], scalar1=SC[:, hd, da:da + 1], scalar2=None, op0=ALU.mult),
                             reads=[C.b("ret_BF"), C.b("ret_SC")], writes=[C.b("ret_tm")])
                        P.op("dve", lambda h, hd=hd, db_=db_: h.scalar_tensor_tensor(out=tm[:], in0=BB[:, hd, :], scalar=SC[:, 8 + hd, db_:db_ + 1], in1=tm[:],
                                                                                   op0=ALU.mult, op1=ALU.add),
                             reads=[C.b("ret_BB"), C.b("ret_SC"), C.b("ret_tm")], writes=[C.b("ret_tm")])
                        P.op("dve", lambda h, srcp=srcp, dstp=dstp: h.tensor_tensor(out=dstp, in0=srcp, in1=tm[:], op=ALU.mult),
                             reads=[C.b(("gb", bi)), C.b("ret_tm")], writes=[pbuf])
                    elif kt < qt:
                        dd = qt - kt
                        P.op("dve", lambda h, srcp=srcp, dstp=dstp, hd=hd, dd=dd: h.scalar_tensor_tensor(
                            out=dstp, in0=srcp, scalar=SC[:, hd, dd:dd + 1], in1=BF[:, hd, :], op0=ALU.mult, op1=ALU.mult),
                            reads=[C.b(("gb", bi)), C.b("ret_SC"), C.b("ret_BF")], writes=[pbuf])
                    else:
                        dd = kt - qt
                        P.op("dve", lambda h, srcp=srcp, dstp=dstp, hd=hd, dd=dd: h.scalar_tensor_tensor(
                            out=dstp, in0=srcp, scalar=SC[:, 8 + hd, dd:dd + 1], in1=BB[:, hd, :], op0=ALU.mult, op1=ALU.mult),
                            reads=[C.b(("gb", bi)), C.b("ret_SC"), C.b("ret_BB")], writes=[pbuf])
                pendr.append((ki, kt, pi, pbuf, nq, len(keys)))
                if len(pendr) > 2:
                    pv4(*pendr.pop(0))
            while pendr:
                pv4(*pendr.pop(0))
            for j in range(4):
                P.op("act", lambda h, j=j, nq=nq: h.activation(out=sq[:, 0:nq], in_=acc[j][:, 0:nq], func=AF.Square), reads=[accb[j]], writes=[C.b("ret_sq")])
                bi = C.next("gb", 4)
                P.op("pe", lambda h, bi=bi, nq=nq: h.matmul(C.gb[bi][:, 0:nq], lhsT=C.onesb[:], rhs=sq[:, 0:nq], start=True, stop=True),
                     reads=[C.b("ret_sq"), C.b("onesb")], writes=[C.b(("gb", bi))])
                if j == 0:
                    P.op("dve", lambda h, bi=bi, nq=nq: h.tensor_copy(out=ssa[:, 0:nq], in_=C.gb[bi][:, 0:nq]), reads=[C.b(("gb", bi))], writes=[C.b("ret_ssa")])
                else:
                    P.op("dve", lambda h, bi=bi, nq=nq: h.tensor_tensor(out=ssa[:, 0:nq], in0=C.gb[bi][:, 0:nq], in1=ssa[:, 0:nq], op=ALU.add),
                         reads=[C.b(("gb", bi)), C.b("ret_ssa")], writes=[C.b("ret_ssa")])
            P.op("act", lambda h, nq=nq: h.activation(out=ssa[:, 0:nq], in_=ssa[:, 0:nq], func=AF.Sqrt, scale=1.0 / 512, bias=C.epsc[:, 0:1]),
                 reads=[C.b("ret_ssa"), C.b("epsc")], writes=[C.b("ret_ssa")])
            P.op("dve", lambda h, nq=nq: h.reciprocal(out=ssa[:, 0:nq], in_=ssa[:, 0:nq]), reads=[C.b("ret_ssa")], writes=[C.b("ret_ssa")])
            for j in range(4):
                gi = C.next("ret_gt", 2)
                row = (hd * 4 + j) * 128
                P.op("sp", lambda h, gi=gi, row=row, t0=t0, nq=nq: h.dma_start(out=gt[gi][:, 0:nq], in_=G[row:row + 128, t0 * 128:t0 * 128 + nq]),
                     reads=[C.b(("ret_G", hd * 4 + j, t)) for t in tiles], writes=[C.b(("ret_gt", gi))], dma="ld")
                P.op("dve", lambda h, gi=gi, nq=nq: h.tensor_tensor(out=gt[gi][:, 0:nq], in0=gt[gi][:, 0:nq], in1=ssa[:, 0:nq], op=ALU.mult),
                     reads=[C.b(("ret_gt", gi)), C.b("ret_ssa")], writes=[C.b(("ret_gt", gi))])
                oi = C.next("ret_oo", 2)
                P.op("dve", lambda h, gi=gi, oi=oi, j=j, nq=nq: h.tensor_tensor(out=oo[oi][:, 0:nq], in0=acc[j][:, 0:nq], in1=gt[gi][:, 0:nq], op=ALU.mult),
                     reads=[accb[j], C.b(("ret_gt", gi))], writes=[C.b(("ret_oo", oi))])
                store_rows(C, OT, "ret_OT", hd * 4 + j, row, 128, t0, nq, oo[oi][:, 0:nq], C.b(("ret_oo", oi)))
    out_proj_fm(C, OT, "ret_OT", 32, Wo, Y)


def hgrn_mixer(C, l, X, Y, w_in, lb_raw, out_gain, w_out):
    P = C.P
    T = C.NT * 128
    NT, NL = C.NT, C.NL
    Win = cast_weight(C, "hg_in%d" % l, w_in, D, 10240)
    Wo = cast_weight(C, "hg_out%d" % l, w_out, D, D)
    QS = C.dram("hg_QS", [D, T], F32)
    KF = [C.dram("hg_K%d" % d_, [D, T], F32) for d_ in range(2)]
    LF = [C.dram("hg_LF%d" % d_, [D, T], F32) for d_ in range(2)]
    GS = C.dram("hg_GS", [D, T], F32)
    IV = C.dram("hg_I", [T, D], BF16)
    OF = C.dram("hg_OF", [D, T], F32)
    OT = C.dram("hg_OT", [D, T], BF16)
    lr = C.sb("hg_lr", [128, 4, KC], F32)
    lbv = C.sb("hg_lb", [128, 4, KC], F32)
    for j in range(4):
        P.op("sp", lambda h, j=j: h.dma_start(out=lr[:, j, :], in_=lb_raw[j].rearrange("(k p) -> p k", p=128), allow_slow_non_contiguous=True),
             writes=[C.b("hg_lr")], dma="ld")
    P.op("act", lambda h: h.activation(out=lr[:], in_=lr[:], func=AF.Exp), reads=[C.b("hg_lr")], writes=[C.b("hg_lr")])
    P.op("dve", lambda h: h.tensor_tensor(out=lbv[:, 2, :], in0=lr[:, 0, :], in1=lr[:, 1, :], op=ALU.add), reads=[C.b("hg_lr")], writes=[C.b("hg_lb")])
    P.op("dve", lambda h: h.tensor_tensor(out=lbv[:, 3, :], in0=lr[:, 2, :], in1=lr[:, 3, :], op=ALU.add), reads=[C.b("hg_lr")], writes=[C.b("hg_lb")])
    P.op("dve", lambda h: h.tensor_tensor(out=lbv[:, 2, :], in0=lbv[:, 2, :], in1=lbv[:, 3, :], op=ALU.add), reads=[C.b("hg_lb")], writes=[C.b("hg_lb")])
    P.op("dve", lambda h: h.reciprocal(out=lbv[:, 2, :], in_=lbv[:, 2, :]), reads=[C.b("hg_lb")], writes=[C.b("hg_lb")])
    P.op("dve", lambda h: h.memset(lbv[:, 0, :], 0.0), reads=[C.b("hg_lb")], writes=[C.b("hg_lb")])
    for j in range(1, l + 1):
        P.op("dve", lambda h, j=j: h.tensor_tensor(out=lbv[:, 0, :], in0=lbv[:, 0, :], in1=lr[:, j, :], op=ALU.add),
             reads=[C.b("hg_lb"), C.b("hg_lr")], writes=[C.b("hg_lb")])
    P.op("dve", lambda h: h.tensor_tensor(out=lbv[:, 0, :], in0=lbv[:, 0, :], in1=lbv[:, 2, :], op=ALU.mult), reads=[C.b("hg_lb")], writes=[C.b("hg_lb")])
    P.op("dve", lambda h: h.tensor_scalar(out=lbv[:, 1, :], in0=lbv[:, 0, :], scalar1=-1.0, scalar2=1.0, op0=ALU.mult, op1=ALU.add),
         reads=[C.b("hg_lb")], writes=[C.b("hg_lb")])

    def mk_act_store(dst, dname, func, m0, ntok, t0):
        def e(m, bank, bbuf):
            si = C.next("stgf", 4)
            sbuf = C.b(("stgf", si))
            P.op("act", lambda h: h.activation(out=C.stg_f[si][:, 0:ntok], in_=bank[:, 0:ntok], func=func), reads=[bbuf], writes=[sbuf])
            store_rows(C, dst, dname, m - m0, (m - m0) * 128, 128, t0, ntok, C.stg_f[si][:, 0:ntok], sbuf)
        return e

    mark_proj = C.top
    fg = C.sb("hg_fg", [128, 512], F32)
    kk = [C.sb("hg_kk%d" % i, [128, 512], F32) for i in range(2)]
    lff = [C.sb("hg_lff%d" % i, [128, 512], F32) for i in range(2)]

    def mk_forget(d_, m0, ntok, t0):
        def e(m, bank, bbuf):
            mm = m - m0
            P.op("act", lambda h: h.activation(out=fg[:, 0:ntok], in_=bank[:, 0:ntok], func=AF.Sigmoid), reads=[bbuf], writes=[C.b("hg_fg")])
            P.op("dve", lambda h: h.tensor_scalar(out=fg[:, 0:ntok], in0=fg[:, 0:ntok], scalar1=lbv[:, 1, mm:mm + 1], scalar2=lbv[:, 0, mm:mm + 1],
                                                  op0=ALU.mult, op1=ALU.add),
                 reads=[C.b("hg_fg"), C.b("hg_lb")], writes=[C.b("hg_fg")])
            i1 = C.next("hg_kk", 2)
            P.op("dve", lambda h: h.tensor_scalar(out=kk[i1][:, 0:ntok], in0=fg[:, 0:ntok], scalar1=-1.0, scalar2=1.0, op0=ALU.mult, op1=ALU.add),
                 reads=[C.b("hg_fg")], writes=[C.b(("hg_kk", i1))])
            store_rows(C, KF[d_], "hg_K%d" % d_, mm, mm * 128, 128, t0, ntok, kk[i1][:, 0:ntok], C.b(("hg_kk", i1)))
            i2 = C.next("hg_lff", 2)
            P.op("act", lambda h: h.activation(out=lff[i2][:, 0:ntok], in_=fg[:, 0:ntok], func=AF.Ln), reads=[C.b("hg_fg")], writes=[C.b(("hg_lff", i2))])
            store_rows(C, LF[d_], "hg_LF%d" % d_, mm, mm * 128, 128, t0, ntok, lff[i2][:, 0:ntok], C.b(("hg_lff", i2)))
        return e

    for tiles in C.blocks():
        r = 1 if tiles[0] < CTX_T else 0
        ntok = len(tiles) * 128
        t0 = tiles[0]
        load_hT_mod(C, X, "X", tiles, C.vecT[:, 1 * 2 + r, :], C.vecT[:, 0 * 2 + r, :], "vecT")
        linear_fm(C, Win, ntok, mk_act_store(QS, "hg_QS", AF.Silu, 0, ntok, t0), mchunks=list(range(0, 16)))
        linear_fm(C, Win, ntok, mk_forget(0, 16, ntok, t0), mchunks=list(range(16, 32)))
        linear_fm(C, Win, ntok, mk_forget(1, 32, ntok, t0), mchunks=list(range(32, 48)))
        eci = epi_copy(C, IV, "hg_I", BF16)
        linear(C, Win, tiles, lambda t, tl, j, bank, bbuf, eci=eci: eci(t, tl, j - 12, bank, bbuf), panels=list(range(12, 16)))
        linear_fm(C, Win, ntok, mk_act_store(GS, "hg_GS", AF.Silu, 64, ntok, t0), mchunks=list(range(64, 80)))

    P.barrier()
    C.top = mark_proj
    W_ = 16 * 128
    rm = C.sb("hg_rm", [128, W_], F32)
    msk = C.sb("hg_msk", [128, 2, 128], F32)
    gv = C.sb("hg_gv", [128, 1], F32)
    P.op("sp", lambda h: h.dma_start(out=rm[:], in_=C.cst3[:, 0:W_]), writes=[C.b("hg_rm")], dma="ld")
    P.op("sp", lambda h: h.dma_start(out=msk[:], in_=C.cst3[:, W_:W_ + 256].rearrange("p (a b) -> p a b", b=128)), writes=[C.b("hg_msk")], dma="ld")
    P.op("sp", lambda h: h.dma_start(out=gv[:], in_=out_gain.rearrange("a p -> p a"), allow_slow_non_contiguous=True), writes=[C.b("hg_gv")], dma="ld")
    qt_ = C.sb("hg_q", [128, W_], F32)
    kt_ = C.sb("hg_k", [128, W_], F32)
    lt_ = C.sb("hg_l", [128, W_], F32)
    cf = C.sb("hg_cf", [128, W_], F32)
    tS = C.xinb[0][:, 0:2 * W_].bitcast(F32)
    ex = C.sb("hg_ex", [128, W_], F32)
    ebl = C.sb("hg_ebl", [128, 32], F32)
    qin = C.sb("hg_qin", [128, W_], BF16)
    kout = C.sb("hg_kout", [128, W_], BF16)
    kd = C.sb("hg_kd", [128, W_], BF16)
    kdT = [C.sb("hg_kdT%d" % i, [128, 128], BF16) for i in range(2)]
    pm = [C.sb("hg_pm%d" % i, [128, 128], BF16) for i in range(2)]
    iv = C.sb("hg_iv", [128, D], BF16)
    St = C.sb("hg_S", [128, 16, 128], F32)
    Sb = C.sb("hg_Sb", [128, 16, 128], BF16)
    oall = C.sa[:].rearrange("p a b -> p (a b)")
    ofl, osq, gsl, ogb = lt_, kout, qt_, qin
    _alias = {"hg_ofl": "hg_l", "hg_osq": "hg_kout", "hg_gsl": "hg_q", "hg_ogb": "hg_qin", "hg_b": "hg_cf"}
    B = lambda k_: C.b(_alias.get(k_, k_) if isinstance(k_, str) else k_)
    c3 = lambda t_: t_[:].rearrange("p (a b) -> p a b", b=64)

    def fm_tile_ap(dr, t):
        return dr[:, t * 128:(t + 1) * 128].rearrange("(h p) t -> p h t", p=128)

    for d_ in range(2):
        P.op("dve", lambda h: h.memset(St[:], 0.0), reads=[B("hg_S")], writes=[B("hg_S")])
        P.op("dve", lambda h: h.memset(Sb[:], 0.0), reads=[B("hg_Sb")], writes=[B("hg_Sb")])
        order = list(range(NT)) if d_ == 0 else [1, 0] + list(range(NT - 1, CTX_T - 1, -1))
        for t in order:
            allm = list(range(16))
            P.op("sp", lambda h, t=t: h.dma_start(out=qt_[:].rearrange("p (h t) -> p h t", t=128), in_=fm_tile_ap(QS, t)),
                 reads=[B(("hg_QS", m, t)) for m in allm], writes=[B("hg_q")], dma="ld")
            P.op("sp", lambda h, t=t, d_=d_: h.dma_start(out=kt_[:].rearrange("p (h t) -> p h t", t=128), in_=fm_tile_ap(KF[d_], t)),
                 reads=[B(("hg_K%d" % d_, m, t)) for m in allm], writes=[B("hg_k")], dma="ld")
            P.op("sp", lambda h, t=t, d_=d_: h.dma_start(out=lt_[:].rearrange("p (h t) -> p h t", t=128), in_=fm_tile_ap(LF[d_], t)),
                 reads=[B(("hg_LF%d" % d_, m, t)) for m in allm], writes=[B("hg_l")], dma="ld")
            P.op("sp", lambda h, t=t: h.dma_start(out=iv[:], in_=IV[t * 128:(t + 1) * 128, :]),
                 reads=dbufs(C, "hg_I", t, 0, D), writes=[B("hg_iv")], dma="ld")
            P.op("dve", lambda h: h.tensor_tensor_scan(out=cf[:], data0=rm[:], data1=lt_[:], initial=0.0, op0=ALU.mult, op1=ALU.add),
                 reads=[B("hg_rm"), B("hg_l")], writes=[B("hg_cf")])
            P.op("dve", lambda h: h.tensor_copy(out=c3(tS), in_=c3(cf)[:, :, 63:64].to_broadcast([128, 32, 64])),
                 reads=[B("hg_cf")], writes=[B("hg_tS")])
            if d_ == 0:
                bsrc, bbuf_ = cf, B("hg_cf")
            else:
                P.op("dve", lambda h: h.tensor_tensor(out=cf[:], in0=lt_[:], in1=cf[:], op=ALU.subtract), reads=[B("hg_l"), B("hg_cf"), B("hg_tS")], writes=[B("hg_cf")])
                P.op("dve", lambda h: h.tensor_tensor(out=cf[:], in0=cf[:], in1=tS[:], op=ALU.add), reads=[B("hg_cf"), B("hg_tS")], writes=[B("hg_cf")])
                bsrc, bbuf_ = cf, B("hg_cf")
            P.op("act", lambda h, bsrc=bsrc: h.activation(out=ex[:], in_=bsrc[:], func=AF.Exp), reads=[bbuf_], writes=[B("hg_ex")])
            P.op("dve", lambda h: h.tensor_tensor(out=qin[:], in0=qt_[:], in1=ex[:], op=ALU.mult), reads=[B("hg_q"), B("hg_ex")], writes=[B("hg_qin")])
            P.op("act", lambda h, bsrc=bsrc: h.activation(out=ex[:], in_=bsrc[:], func=AF.Exp, scale=-1.0), reads=[bbuf_, B("hg_qin")], writes=[B("hg_ex")])
            P.op("dve", lambda h: h.tensor_tensor(out=kout[:], in0=kt_[:], in1=ex[:], op=ALU.mult), reads=[B("hg_k"), B("hg_ex")], writes=[B("hg_kout")])
            P.op("dve", lambda h, bsrc=bsrc: h.tensor_tensor(out=ex[:], in0=tS[:], in1=bsrc[:], op=ALU.subtract), reads=[B("hg_tS"), bbuf_, B("hg_kout")], writes=[B("hg_ex")])
            P.op("act", lambda h: h.activation(out=ex[:], in_=ex[:], func=AF.Exp), reads=[B("hg_ex")], writes=[B("hg_ex")])
            P.op("dve", lambda h: h.tensor_tensor(out=kd[:], in0=kt_[:], in1=ex[:], op=ALU.mult), reads=[B("hg_k"), B("hg_ex")], writes=[B("hg_kd")])
            P.op("act", lambda h: h.activation(out=ebl[:], in_=c3(tS)[:, :, 0], func=AF.Exp), reads=[B("hg_tS")], writes=[B("hg_ebl")])
            chunks = [0, 1] if d_ == 0 else [1, 0]
            for hd in range(16):
                hs = slice(hd * 128, (hd + 1) * 128)
                b1 = C.next("gb", 4)
                P.op("pe", lambda h, b1=b1, hs=hs: h.matmul(C.gb[b1][:, 0:128], lhsT=kout[:, hs], rhs=qin[:, hs], start=True, stop=True),
                     reads=[B("hg_kout"), B("hg_qin")], writes=[B(("gb", b1))])
                pi = C.next("hg_pm", 2)
                P.op("dve", lambda h, b1=b1, pi=pi, d_=d_: h.tensor_tensor(out=pm[pi][:], in0=C.gb[b1][:, 0:128], in1=msk[:, d_, :], op=ALU.mult),
                     reads=[B(("gb", b1)), B("hg_msk")], writes=[B(("hg_pm", pi))])
                P.op("pe", lambda h, hs=hs: h.transpose(C.tpb[:, 0:128], kd[:, hs], C.identb[:]), reads=[B("hg_kd"), B("identb")], writes=[B("tpb")])
                ki = C.next("hg_kdT", 2)
                P.op("act", lambda h, ki=ki: h.activation(out=kdT[ki][:], in_=C.tpb[:, 0:128], func=AF.Copy), reads=[B("tpb")], writes=[B(("hg_kdT", ki))])
                b2 = C.next("gb", 4)
                for ch in chunks:
                    cs_ = slice(ch * 64, ch * 64 + 64)
                    hcs = slice(hd * 128 + ch * 64, hd * 128 + ch * 64 + 64)
                    P.op("pe", lambda h, b2=b2, cs_=cs_, hcs=hcs, hd=hd: h.matmul(C.gb[b2][:, cs_], lhsT=Sb[:, hd, :], rhs=qin[:, hcs], start=True, stop=False),
                         reads=[B(("hg_Sb", hd)), B("hg_qin")], writes=[B(("gb", b2))])
                    P.op("pe", lambda h, b2=b2, cs_=cs_, pi=pi, hs=hs: h.matmul(C.gb[b2][:, cs_], lhsT=iv[:, hs], rhs=pm[pi][:, cs_], start=False, stop=True),
                         reads=[B("hg_iv"), B(("hg_pm", pi))], writes=[B(("gb", b2))])
                    b3 = C.next("gb", 4)
                    P.op("pe", lambda h, b3=b3, cs_=cs_, ki=ki, hs=hs: h.matmul(C.gb[b3][:, 0:128], lhsT=kdT[ki][cs_, :], rhs=iv[cs_, hs], start=True, stop=True),
                         reads=[B(("hg_kdT", ki)), B("hg_iv")], writes=[B(("gb", b3))])
                    ci = hd * 2 + ch
                    P.op("dve", lambda h, b3=b3, hd=hd, ci=ci: h.scalar_tensor_tensor(out=St[:, hd, :], in0=St[:, hd, :], scalar=ebl[:, ci:ci + 1], in1=C.gb[b3][:, 0:128],
                                                                                 op0=ALU.mult, op1=ALU.add),
                         reads=[B(("hg_S", hd)), B("hg_ebl"), B(("gb", b3))], writes=[B(("hg_S", hd))])
                    P.op("act", lambda h, hd=hd: h.activation(out=Sb[:, hd, :], in_=St[:, hd, :], func=AF.Copy),
                         reads=[B(("hg_S", hd))], writes=[B(("hg_Sb", hd))])
                evac(C, oall[:, hs], C.gb[b2][:, 0:128], [B(("gb", b2))], [B("hg_oall")])
            if d_ == 0:
                P.op("sp", lambda h, t=t: h.dma_start(out=fm_tile_ap(OF, t), in_=oall[:].rearrange("p (h t) -> p h t", t=128)),
                     reads=[B("hg_oall")], writes=[B(("hg_OF", t))], dma="st")
            else:
                P.op("sp", lambda h, t=t: h.dma_start(out=ofl[:].rearrange("p (h t) -> p h t", t=128), in_=fm_tile_ap(OF, t)),
                     reads=[B(("hg_OF", t))], writes=[B("hg_ofl")], dma="ld")
                P.op("sp", lambda h, t=t: h.dma_start(out=gsl[:].rearrange("p (h t) -> p h t", t=128), in_=fm_tile_ap(GS, t)),
                     reads=[B(("hg_GS", m, t)) for m in allm], writes=[B("hg_gsl")], dma="ld")
                P.op("dve", lambda h: h.tensor_tensor(out=oall[:], in0=oall[:], in1=ofl[:], op=ALU.add), reads=[B("hg_oall"), B("hg_ofl")], writes=[B("hg_oall")])
                P.op("act", lambda h: h.activation(out=osq[:], in_=oall[:], func=AF.Square), reads=[B("hg_oall")], writes=[B("hg_osq")])
                for q4 in range(4):
                    b4 = C.next("gb", 4)
                    P.op("pe", lambda h, b4=b4, q4=q4: h.matmul(C.gb[b4][:, :], lhsT=C.onesb[:], rhs=osq[:, q4 * 512:(q4 + 1) * 512], start=True, stop=True),
                         reads=[B("hg_osq"), B("onesb")], writes=[B(("gb", b4))])
                    P.op("act", lambda h, b4=b4, q4=q4: h.activation(out=ofl[:, q4 * 512:(q4 + 1) * 512], in_=C.gb[b4][:, :], func=AF.Sqrt, scale=1.0 / 128, bias=C.epsc[:, 0:1]),
                         reads=[B(("gb", b4)), B("epsc"), B("hg_ofl")], writes=[B("hg_ofl")])
                P.op("dve", lambda h: h.reciprocal(out=ofl[:], in_=ofl[:]), reads=[B("hg_ofl")], writes=[B("hg_ofl")])
                P.op("dve", lambda h: h.scalar_tensor_tensor(out=oall[:], in0=oall[:], scalar=gv[:, 0:1], in1=ofl[:], op0=ALU.mult, op1=ALU.mult),
                     reads=[B("hg_oall"), B("hg_gv"), B("hg_ofl")], writes=[B("hg_oall")])
                P.op("dve", lambda h: h.tensor_tensor(out=ogb[:], in0=oall[:], in1=gsl[:], op=ALU.mult), reads=[B("hg_oall"), B("hg_gsl")], writes=[B("hg_ogb")])
                P.op("sp", lambda h, t=t: h.dma_start(out=fm_tile_ap(OT, t), in_=ogb[:].rearrange("p (h t) -> p h t", t=128)),
                     reads=[B("hg_ogb")], writes=[B(("hg_OT", m, t)) for m in allm], dma="st")
    out_proj_fm(C, OT, "hg_OT", 16, Wo, Y)


def build(n_lat_tiles, layers, dbg=None):
    nc = bass.Bass("TRN2", target_bir_lowering=False)
    NL = n_lat_tiles
    T = (CTX_T + NL) * 128
    C = Ctx(nc, NL)
    P = C.P
    ein = lambda name, shape: nc.dram_tensor(name, shape, F32, kind="ExternalInput").ap()
    x_in = ein("x", [NL * 128, D])
    ctx_in = ein("ctx", [256, D])
    cvec = ein("cvec", [2, D])
    ada_w = ein("ada_w", [DEPTH, D, 6 * D])
    ada_b = ein("ada_b", [DEPTH, 6 * D])
    ln_g = ein("ln_g", [DEPTH, 2, D])
    ln_b = ein("ln_b", [DEPTH, 2, D])
    ffn_w_in = ein("ffn_w_in", [DEPTH, D, 2 * FFN_H])
    ffn_w_out = ein("ffn_w_out", [DEPTH, FFN_H, D])
    out = nc.dram_tensor("out", [NL * 128, D], F32, kind="ExternalOutput").ap()
    mixset = set(l % 4 for l in layers)
    C.cst2 = nc.dram_tensor("cst2", [128, 576], F32, kind="ExternalInput").ap()
    C.cst3 = nc.dram_tensor("cst3", [128, 2048 + 256], F32, kind="ExternalInput").ap()
    ein0 = ein
    ein_h = ein if 3 in mixset else (lambda name, shape: None)
    hg_in = {k: ein_h("hg_" + k, shp) for k, shp in (("w_in", [D, 10240]), ("lb_raw", [4, D]), ("gain", [1, 128]), ("w_out", [D, D]))}
    ein_r = ein if 0 in mixset else (lambda name, shape: None)
    ret_in = {k: ein_r("ret_" + k, shp) for k, shp in (("w_in", [D, 12288]), ("w_sw", [D, 4096]), ("decay", [1, 16]), ("rope", [2, 256, T]), ("w_out", [4096, D]))}
    if 1 not in mixset:
        ein = lambda name, shape: None
    gqa_w_in = ein("gqa_w_in", [D, 3072])
    gqa_w_sw = ein("gqa_w_sw", [D, 2560])
    gqa_gain = ein("gqa_gain", [4, 128])
    gqa_rope = ein("gqa_rope", [2, 128, T])
    gqa_w_out = ein("gqa_w_out", [D, D])
    ein = (lambda name, shape: nc.dram_tensor(name, shape, F32, kind="ExternalInput").ap()) if 2 in mixset else (lambda name, shape: None)
    mla_in = {k: ein("mla_" + k, shp) for k, shp in (("w_a", [D, 1024]), ("w_kr", [D, 64]), ("w_kr_sw", [D, 64]), ("w_qn", [512, 2048]),
                                                    ("w_qr", [512, 1024]), ("w_qr_sw", [512, 1024]), ("w_kn", [512, 2048]), ("w_v", [512, 2048]),
                                                    ("norms", [2, 512]), ("rope", [2, 64, T]), ("w_out", [D, D]))}

    C.cst = nc.dram_tensor("cst", [128, 258], F32, kind="ExternalInput").ap()
    setup_common(C)

    X = C.dram("X", [T, D], F32)
    Y = C.dram("Y", [T, D], F32)
    Y2 = C.dram("Y2", [T, D], F32)
    U = C.dram("U", [T, FFN_H], BF16)
    for t in range(C.NT):
        src = ctx_in[t * 128:(t + 1) * 128, :] if t < CTX_T else x_in[(t - CTX_T) * 128:(t - CTX_T + 1) * 128, :]
        P.op("sp", lambda h, t=t, src=src: h.dma_start(out=X[t * 128:(t + 1) * 128, :], in_=src),
             writes=dbufs(C, "X", t, 0, D), dma="st")

    adaln_all(C, cvec, ada_w, ada_b, layers)
    adaln_layer(C, layers[0])
    if dbg and dbg.get("stop_after_adaln"):
        o = nc.dram_tensor("dbg_MOD", [DEPTH, 2, 6 * D], F32, kind="ExternalOutput").ap()
        P.op("sp", lambda h: h.dma_start(out=o, in_=C.MOD), reads=[C.b(("MOD", l)) for l in layers], dma="st")
        o2 = nc.dram_tensor("dbg_csT", [128, KC * 2], BF16, kind="ExternalOutput").ap()
        P.op("sp", lambda h: h.dma_start(out=o2, in_=C.csT[:].rearrange("p k c -> p (k c)")), reads=[C.b("csT")], dma="st")
        P.emit()
        C.es.close()
        return nc, C

    for li, l in enumerate(layers):
        last = (li == len(layers) - 1)
        load_layer_vectors(C, l, ln_g, ln_b)
        mix = (l % 4) if (dbg is None or dbg.get("mixers", True)) else None
        with C.scope():
            if mix == 0:
                R_ = ret_in
                ret_mixer(C, l, X, Y, R_["w_in"], R_["w_sw"], R_["decay"], R_["rope"], R_["w_out"])
            if mix == 3:
                H_ = hg_in
                hgrn_mixer(C, l, X, Y, H_["w_in"], H_["lb_raw"], H_["gain"], H_["w_out"])
            if mix == 1:
                gqa_mixer(C, l, X, Y, gqa_w_in, gqa_w_sw, gqa_gain, gqa_rope, gqa_w_out)
            if mix == 2:
                M = mla_in
                mla_mixer(C, l, X, Y, M["w_a"], M["w_kr"], M["w_kr_sw"], M["w_qn"], M["w_qr"], M["w_qr_sw"], M["w_kn"], M["w_v"], M["norms"], M["rope"], M["w_out"])
        if mix in (0, 1, 2, 3):
            with C.scope():
                load_bcast(C, l, 0, ln_g, ln_b)
                resid_ln(C, X, "X", Y, "Y", X, "X")
        Wi = cast_weight(C, "ffi%d" % l, ffn_w_in[l], D, 2 * FFN_H)
        Wo = cast_weight(C, "ffo%d" % l, ffn_w_out[l], FFN_H, D)
        if not last:
            adaln_layer(C, layers[li + 1])
        ffn(C, l, X, "X", Wi, Wo, U, Y2)
        with C.scope():
            load_bcast(C, l, 1, ln_g, ln_b)
            if last:
                resid_ln(C, X, "X", Y2, "Y2", out, "out", out_row0=0)
            else:
                resid_ln(C, X, "X", Y2, "Y2", X, "X")
    if dbg and dbg.get("dump"):
        for nm, (ap, shape, dt) in dict(MOD=(C.MOD, [DEPTH, 2, 6 * D], F32), U=(U, [T, FFN_H], BF16), Y2=(Y2, [T, D], F32),
                                        Y=(Y, [T, D], F32), X=(X, [T, D], F32)).items():
            if nm in dbg["dump"]:
                o = nc.dram_tensor("dbg_" + nm, shape, dt, kind="ExternalOutput").ap()
                allb = [b_ for k_, b_ in P.bufs.items() if isinstance(k_, tuple) and k_[0] == nm]
                P.op("sp", lambda h, o=o, ap=ap: h.dma_start(out=o, in_=ap), reads=allb, dma="st")
    P.emit()
    C.es.close()
    return nc, C


def make_consts2():
    c = np.zeros((128, 576), np.float32)
    s_ = np.arange(128)[:, None].astype(np.float32)
    c_ = np.arange(128)[None, :].astype(np.float32)
    c[:, 0:128] = c_ - s_
    c[:, 128:256] = s_ - c_
    c[:, 256:384] = (c_ >= s_)
    c[:, 384:512] = (c_ <= s_)
    c[:, 512:576] = 128.0 * np.arange(64)[None, :]
    return c


def ret_host_inputs(ret_w_in, ret_decay_fwd, ret_decay_bwd, ret_w_out, nl_tiles):
    w = ret_w_in[0]
    return {"ret_w_in": np.ascontiguousarray(w), "ret_w_sw": swap_halves_cols(w[:, :4096], 128, 64),
            "ret_decay": np.concatenate([ret_decay_fwd[0], ret_decay_bwd[0]])[None].astype(np.float32),
            "ret_rope": rope_tables(128, nl_tiles), "ret_w_out": np.ascontiguousarray(ret_w_out[0])}


def make_consts3():
    c = np.zeros((128, 2048 + 256), np.float32)
    col = np.arange(2048)
    c[:, 0:2048] = (col % 64 != 0).astype(np.float32)[None, :]
    s_ = np.arange(128)[:, None]
    c_ = np.arange(128)[None, :]
    same = (s_ // 64) == (c_ // 64)
    c[:, 2048:2176] = (same & (c_ >= s_))
    c[:, 2176:2304] = (same & (c_ <= s_))
    return c


def hgrn_host_inputs(hgrn_w_in, hgrn_lb_raw, hgrn_out_norm, hgrn_w_out):
    return {"hg_w_in": np.ascontiguousarray(hgrn_w_in[0]), "hg_lb_raw": np.ascontiguousarray(hgrn_lb_raw),
            "hg_gain": np.ascontiguousarray(hgrn_out_norm[0:1]), "hg_w_out": np.ascontiguousarray(hgrn_w_out[0])}


def make_consts():
    c = np.zeros((128, 258), np.float32)
    c[:, 0:128] = np.eye(128, dtype=np.float32)
    c[:, 128:256] = 1.0
    c[:, 256] = EPS
    return c


def rope_tables(half, nl_tiles, d_chunk=128):
    q = half // 2
    T = (CTX_T + nl_tiles) * 128
    pos = np.arange(nl_tiles * 128)
    row, col = pos // GRID_W, pos % GRID_W
    freqs = THETA ** (-np.arange(0, half, 2, dtype=np.float32) / half)
    cos = np.ones((4 * q, T), np.float32)
    sin = np.zeros((4 * q, T), np.float32)
    a_row = (row[None, :].astype(np.float32) * freqs[:, None]).astype(np.float32)
    a_col = (col[None, :].astype(np.float32) * freqs[:, None]).astype(np.float32)
    L0 = CTX_T * 128
    for blk, ang in ((0, a_row), (1, a_col)):
        c, s_ = np.cos(ang), np.sin(ang)
        cos[blk * 2 * q:blk * 2 * q + q, L0:] = c
        cos[blk * 2 * q + q:blk * 2 * q + 2 * q, L0:] = c
        sin[blk * 2 * q:blk * 2 * q + q, L0:] = -s_
        sin[blk * 2 * q + q:blk * 2 * q + 2 * q, L0:] = s_
    return np.stack([cos, sin]).astype(np.float32)


def swap_halves_cols(w, d, q):
    n = w.shape[-1]
    idx = np.arange(n)
    within = idx % d
    partner = np.where((within // q) % 2 == 0, idx + q, idx - q)
    return np.ascontiguousarray(w[..., partner])


N_LAT_TILES = 32
_NC_CACHE = {}


def kernel(x, c, ctx, c_ctx, ada_w, ada_b, ln_g, ln_b, ffn_w_in, ffn_w_out,
           ret_w_in, ret_decay_fwd, ret_decay_bwd, ret_w_out,
           gqa_w_in, gqa_q_norm, gqa_k_norm, gqa_w_out,
           mla_w_in, mla_q_norm, mla_w_q_up, mla_kv_norm, mla_w_kv_up, mla_w_out,
           hgrn_w_in, hgrn_lb_raw, hgrn_out_norm, hgrn_w_out):
    f = lambda a: np.ascontiguousarray(np.asarray(a, dtype=np.float32))
    x, c, ctx, c_ctx = f(x), f(c), f(ctx), f(c_ctx)
    if "nc" not in _NC_CACHE:
        _NC_CACHE["nc"] = build(N_LAT_TILES, [0, 1, 2, 3])[0]
    nc = _NC_CACHE["nc"]
    gw = f(gqa_w_in)[0]
    qg, kg = f(gqa_q_norm)[0], f(gqa_k_norm)[0]
    sw = lambda v: swap_halves_cols(v, 128, 32)
    shared = {"cst": make_consts(), "cst2": make_consts2(), "cst3": make_consts3(),
              "ada_w": f(ada_w), "ada_b": f(ada_b), "ln_g": f(ln_g), "ln_b": f(ln_b),
              "ffn_w_in": f(ffn_w_in), "ffn_w_out": f(ffn_w_out),
              "gqa_w_in": gw, "gqa_w_sw": sw(gw[:, :2560]), "gqa_gain": np.stack([qg, sw(qg), kg, sw(kg)]),
              "gqa_rope": rope_tables(64, N_LAT_TILES), "gqa_w_out": f(gqa_w_out)[0]}
    shared.update(ret_host_inputs(f(ret_w_in), f(ret_decay_fwd), f(ret_decay_bwd), f(ret_w_out), N_LAT_TILES))
    shared.update(mla_host_inputs(f(mla_w_in), f(mla_q_norm), f(mla_w_q_up), f(mla_kv_norm), f(mla_w_kv_up), f(mla_w_out), N_LAT_TILES))
    shared.update(hgrn_host_inputs(f(hgrn_w_in), f(hgrn_lb_raw), f(hgrn_out_norm), f(hgrn_w_out)))
    work = {0: 0, 1: 1, 4: 2, 5: 3}
    zeros = {k: np.zeros_like(v) for k, v in shared.items()}
    zx, zc, zv = np.zeros_like(x[0]), np.zeros_like(ctx[0]), np.zeros((2, D), np.float32)
    in_maps = []
    for core in range(8):
        if core in work:
            b = work[core]
            m = dict(shared)
            m["x"], m["ctx"], m["cvec"] = x[b], ctx[b], np.stack([c[b], c_ctx])
        else:
            m = dict(zeros)
            m["x"], m["ctx"], m["cvec"] = zx, zc, zv
        in_maps.append(m)
    res = run_bass_kernel_spmd(nc, in_maps, core_ids=list(range(8)))
    return np.stack([res.results[core]["out"] for core in (0, 1, 4, 5)]).astype(np.float32)


def mla_host_inputs(mla_w_in, mla_q_norm, mla_w_q_up, mla_kv_norm, mla_w_kv_up, mla_w_out, nl_tiles):
    w_in, wq, wkv = mla_w_in[0], mla_w_q_up[0], mla_w_kv_up[0]
    qh = wq.reshape(512, 16, 192)
    kvh = wkv.reshape(512, 16, 256)
    w_kr = np.ascontiguousarray(w_in[:, 1024:1088])
    w_qr = np.ascontiguousarray(qh[:, :, 128:].reshape(512, 1024))
    return {"mla_w_a": np.ascontiguousarray(w_in[:, :1024]), "mla_w_kr": w_kr, "mla_w_kr_sw": swap_halves_cols(w_kr, 64, 16),
            "mla_w_qn": np.ascontiguousarray(qh[:, :, :128].reshape(512, 2048)), "mla_w_qr": w_qr, "mla_w_qr_sw": swap_halves_cols(w_qr, 64, 16),
            "mla_w_kn": np.ascontiguousarray(kvh[:, :, :128].reshape(512, 2048)), "mla_w_v": np.ascontiguousarray(kvh[:, :, 128:].reshape(512, 2048)),
            "mla_norms": np.stack([mla_q_norm[0], mla_kv_norm[0]]), "mla_rope": rope_tables(32, nl_tiles), "mla_w_out": np.ascontiguousarray(mla_w_out[0])}
```
